# Optimizing a Trainium2 kernel written in Bass

```python
import jax, jax.numpy as jnp
from jax import lax
import numpy as np

D_MODEL = 1024
BATCH = 2
SEQ = 16384
DEPTH = 1
DEC_BATCH = 8
DEC_SEQ = 64
PAST_LEN = 1024

CHUNK = 64
HEAD_DIM = 64
WIN_HEADS = 8
WIN_KV_HEADS = 2
WINDOW = 128
WIN_CHUNKS = WINDOW // CHUNK
BAND_HEADS = 8
BAND_CHUNKS = 8
BAND = BAND_CHUNKS * CHUNK
REL_CLIP = 128
PLE_DIM = 256
ALIBI_MAX = 8.0
EPS = 1e-6
NEG_INF = -1e30

WIN_WIDTH = WIN_HEADS * HEAD_DIM
WIN_KV_WIDTH = WIN_KV_HEADS * HEAD_DIM
BAND_WIDTH = BAND_HEADS * HEAD_DIM
IN_SPLITS = (WIN_WIDTH, WIN_KV_WIDTH, WIN_KV_WIDTH, WIN_WIDTH,
             BAND_WIDTH, BAND_WIDTH, BAND_WIDTH, BAND_WIDTH, D_MODEL, D_MODEL)
IN_COLS = sum(IN_SPLITS)

kernel_name = "hybrid_streaming_encoder_step"


def rms_norm(x, g):
    xf = x.astype(jnp.float32)
    y = xf * lax.rsqrt(jnp.mean(xf * xf, axis=-1, keepdims=True) + EPS)
    return (y * g.astype(jnp.float32)).astype(x.dtype)


def project(h, g_in, w_in, g_q_win, g_k_win, g_q_band, g_k_band):
    B, S, _ = h.shape
    u = rms_norm(h, g_in) @ w_in
    cuts = np.cumsum(IN_SPLITS)[:-1].tolist()
    qa, ka, va, za, qb, kb, vb, zb, ga, gb = jnp.split(u, cuts, axis=-1)
    heads = lambda t, n: t.reshape(B, S, n, HEAD_DIM)
    qa = rms_norm(heads(qa, WIN_HEADS), g_q_win)
    ka = rms_norm(heads(ka, WIN_KV_HEADS), g_k_win)
    va = heads(va, WIN_KV_HEADS)
    qb = rms_norm(heads(qb, BAND_HEADS), g_q_band)
    kb = rms_norm(heads(kb, BAND_HEADS), g_k_band)
    vb = heads(vb, BAND_HEADS)
    return qa, ka, va, za, qb, kb, vb, zb, ga, gb


def rel_distance(past, cq, kn):
    return jnp.arange(cq)[:, None] + past - jnp.arange(kn)[None, :]


def alibi_bias(past, cq, kn):
    dist = jnp.abs(rel_distance(past, cq, kn)).astype(jnp.float32)
    slopes = jnp.exp2(-ALIBI_MAX * jnp.arange(1, WIN_HEADS + 1, dtype=jnp.float32) / WIN_HEADS)
    return -slopes[:, None, None] * dist[None]


def rel_pos_bias(table, past, cq, kn):
    idx = jnp.clip(rel_distance(past, cq, kn), -REL_CLIP, REL_CLIP) + REL_CLIP
    return table.astype(jnp.float32)[:, idx]


def band_attention(q, k, v, bias, valid, sink):
    B, nC, Cq, H, dh = q.shape
    Kn, Hkv = k.shape[2], k.shape[3]
    G = H // Hkv
    qg = q.reshape(B, nC, Cq, Hkv, G, dh)
    s = jnp.einsum('bnqhgd,bnkhd->bnhgqk', qg, k, preferred_element_type=jnp.float32) * (dh ** -0.5)
    s = s + bias.reshape(Hkv, G, Cq, Kn)
    if valid is not None:
        s = jnp.where(valid[None, :, None, None, None, :], s, NEG_INF)
    if sink is not None:
        sk = sink.astype(jnp.float32).reshape(1, 1, Hkv, G, 1, 1)
        m = jnp.maximum(jnp.max(s, axis=-1, keepdims=True), sk)
        e = jnp.exp(s - m)
        pr = e / (jnp.sum(e, axis=-1, keepdims=True) + jnp.exp(sk - m))
    else:
        pr = jax.nn.softmax(s, axis=-1)
    o = jnp.einsum('bnhgqk,bnkhd->bnqhgd', pr.astype(v.dtype), v)
    return o.reshape(B, nC, Cq, H * dh)


def chunk_band(t, nb):
    B, S, Hk, dh = t.shape
    nC = S // CHUNK
    tp = jnp.pad(t.reshape(B, nC, CHUNK, Hk, dh), ((0, 0), (nb, 0), (0, 0), (0, 0), (0, 0)))
    return jnp.concatenate([tp[:, j:j + nC] for j in range(nb + 1)], axis=2)


def band_valid(nC, nb):
    ci = jnp.arange(nC)[:, None] - nb + jnp.arange(nb + 1)[None, :]
    return jnp.repeat(ci >= 0, CHUNK, axis=1)


def merge_and_finish(h, oa, ob, za, zb, ga, gb, p, w_o_win, w_o_band, w_out, g_ple, w_ple_gate, w_ple):
    branch_a = (oa * jax.nn.silu(za)) @ w_o_win
    branch_b = (ob * jax.nn.silu(zb)) @ w_o_band
    h = h + (jax.nn.sigmoid(ga) * branch_a + jax.nn.sigmoid(gb) * branch_b) @ w_out
    gate = jax.nn.sigmoid(rms_norm(h, g_ple) @ w_ple_gate)
    return h + gate * (p @ w_ple)


def prompt_layer(h, p, lw):
    (g_in, w_in, g_q_win, g_k_win, sink_win, g_q_band, g_k_band, rel_bias_band,
     w_o_win, w_o_band, w_out, g_ple, w_ple_gate, w_ple) = lw
    B, S, _ = h.shape
    nC = S // CHUNK
    qa, ka, va, za, qb, kb, vb, zb, ga, gb = project(h, g_in, w_in, g_q_win, g_k_win, g_q_band, g_k_band)
    kn_a = (WIN_CHUNKS + 1) * CHUNK
    oa = band_attention(qa.reshape(B, nC, CHUNK, WIN_HEADS, HEAD_DIM),
                        chunk_band(ka, WIN_CHUNKS), chunk_band(va, WIN_CHUNKS),
                        alibi_bias(WIN_CHUNKS * CHUNK, CHUNK, kn_a),
                        band_valid(nC, WIN_CHUNKS), sink_win).reshape(B, S, WIN_WIDTH)
    kn_b = (BAND_CHUNKS + 1) * CHUNK
    ob = band_attention(qb.reshape(B, nC, CHUNK, BAND_HEADS, HEAD_DIM),
                        chunk_band(kb, BAND_CHUNKS), chunk_band(vb, BAND_CHUNKS),
                        rel_pos_bias(rel_bias_band, BAND_CHUNKS * CHUNK, CHUNK, kn_b),
                        band_valid(nC, BAND_CHUNKS), None).reshape(B, S, BAND_WIDTH)
    out = merge_and_finish(h, oa, ob, za, zb, ga, gb, p, w_o_win, w_o_band, w_out, g_ple, w_ple_gate, w_ple)
    la, lb = min(WINDOW, S), min(BAND, S)
    return out, ka[:, S - la:], va[:, S - la:], kb[:, S - lb:], vb[:, S - lb:]


def sample_layer(h, p, ck_win, cv_win, ck_band, cv_band, lw):
    (g_in, w_in, g_q_win, g_k_win, sink_win, g_q_band, g_k_band, rel_bias_band,
     w_o_win, w_o_band, w_out, g_ple, w_ple_gate, w_ple) = lw
    B, S, _ = h.shape
    qa, ka, va, za, qb, kb, vb, zb, ga, gb = project(h, g_in, w_in, g_q_win, g_k_win, g_q_band, g_k_band)
    ka_all = jnp.concatenate([ck_win, ka], axis=1)
    va_all = jnp.concatenate([cv_win, va], axis=1)
    kb_all = jnp.concatenate([ck_band, kb], axis=1)
    vb_all = jnp.concatenate([cv_band, vb], axis=1)
    la, lb = ck_win.shape[1], ck_band.shape[1]
    oa = band_attention(qa[:, None], ka_all[:, None], va_all[:, None],
                        alibi_bias(la, S, la + S), None, sink_win)[:, 0]
    ob = band_attention(qb[:, None], kb_all[:, None], vb_all[:, None],
                        rel_pos_bias(rel_bias_band, lb, S, lb + S), None, None)[:, 0]
    out = merge_and_finish(h, oa, ob, za, zb, ga, gb, p, w_o_win, w_o_band, w_out, g_ple, w_ple_gate, w_ple)
    na, nb = ka_all.shape[1], kb_all.shape[1]
    ra, rb = min(WINDOW, na), min(BAND, nb)
    return out, ka_all[:, na - ra:], va_all[:, na - ra:], kb_all[:, nb - rb:], vb_all[:, nb - rb:]


def setup_inputs(seed: int = 0) -> dict:
    key = jax.random.key(seed)
    ks = jax.random.split(key, 22)
    nrm = lambda k, shape, s=1.0: s * jax.random.normal(k, shape, jnp.float32)
    la, lb = min(WINDOW, PAST_LEN), min(BAND, PAST_LEN)
    return {
        "x_prompt": nrm(ks[0], (BATCH, SEQ, D_MODEL)),
        "x_sample": nrm(ks[1], (DEC_BATCH, DEC_SEQ, D_MODEL)),
        "cache_k_win": nrm(ks[2], (DEPTH, DEC_BATCH, la, WIN_KV_HEADS, HEAD_DIM)),
        "cache_v_win": nrm(ks[3], (DEPTH, DEC_BATCH, la, WIN_KV_HEADS, HEAD_DIM)),
        "cache_k_band": nrm(ks[4], (DEPTH, DEC_BATCH, lb, BAND_HEADS, HEAD_DIM)),
        "cache_v_band": nrm(ks[5], (DEPTH, DEC_BATCH, lb, BAND_HEADS, HEAD_DIM)),
        "p_prompt": nrm(ks[6], (DEPTH, BATCH, SEQ, PLE_DIM)),
        "p_sample": nrm(ks[7], (DEPTH, DEC_BATCH, DEC_SEQ, PLE_DIM)),
        "g_in": 1.0 + nrm(ks[8], (DEPTH, D_MODEL), 0.05),
        "w_in": nrm(ks[9], (DEPTH, D_MODEL, IN_COLS), D_MODEL ** -0.5),
        "g_q_win": 1.0 + nrm(ks[10], (DEPTH, HEAD_DIM), 0.05),
        "g_k_win": 1.0 + nrm(ks[11], (DEPTH, HEAD_DIM), 0.05),
        "sink_win": nrm(ks[12], (DEPTH, WIN_HEADS), 0.5),
        "g_q_band": 1.0 + nrm(ks[13], (DEPTH, HEAD_DIM), 0.05),
        "g_k_band": 1.0 + nrm(ks[14], (DEPTH, HEAD_DIM), 0.05),
        "rel_bias_band": nrm(ks[15], (DEPTH, BAND_HEADS, 2 * REL_CLIP + 1), 0.1),
        "w_o_win": nrm(ks[16], (DEPTH, WIN_WIDTH, D_MODEL), WIN_WIDTH ** -0.5),
        "w_o_band": nrm(ks[17], (DEPTH, BAND_WIDTH, D_MODEL), BAND_WIDTH ** -0.5),
        "w_out": nrm(ks[18], (DEPTH, D_MODEL, D_MODEL), D_MODEL ** -0.5),
        "g_ple": 1.0 + nrm(ks[19], (DEPTH, D_MODEL), 0.05),
        "w_ple_gate": nrm(ks[20], (DEPTH, D_MODEL, D_MODEL), D_MODEL ** -0.5),
        "w_ple": nrm(ks[21], (DEPTH, PLE_DIM, D_MODEL), PLE_DIM ** -0.5),
    }


def reference(x_prompt, x_sample, cache_k_win, cache_v_win, cache_k_band, cache_v_band,
              p_prompt, p_sample, g_in, w_in, g_q_win, g_k_win, sink_win, g_q_band, g_k_band,
              rel_bias_band, w_o_win, w_o_band, w_out, g_ple, w_ple_gate, w_ple):
    h_p, h_s = x_prompt, x_sample
    kwp, vwp, kbp, vbp, kws, vws, kbs, vbs = [], [], [], [], [], [], [], []
    for i in range(DEPTH):
        lw = (g_in[i], w_in[i], g_q_win[i], g_k_win[i], sink_win[i], g_q_band[i], g_k_band[i],
              rel_bias_band[i], w_o_win[i], w_o_band[i], w_out[i], g_ple[i], w_ple_gate[i], w_ple[i])
        h_p, a, b, c, d = prompt_layer(h_p, p_prompt[i], lw)
        kwp.append(a); vwp.append(b); kbp.append(c); vbp.append(d)
        h_s, a, b, c, d = sample_layer(h_s, p_sample[i], cache_k_win[i], cache_v_win[i],
                                       cache_k_band[i], cache_v_band[i], lw)
        kws.append(a); vws.append(b); kbs.append(c); vbs.append(d)
    return (h_p, h_s,
            jnp.stack(kwp), jnp.stack(vwp), jnp.stack(kbp), jnp.stack(vbp),
            jnp.stack(kws), jnp.stack(vws), jnp.stack(kbs), jnp.stack(vbs))
```

```python
import numpy as np
from contextlib import ExitStack

import concourse.bass as bass
import concourse.mybir as mybir
from concourse.bass_utils import run_bass_kernel_spmd

F32 = mybir.dt.float32
BF16 = mybir.dt.bfloat16
AF = mybir.ActivationFunctionType
ALU = mybir.AluOpType
AX = mybir.AxisListType

D = 1024
INC = 5376
PLE = 256
EPS = 1e-6
NEG = -30000.0
RA = 2
RB = 5
SAME_ENGINE_SYNC = True
ATTACH_WAIT = True
RESCHEDULE = True
SCHED_WINDOW = 400
P_BOOST = 200
INPROJ_SPLIT = 2


class _Op:
    __slots__ = ("eng", "fn", "deps", "dma", "signal", "val", "sem", "order", "cost", "lat", "users", "ndeps",
                 "ready", "start", "tag", "prio")


class Sched:
    ENGS = ("pe", "act", "dve", "pool", "sp")

    def __init__(self):
        self.streams = {e: [] for e in self.ENGS}
        self.lastw = {}
        self.readers = {}
        self.n = 0
        self.dma_tot = {}

    LIMIT = None

    def op(self, eng, fn, r=(), w=(), dma=None, cost=0.3, lat=0.0):
        if Sched.LIMIT is not None and self.n >= Sched.LIMIT:
            return None
        o = _Op()
        o.cost, o.lat = cost, lat
        o.tag = getattr(self, "cur", None)
        o.prio = self.n - getattr(self, "boost", 0)
        o.eng, o.fn, o.dma, o.signal, o.val, o.sem = eng, fn, dma, False, None, None
        o.order = self.n
        self.n += 1
        w = list(w) + [res for res in r if res.startswith("ps") and res not in w]
        deps = {}
        for res in r:
            d = self.lastw.get(res)
            if d is not None:
                deps[d.order] = d
        for res in w:
            d = self.lastw.get(res)
            if d is not None:
                deps[d.order] = d
            for rd in self.readers.get(res, ()):
                deps[rd.order] = rd
        for res in w:
            self.lastw[res] = o
            self.readers[res] = []
        ws = set(w)
        for res in r:
            if res not in ws:
                self.readers.setdefault(res, []).append(o)
        o.deps = list(deps.values())
        for d in o.deps:
            d.signal = True
        if dma is not None:
            o.signal = True
        self.streams[eng].append(o)
        return o

    @staticmethod
    def _fsize(ap):
        try:
            n = 1
            for d in ap.shape[1:]:
                n *= int(d)
            return n
        except Exception:
            return 256

    def ins(self, eng, method, *args, r=(), w=(), dma=None, lat_override=None, **kw):
        out = kw.get("out", args[0] if args else None)
        n = self._fsize(out) if out is not None else 256
        lat = 0.0
        if dma is not None:
            cost = 0.45 if eng == "sp" else 1.0
            nbytes = n * out.shape[0] * 4
            lat = 2.5 + nbytes / 150e3
            if lat_override is not None:
                lat = lat_override
        elif eng == "pe":
            cost = n / 2000.0 + (0.02 if n < 512 else 0.0)
        elif eng == "act":
            cost = 0.18 + n / 1200.0
        elif eng == "dve":
            cost = 0.15 + n / 950.0
        else:
            cost = (1.0 if kw.get("op", None) == ALU.pow else 0.3 + n / 600.0)
        def fn(e, hook=None):
            inst = getattr(e, method)(*args, **kw)
            if hook is not None:
                hook(inst)
            return inst
        return self.op(eng, fn, r=r, w=w, dma=dma, cost=cost, lat=lat)

    def group(self, eng, calls, r=(), w=()):
        def fn(e, hook=None):
            last = None
            for i_, (m, a, k) in enumerate(calls):
                last = getattr(e, m)(*a, **k)
                if i_ == 0 and hook is not None:
                    hook(last)
            return last
        cost = 0.0
        for m, a, k in calls:
            n = 128 if m == "transpose" else self._fsize(a[2])
            cost += n / 2000.0 + (0.02 if n < 512 else 0.0)
        return self.op(eng, fn, r=r, w=w, cost=cost)

    def reschedule(self):
        allops = sorted((o for s in self.streams.values() for o in s), key=lambda o: o.order)
        for o in allops:
            o.users = []
        for o in allops:
            o.ndeps = len(o.deps)
            o.ready = 0.0
            for d in o.deps:
                d.users.append(o)
        finish = {}
        free = {e: 0.0 for e in self.ENGS}
        cand = {e: [] for e in self.ENGS}
        for o in allops:
            if o.ndeps == 0:
                cand[o.eng].append(o)
        newstreams = {e: [] for e in self.ENGS}
        remaining = len(allops)
        WINDOW = SCHED_WINDOW
        oldest = 0
        scheduled = set()
        while remaining:
            best = None
            while oldest < len(allops) and allops[oldest].order in scheduled:
                oldest += 1
            lim = allops[oldest].order + WINDOW if oldest < len(allops) else 1 << 60
            for e in self.ENGS:
                c = cand[e]
                if not c:
                    continue
                bo, bk = None, None
                for o in c:
                    if o.order > lim:
                        continue
                    k = (max(free[e], o.ready), o.prio)
                    if bk is None or k < bk:
                        bo, bk = o, k
                if bo is not None and (best is None or bk < best[1]):
                    best = (bo, bk)
            o, (t0, _) = best
            e = o.eng
            cand[e].remove(o)
            o.start = t0
            t1 = t0 + o.cost
            free[e] = t1
            fin = t1 + o.lat + 0.2
            newstreams[e].append(o)
            scheduled.add(o.order)
            remaining -= 1
            for u in o.users:
                u.ndeps -= 1
                if fin > u.ready:
                    u.ready = fin
                if u.ndeps == 0:
                    cand[u.eng].append(u)
        self.streams = newstreams
        self.sim_time = max(free.values())

    def assign(self, sems, dma_sems):
        cnt = {e: 0 for e in self.ENGS}
        dcnt = {}
        for e in self.ENGS:
            for o in self.streams[e]:
                if o.dma is not None:
                    key, n = o.dma
                    dcnt[key] = dcnt.get(key, 0) + 16 * n
                    o.sem, o.val = dma_sems[key], dcnt[key]
                elif o.signal:
                    cnt[o.eng] += 1
                    o.sem, o.val = sems[o.eng], cnt[o.eng]
        self.dma_tot = dcnt

    def emit_stream(self, eng, e):
        seen = {}
        for o in self.streams[eng]:
            waits = {}
            for d in o.deps:
                if d.dma is None and d.eng == eng:
                    if eng == "pe" or not SAME_ENGINE_SYNC:
                        continue
                k = id(d.sem)
                if seen.get(k, -1) >= d.val:
                    continue
                if k not in waits or waits[k][1] < d.val:
                    waits[k] = (d.sem, d.val)
            wl = list(waits.values())
            for sem, val in wl:
                seen[id(sem)] = val
            hook = None
            if ATTACH_WAIT and wl and o.dma is None:
                s0, v0 = wl.pop()
                hook = (lambda inst, s0=s0, v0=v0: inst._wait_ge(s0, v0))
            for sem, val in wl:
                e.wait_ge(sem, val)
            res = o.fn(e, hook)
            if o.dma is not None:
                lst = res if isinstance(res, (list, tuple)) else [res]
                assert len(lst) == o.dma[1], (len(lst), o.dma)
                for i_ in lst:
                    i_.then_inc(o.sem, 16)
            elif o.signal:
                last = res[-1] if isinstance(res, (list, tuple)) else res
                last.then_inc(o.sem, 1)


def _geometry():
    j = np.arange(128)[:, None]
    i = np.arange(128)[None, :]
    slopes = (2.0 ** (-np.arange(1, 9))).astype(np.float32)
    m0 = (i >= 64) & (j < 64)
    m1 = (j >= 64) & (i < 64)
    d0 = (128 + i - j).astype(np.float32)
    d1 = np.abs(i - j).astype(np.float32)
    tabA = np.zeros((128, 2, 8, 128), np.float32)
    for h in range(8):
        tabA[:, 0, h, :] = np.where(m0, NEG, -slopes[h] * d0)
        tabA[:, 1, h, :] = np.where(m1, NEG, -slopes[h] * d1)
    idx0 = np.clip(128 + i - j, -128, 128) + 128
    idx1 = np.clip(i - j, -128, 128) + 128
    maskT = np.zeros((128, 8, 128), np.float32)
    maskT[:] = np.where(m1, NEG, 0.0)[:, None, :]
    return tabA.reshape(128, 2048), idx0, idx1, maskT.reshape(128, 1024)


def build(NT, stage=9):
    nc = bass.Bass("TRN2", target_bir_lowering=False)
    S = Sched()
    TPC = NT * 128
    RA_, RB_ = 3, 6
    NWR = 6
    NXB = 4

    def din(name, shape):
        return nc.dram_tensor(name, list(shape), F32, kind="ExternalInput").ap()

    def dout(name, shape):
        return nc.dram_tensor(name, list(shape), F32, kind="ExternalOutput").ap()

    xh = din("xh", [512, D]); xm = din("xm", [TPC, D]); pm = din("pm", [TPC, PLE])
    xs = din("xs", [128, D]); psd = din("psd", [128, PLE])
    ckw = din("ckw", [128, 128]); cvw = din("cvw", [128, 128])
    ckb = din("ckb", [512, 512]); cvb = din("cvb", [512, 512])
    w_in = din("w_in", [D, INC]); w_ow = din("w_ow", [512, D]); w_ob = din("w_ob", [512, D])
    w_out = din("w_out", [D, D]); w_pg = din("w_pg", [D, D]); w_pl = din("w_pl", [PLE, D])
    identd = din("ident", [128, 128]); tabAd = din("tabA", [128, 2048]); tabBd = din("tabB", [128, 2048])
    maskTd = din("maskT", [128, 1024]); cbd = din("cbrow", [1, 8]); hmaskd = din("hmask", [128, 4])
    gTind = din("gTin", [128, 8]); gTpled = din("gTple", [128, 8]); gcold = din("gcol", [128, 4])
    gkwbd = din("gkwb", [1, 64]); gkbbd = din("gkbb", [1, 64]); sinkd = din("sinkrow", [1, 8])
    wsc = nc.dram_tensor("wsc", [2 * D, D], BF16, kind="Internal").ap()

    y = dout("y", [TPC, D]); ys = dout("ys", [64, D])
    kwin = dout("kwin", [128, 128]); vwin = dout("vwin", [128, 128])
    kband = dout("kband", [512, 512]); vband = dout("vband", [512, 512])
    kws = dout("kws", [128, 128]); vws = dout("vws", [128, 128])
    kbs = dout("kbs", [512, 512]); vbs = dout("vbs", [512, 512])

    es = ExitStack()

    def sb(name, shape, dt):
        return es.enter_context(nc.sbuf_tensor(name, list(shape), dt))

    def pst(name, shape, dt):
        return es.enter_context(nc.psum_tensor(name, list(shape), dt))

    W_in = sb("W_in", [128, 8, INC], BF16)
    W_ow = sb("W_ow", [128, 4, D], BF16); W_ob = sb("W_ob", [128, 4, D], BF16)
    W_pl = sb("W_pl", [128, 2, D], BF16)
    wring = sb("wring", [128, NWR, D], BF16)
    ident = sb("identb", [128, 128], BF16)
    tabA = sb("tabAs", [128, 2048], BF16); tabB = sb("tabBs", [128, 2048], BF16)
    gTin = sb("gTins", [128, 8], F32); gTple = sb("gTples", [128, 8], F32); gcol = sb("gcols", [128, 4], F32)
    gkwb = sb("gkwbs", [128, 64], F32); gkbb = sb("gkbbs", [128, 64], F32)
    esink = sb("esink", [128, 8], F32); hmask = sb("hmasks", [128, 4], F32); cb = sb("cbs", [128, 8], F32)
    kaT = sb("kaT", [128, RA_, 128], BF16); kbT = sb("kbT", [128, RB_, 512], BF16)
    va = sb("va", [128, RA_, 2, 65], BF16); vb = sb("vb", [128, RB_, 8, 65], BF16)
    xb = [sb(f"xb{i}", [128, D], F32) for i in range(NXB)]
    pb = [sb(f"pb{i}", [128, PLE], F32) for i in range(2)]
    xn = sb("xn", [128, D], BF16); xnT = sb("xnT", [128, D], BF16)
    ybufs = [sb(f"ybuf{i}", [128, D], BF16) for i in range(2)]
    mbuf = sb("mbuf", [128, D], BF16); actT = sb("actT", [128, D], BF16)
    pbf = sb("pbf", [128, PLE], BF16); pT = sb("pT", [128, PLE], BF16)
    qkn = sb("qkn", [128, 1664], BF16)
    qaT = [sb(f"qaT{i}", [128, 512], BF16) for i in range(2)]
    qbT = [sb(f"qbT{i}", [128, 512], BF16) for i in range(2)]
    zs = [sb(f"zs{i}", [128, D], BF16) for i in range(2)]
    tg = [sb(f"tg{i}", [128, 2048], BF16) for i in range(2)]
    ET = [sb(f"ET{i}", [128, 512], BF16) for i in range(3)]
    sqt = sb("sqt", [128, 512], F32); tzt = sb("tzt", [128, 512], F32)
    kvft = sb("kvft", [128, 1280], F32); t1t = sb("t1t", [128, D], F32); ont = sb("ont", [128, 256], F32)
    st = sb("st", [128, 64], F32)
    stq = sb("stq", [128, 16], F32)
    mhalf = sb("mhalf", [128, 8], F32)

    NMM = 3
    ps_mm = [pst(f"psmm{i}", [128, 512], F32) for i in range(NMM)]
    ps_s = [pst(f"pss{i}", [128, 512], F32) for i in range(2)]
    ps_o1 = pst("pso0", [128, 512], F32)
    ps_tl = [pst(f"pstl{i}", [128, 512], F32) for i in range(2)]

    sq = sqt[:]; tz = tzt[:]; kvf = kvft[:]; t1 = t1t[:]; on = ont[:]

    def bc3(ap2, n):
        k = ap2.shape[1]
        return ap2.unsqueeze(2).to_broadcast([128, k, n])

    def v3(ap2, inner):
        return ap2.rearrange("p (a b) -> p a b", b=inner)

    S.ins("sp", "dma_start", out=xb[0][:], in_=xh[0:128, :], w=["xb0"], dma=("ld0", 1))
    S.ins("sp", "dma_start", out=xb[1][:], in_=xh[128:256, :], w=["xb1"], dma=("ld1", 1))
    S.ins("pool", "dma_start", out=ident[:], in_=identd, w=["ident"], dma=("c_ident", 1))
    for nm, dst, src in (("gTin", gTin, gTind), ("gTple", gTple, gTpled), ("gcol", gcol, gcold),
                         ("hmask", hmask, hmaskd)):
        S.ins("sp", "dma_start", out=dst[:], in_=src, w=[nm], dma=("c_" + nm, 1))
    S.ins("sp", "dma_start", out=gkwb[:], in_=gkwbd.to_broadcast([128, 64]), w=["gkwb"], dma=("c_gkwb", 1))
    S.ins("sp", "dma_start", out=gkbb[:], in_=gkbbd.to_broadcast([128, 64]), w=["gkbb"], dma=("c_gkbb", 1))
    S.ins("sp", "dma_start", out=esink[:], in_=sinkd.to_broadcast([128, 8]), w=["esink"], dma=("c_esink", 1))
    S.ins("sp", "dma_start", out=cb[:], in_=cbd.to_broadcast([128, 8]), w=["cb"], dma=("c_cb", 1))
    S.ins("sp", "dma_start", out=kvf[:, 0:1024], in_=tabBd[:, 0:1024], w=["kvf"], dma=("c_tb0", 1))
    S.ins("sp", "dma_start", out=t1[:, 0:1024], in_=tabBd[:, 1024:2048], w=["t1"], dma=("c_tb1", 1))
    mstage = tg[0][:].bitcast(F32)
    MST = [f"tg0_{i}" for i in range(4)]
    S.ins("sp", "dma_start", out=mstage, in_=maskTd, w=MST, dma=("c_mt", 1))
    w_in3 = w_in.rearrange("(c p) n -> p c n", p=128)
    WRANGES = [(512, 768), (1792, 2816), (0, 512), (768, 1792), (2816, 4096), (4096, INC)]
    cumb = [0.0]

    def wlat(nbytes):
        cumb[0] += nbytes
        return 3.0 + cumb[0] / 250e3
    for ri, (c0_, c1_) in enumerate(WRANGES):
        S.ins("pool", "dma_start", out=W_in[:, :, c0_:c1_], in_=w_in3[:, :, c0_:c1_],
              w=[f"W_in_r{ri}"], dma=(f"w_in_r{ri}", 1), lat_override=wlat(D * (c1_ - c0_) * 4))

    S.ins("pool", "dma_start", out=tabA[:], in_=tabAd, w=["tabA"], dma=("c_tabA", 1))

    def win_res(col0, ncols):
        return [f"W_in_r{ri}" for ri, (a_, b_) in enumerate(WRANGES) if a_ < col0 + ncols and col0 < b_]
    for nm, dst, src, kc in (("W_ow", W_ow, w_ow, 4), ("W_ob", W_ob, w_ob, 4), ("W_pl", W_pl, w_pl, 2)):
        for c in range(kc):
            S.ins("pool", "dma_start", out=dst[:, c, :], in_=src[c * 128:(c + 1) * 128, :],
                  w=[f"{nm}{c}"], dma=(f"w_{nm}{c}", 1), lat_override=wlat(128 * D * 4))
    S.ins("pool", "dma_start", out=wsc[0:D, :], in_=w_out, w=["wsc0"], dma=("w2a", 1), lat_override=wlat(D * D * 4))
    S.ins("pool", "dma_start", out=wsc[D:2 * D, :], in_=w_pg, w=["wsc1"], dma=("w2b", 1), lat_override=wlat(D * D * 4))
    S.ins("dve", "tensor_scalar", out=gTin[:], in0=gTin[:], scalar1=32.0, scalar2=None, op0=ALU.mult,
          r=["gTin"], w=["gTin"])
    S.ins("dve", "tensor_scalar", out=gTple[:], in0=gTple[:], scalar1=32.0, scalar2=None, op0=ALU.mult,
          r=["gTple"], w=["gTple"])
    for col in (1, 3):
        S.ins("dve", "tensor_scalar", out=gcol[:, col:col + 1], in0=gcol[:, col:col + 1], scalar1=8.0,
              scalar2=None, op0=ALU.mult, r=["gcol"], w=["gcol"])
    S.ins("dve", "tensor_scalar", out=gkwb[:], in0=gkwb[:], scalar1=8.0, scalar2=None, op0=ALU.mult,
          r=["gkwb"], w=["gkwb"])
    S.ins("dve", "tensor_scalar", out=gkbb[:], in0=gkbb[:], scalar1=8.0, scalar2=None, op0=ALU.mult,
          r=["gkbb"], w=["gkbb"])
    S.ins("pool", "memset", mhalf[:], -0.5, w=["mhalf"])
    S.ins("pool", "memset", va[:], 1.0, w=[f"va{i}" for i in range(RA_)])
    S.ins("pool", "memset", vb[:], 1.0, w=[f"vb{i}" for i in range(RB_)])

    tiles = []
    for i in range(4):
        tiles.append(dict(kind="halo", vt=i - 4, x=xh[i * 128:(i + 1) * 128, :], p=None))
    for n in range(NT):
        tiles.append(dict(kind="main", vt=n, n=n, x=xm[n * 128:(n + 1) * 128, :], p=pm[n * 128:(n + 1) * 128, :]))
    SB = NT + 8
    tiles.append(dict(kind="sample", vt=SB + 4, x=xs, p=psd))
    for li, t in enumerate(tiles):
        t["li"] = li

    def issue_load(t):
        bi = t["li"] % NXB
        S.ins("sp", "dma_start", out=xb[bi][:], in_=t["x"], w=[f"xb{bi}"], dma=(f"ld{bi}", 1))

    def issue_pload(t):
        pi = t["li"] % 2
        S.ins("sp", "dma_start", out=pb[pi][:], in_=t["p"], w=[f"pb{pi}"], dma=(f"ldp{pi}", 1))

    mmc = [0]

    def next_mm():
        b = mmc[0] % NMM
        mmc[0] += 1
        return ps_mm[b], f"psmm{b}"

    def next_tr():
        bank, res = next_mm()
        return bank[:].bitcast(BF16), res

    def inproj_ct(col0, ncols):
        bank, bres_ = next_mm()
        b = int(bres_[-1])
        for g0 in range(0, 8, INPROJ_SPLIT):
            calls = []
            for c in range(g0, g0 + INPROJ_SPLIT):
                calls.append(("matmul", (bank[:, 0:ncols], xnT[:, c * 128:(c + 1) * 128], W_in[:, c, col0:col0 + ncols]),
                              dict(start=(c == 0), stop=(c == 7))))
            S.group("pe", calls, r=["xnT"] + win_res(col0, ncols), w=[f"psmm{b}"])
        return bank, f"psmm{b}"

    def qknorm(bank, bres, c0, nh, dst, dst_res, sscol, kvf_ap=None, gtab=None, permute=False):
        w = 64 * nh
        S.ins("act", "activation", out=sq[:, 0:w], in_=bank[:, c0:c0 + w], func=AF.Square, r=[bres], w=["sq"])
        S.ins("dve", "tensor_reduce", out=st[:, sscol:sscol + nh], in_=v3(sq[:, 0:w], 64), axis=AX.X, op=ALU.add,
              r=["sq"], w=["st_ss"])
        S.ins("dve", "tensor_scalar", out=st[:, sscol:sscol + nh], in0=st[:, sscol:sscol + nh],
              scalar1=64 * EPS, scalar2=None, op0=ALU.add, r=["st_ss"], w=["st_ss"])
        S.ins("pool", "tensor_tensor", out=st[:, 32 + sscol:32 + sscol + nh], in0=st[:, sscol:sscol + nh],
              in1=mhalf[:, 0:nh], op=ALU.pow, r=["st_ss", "mhalf"], w=["st_rs"])
        rsb = bc3(st[:, 32 + sscol:32 + sscol + nh], 64)
        if permute:
            o_ap = dst.rearrange("p (i kv d) -> p kv i d", kv=2, d=64)
            i_ap = bank[:, c0:c0 + w].rearrange("p (kv i d) -> p kv i d", kv=2, d=64)
            r_ap = st[:, 32 + sscol:32 + sscol + nh].rearrange("p (kv i) -> p kv i", kv=2).unsqueeze(3).to_broadcast(
                [128, 2, 4, 64])
            S.ins("dve", "tensor_tensor", out=o_ap, in0=i_ap, in1=r_ap, op=ALU.mult, r=[bres, "st_rs"], w=[dst_res])
        else:
            S.ins("dve", "tensor_tensor", out=v3(dst, 64), in0=v3(bank[:, c0:c0 + w], 64), in1=rsb, op=ALU.mult,
                  r=[bres, "st_rs"], w=[dst_res])
        if kvf_ap is not None:
            S.ins("dve", "tensor_tensor", out=v3(kvf_ap, 64), in0=v3(bank[:, c0:c0 + w], 64), in1=rsb, op=ALU.mult,
                  r=[bres, "st_rs"], w=["kvf"])
            S.ins("pool", "tensor_tensor", out=v3(kvf_ap, 64), in0=v3(kvf_ap, 64),
                  in1=gtab.unsqueeze(1).to_broadcast([128, nh, 64]), op=ALU.mult, r=["kvf"], w=["kvf"])

    def stage_P(t):
        kind = t["kind"]
        vt = t["vt"]
        X = xb[t["li"] % NXB]
        xres = f"xb{t['li'] % NXB}"
        par = t["li"] % 2
        kvout = (kind == "sample") or (kind == "main" and t["n"] >= NT - 4)
        sa, sbn = vt % RA_, vt % RB_
        S.boost = P_BOOST
        S.ins("act", "activation", out=xn[:], in_=X[:], func=AF.Square, accum_out=st[:, 60:61], r=[xres],
              w=["xn", "st_x"])
        S.ins("dve", "tensor_scalar", out=st[:, 60:61], in0=st[:, 60:61], scalar1=1024 * EPS, scalar2=None,
              op0=ALU.add, r=["st_x"], w=["st_x"])
        S.ins("pool", "tensor_tensor", out=st[:, 61:62], in0=st[:, 60:61], in1=mhalf[:, 0:1], op=ALU.pow,
              r=["st_x", "mhalf"], w=["st_rx"])
        S.ins("act", "mul", xn[:], X[:], st[:, 61:62], r=[xres, "st_rx"], w=["xn"])
        trx, trres = next_tr()
        S.group("pe", [("transpose", (trx[:, c * 128:(c + 1) * 128], xn[:, c * 128:(c + 1) * 128], ident[:]), {})
                       for c in range(8)], r=["xn", "ident"], w=[trres])
        S.ins("dve", "tensor_tensor", out=v3(xnT[:], 128), in0=v3(trx[:, :], 128), in1=bc3(gTin[:, 0:8], 128),
              op=ALU.mult, r=[trres, "gTin"], w=["xnT"])
        S.boost = 0
        yield
        if kind != "halo":
            bank, br = inproj_ct(0, 512)
            qknorm(bank, br, 0, 8, qkn[:, 0:512], "qkn_qa", 0, permute=True)
            yield
        bank, br = inproj_ct(512, 256)
        qknorm(bank, br, 0, 2, qkn[:, 1024:1152], "qkn_ka", 8,
               kvf_ap=(kvf[:, 0:128] if kvout else None), gtab=gkwb[:])
        S.ins("act", "copy", va[:, sa, :, 0:64], v3(bank[:, 128:256], 64), r=[br], w=[f"va{sa}"])
        if kvout:
            S.ins("dve", "tensor_copy", out=kvf[:, 128:256], in_=bank[:, 128:256], r=[br], w=["kvf"])
        yield
        if kind != "halo":
            bank, br = inproj_ct(768, 512)
            S.ins("act", "activation", out=tz, in_=bank[:], func=AF.Tanh, scale=0.5, r=[br], w=["tz"])
            S.ins("dve", "scalar_tensor_tensor", out=zs[par][:, 0:512], in0=tz, scalar=1.0, in1=bank[:], op0=ALU.add,
                  op1=ALU.mult, r=["tz", br], w=[f"zs{par}a"])
            yield
            bank, br = inproj_ct(1280, 512)
            qknorm(bank, br, 0, 8, qkn[:, 512:1024], "qkn_qb", 10)
            yield
        bank, br = inproj_ct(1792, 512)
        qknorm(bank, br, 0, 8, qkn[:, 1152:1664], "qkn_kb", 18,
               kvf_ap=(kvf[:, 256:768] if kvout else None), gtab=gkbb[:])
        yield
        bank, br = inproj_ct(2304, 512)
        S.ins("act", "copy", vb[:, sbn, :, 0:64], v3(bank[:], 64), r=[br], w=[f"vb{sbn}"])
        if kvout:
            S.ins("dve", "tensor_copy", out=kvf[:, 768:1280], in_=bank[:], r=[br], w=["kvf"])
        yield
        if kind != "halo":
            bank, br = inproj_ct(2816, 512)
            S.ins("act", "activation", out=tz, in_=bank[:], func=AF.Tanh, scale=0.5, r=[br], w=["tz"])
            S.ins("dve", "scalar_tensor_tensor", out=zs[par][:, 512:1024], in0=tz, scalar=1.0, in1=bank[:],
                  op0=ALU.add, op1=ALU.mult, r=["tz", br], w=[f"zs{par}b"])
            yield
            for gi in range(4):
                bank, br = inproj_ct(3328 + gi * 512, 512)
                S.ins("act", "activation", out=tg[par][:, gi * 512:(gi + 1) * 512], in_=bank[:], func=AF.Tanh,
                      scale=0.5, r=[br], w=[f"tg{par}_{gi}"])
                yield
        if kvout:
            if kind == "main":
                r0 = (t["n"] - (NT - 4)) * 128
                S.ins("sp", "dma_start", out=kband[r0:r0 + 128, :], in_=kvf[:, 256:768], r=["kvf"], dma=("kv", 1))
                S.ins("sp", "dma_start", out=vband[r0:r0 + 128, :], in_=kvf[:, 768:1280], r=["kvf"], dma=("kv", 1))
                if t["n"] == NT - 1:
                    S.ins("sp", "dma_start", out=kwin[:, :], in_=kvf[:, 0:128], r=["kvf"], dma=("kv", 1))
                    S.ins("sp", "dma_start", out=vwin[:, :], in_=kvf[:, 128:256], r=["kvf"], dma=("kv", 1))
            else:
                S.ins("sp", "dma_start", out=kbs[448:512, :], in_=kvf[0:64, 256:768], r=["kvf"], dma=("kv", 1))
                S.ins("sp", "dma_start", out=vbs[448:512, :], in_=kvf[0:64, 768:1280], r=["kvf"], dma=("kv", 1))
                S.ins("sp", "dma_start", out=kws[64:128, :], in_=kvf[0:64, 0:128], r=["kvf"], dma=("kv", 1))
                S.ins("sp", "dma_start", out=vws[64:128, :], in_=kvf[0:64, 128:256], r=["kvf"], dma=("kv", 1))
        if kind != "halo":
            trx, trres = next_tr()
            S.group("pe", [("transpose", (trx[:, c * 128:(c + 1) * 128], qkn[:, c * 128:(c + 1) * 128], ident[:]), {})
                           for c in range(8)], r=["qkn_qa", "qkn_qb", "ident"], w=[trres])
            S.ins("act", "mul", qaT[par][:], trx[:, 0:512], gcol[:, 0:1], r=[trres, "gcol"], w=[f"qaT{par}"])
            S.ins("act", "mul", qbT[par][:], trx[:, 512:1024], gcol[:, 2:3], r=[trres, "gcol"], w=[f"qbT{par}"])
            yield
        trx, trres = next_tr()
        S.group("pe", [("transpose", (trx[:, c * 128:(c + 1) * 128], qkn[:, 1024 + c * 128:1152 + c * 128], ident[:]), {})
                       for c in range(5)], r=["qkn_ka", "qkn_kb", "ident"], w=[trres])
        S.ins("act", "mul", kaT[:, sa, :], trx[:, 0:128], gcol[:, 1:2], r=[trres, "gcol"], w=[f"kaT{sa}"])
        S.ins("act", "mul", kbT[:, sbn, :], trx[:, 128:640], gcol[:, 3:4], r=[trres, "gcol"], w=[f"kbT{sbn}"])
        yield

    wrc = [0]

    def stage_A(t):
        kind = t["kind"]
        vt = t["vt"]
        par = t["li"] % 2
        ybuf = ybufs[par]
        yres = f"ybuf{par}"
        uc = [0]
        obank = ps_o1
        ores = "pso0"
        pi = t["li"] % 2
        S.ins("pool", "tensor_copy", out=pbf[:], in_=pb[pi][:], r=[f"pb{pi}"], w=["pbf"])
        trp, trpres = ps_o1[:].bitcast(BF16), "pso0"
        S.group("pe", [("transpose", (trp[:, c * 128:(c + 1) * 128], pbf[:, c * 128:(c + 1) * 128], ident[:]), {})
                       for c in range(2)], r=["pbf", "ident"], w=[trpres])
        S.ins("act", "copy", pT[:], trp[:, 0:256], r=[trpres], w=["pT"])
        for mixer in ("A", "B"):
            kts = [vt - 1, vt] if mixer == "A" else [vt - 4, vt - 3, vt - 2, vt - 1, vt]
            for hh in range(2):
                for n_, kt in enumerate(kts):
                    u = uc[0]
                    uc[0] += 1
                    sbk = ps_s[u % 2]
                    sres = f"pss{u % 2}"
                    e_i = u % 3
                    E = ET[e_i]
                    eres = f"ET{e_i}"
                    rel = kt - vt
                    tab = None
                    if mixer == "A":
                        sl = kt % RA_
                        S.ins("pe", "matmul", sbk[:, 0:512], kaT[hh * 64:(hh + 1) * 64, sl, :],
                              qaT[par][hh * 64:(hh + 1) * 64, :], start=True, stop=True,
                              r=[f"kaT{sl}", f"qaT{par}"], w=[sres])
                        tab = v3(tabA[:, (rel + 1) * 1024 + hh * 512:(rel + 1) * 1024 + (hh + 1) * 512], 128)
                        tres = "tabA"
                    else:
                        sl = kt % RB_
                        calls = []
                        for hq in range(4):
                            pr, half = hq, hh
                            calls.append(("matmul", (sbk[:, hq * 128:(hq + 1) * 128],
                                                     kbT[half * 64:(half + 1) * 64, sl, pr * 128:(pr + 1) * 128],
                                                     qbT[par][half * 64:(half + 1) * 64, pr * 128:(pr + 1) * 128]),
                                          dict(start=True, stop=True)))
                        S.group("pe", calls, r=[f"kbT{sl}", f"qbT{par}"], w=[sres])
                        if rel >= -1:
                            tab = tabB[:, (rel + 1) * 1024:(rel + 2) * 1024].rearrange(
                                "p (hq two i) -> p two hq i", two=2, i=128)[:, hh]
                            tres = "tabB"
                    if tab is not None:
                        S.ins("dve", "tensor_tensor", out=v3(sbk[:], 128), in0=v3(sbk[:], 128), in1=tab, op=ALU.add,
                              r=[sres, tres], w=[sres])
                    if kind == "main" and kt < 0:
                        S.ins("act", "activation", out=E[:], in_=sbk[:], func=AF.Exp, bias=hmask[:, kt + 4:kt + 5],
                              r=[sres, "hmask"], w=[eres])
                    else:
                        S.ins("act", "activation", out=E[:], in_=sbk[:], func=AF.Exp, r=[sres], w=[eres])
                    if mixer == "B" and rel == -4:
                        S.ins("pool", "memset", v3(E[0:64, :], 128)[:, :, 64:128], 0.0, r=[eres], w=[eres])
                    calls = []
                    for hq in range(4):
                        if mixer == "A":
                            rhs = va[:, sl, hh, 0:65]
                        else:
                            rhs = vb[:, sl, 2 * hq + hh, 0:65]
                        calls.append(("matmul", (obank[:, hq * 65:(hq + 1) * 65], E[:, hq * 128:(hq + 1) * 128], rhs),
                                      dict(start=(n_ == 0 and hq == 0), stop=(n_ == len(kts) - 1 and hq == 3),
                                           skip_group_check=True)))
                    S.group("pe", calls, r=[eres, (f"va{sl}" if mixer == "A" else f"vb{sl}")], w=[ores])
                    yield
                o3 = obank[:, 0:260].rearrange("p (h e) -> p h e", e=65)
                if mixer == "A":
                    S.ins("dve", "tensor_tensor", out=stq[:, 4:8], in0=o3[:, :, 64], in1=esink[:, 4 * hh:4 * hh + 4],
                          op=ALU.add, r=[ores, "esink"], w=["st_den"])
                    S.ins("dve", "reciprocal", out=stq[:, 0:4], in_=stq[:, 4:8], r=["st_den"], w=["st_rd"])
                else:
                    S.ins("dve", "reciprocal", out=stq[:, 0:4], in_=o3[:, :, 64], r=[ores], w=["st_rd"])
                S.ins("dve", "tensor_tensor", out=v3(on, 64), in0=o3[:, :, 0:64], in1=bc3(stq[:, 0:4], 64), op=ALU.mult,
                      r=[ores, "st_rd"], w=["on"])
                if mixer == "A":
                    yc = hh * 256
                    y_ap, z_ap = v3(ybuf[:, yc:yc + 256], 64), v3(zs[par][:, yc:yc + 256], 64)
                else:
                    y_ap = ybuf[:, 512:1024].rearrange("p (hq two d) -> p two hq d", two=2, d=64)[:, hh]
                    z_ap = zs[par][:, 512:1024].rearrange("p (hq two d) -> p two hq d", two=2, d=64)[:, hh]
                S.ins("pool", "tensor_tensor", out=y_ap, in0=v3(on, 64), in1=z_ap, op=ALU.mult,
                      r=["on", f"zs{par}a" if mixer == "A" else f"zs{par}b"], w=[yres])
                yield

    def stage_Z(t):
        kind = t["kind"]
        X = xb[t["li"] % NXB]
        xres = f"xb{t['li'] % NXB}"
        par = t["li"] % 2
        pi = t["li"] % 2
        ybuf = ybufs[par]
        yres = f"ybuf{par}"
        TL = ["pstl0", "pstl1"]
        trz = ps_tl[0][:].bitcast(BF16)
        wslots = []
        for c in range(16):
            wslots.append(wrc[0] % NWR)
            wrc[0] += 1

        def wload(c):
            S.ins("sp", "dma_start", out=wring[:, wslots[c], :], in_=wsc[c * 128:(c + 1) * 128, :],
                  r=["wsc0" if c < 8 else "wsc1"], w=[f"wr{wslots[c]}"], dma=(f"wr{wslots[c]}", 1))
        for c in range(NWR):
            wload(c)
        wnext = [NWR]

        trzs = [ps_tl[0][:].bitcast(BF16), ps_tl[1][:].bitcast(BF16)]

        def transposes8(src, sres_, banks=None):
            sres_l = sres_ if isinstance(sres_, (list, tuple)) else [sres_, sres_]
            if banks is None:
                banks = [(trzs[0], TL[0]), (trzs[1], TL[1])]
            for hf in range(2):
                tb, tres_ = banks[hf]
                S.group("pe", [("transpose", (tb[:, c * 128:(c + 1) * 128],
                                              src[:, (4 * hf + c) * 128:(4 * hf + c + 1) * 128], ident[:]), {})
                               for c in range(4)], r=[sres_l[hf], "ident"], w=[tres_])
            return banks

        def evac_plain(banks):
            S.ins("act", "copy", actT[:, 0:512], banks[0][0][:, 0:512], r=[banks[0][1]], w=["actT_a"])
            S.ins("dve", "tensor_copy", out=actT[:, 512:1024], in_=banks[1][0][:, 0:512], r=[banks[1][1]], w=["actT_b"])

        def dense(Wt, wres, k0, kc, src, sres_):
            for nb in range(2):
                calls = []
                for c in range(kc):
                    calls.append(("matmul", (ps_tl[nb][:], src[:, (k0 + c) * 128:(k0 + c + 1) * 128],
                                             Wt[:, c, nb * 512:(nb + 1) * 512]),
                                  dict(start=(c == 0), stop=(c == kc - 1))))
                S.group("pe", calls, r=[sres_] + [f"{wres}{c}" for c in range(kc)], w=[TL[nb]])

        def dense_stream(c0):
            for c in range(8):
                slot = wslots[c0 + c]
                calls = []
                for nb in range(2):
                    calls.append(("matmul", (ps_tl[nb][:], actT[:, c * 128:(c + 1) * 128],
                                             wring[:, slot, nb * 512:(nb + 1) * 512]),
                                  dict(start=(c == 0), stop=(c == 7))))
                S.group("pe", calls, r=["actT_a" if c < 4 else "actT_b", f"wr{slot}"], w=[TL[0], TL[1]])
                if wnext[0] < 16:
                    wload(wnext[0])
                    wnext[0] += 1

        evac_plain(transposes8(ybuf, yres))
        yield
        dense(W_ow, "W_ow", 0, 4, actT, "actT_a")
        for nb in range(2):
            S.ins("dve", "scalar_tensor_tensor", out=t1[:, nb * 512:(nb + 1) * 512],
                  in0=tg[par][:, nb * 512:(nb + 1) * 512], scalar=1.0, in1=ps_tl[nb][:], op0=ALU.add, op1=ALU.mult,
                  r=[f"tg{par}_{nb}", TL[nb]], w=["t1"])
        yield
        dense(W_ob, "W_ob", 4, 4, actT, "actT_b")
        for nb in range(2):
            S.ins("dve", "scalar_tensor_tensor", out=ps_tl[nb][:],
                  in0=tg[par][:, 1024 + nb * 512:1024 + (nb + 1) * 512],
                  scalar=1.0, in1=ps_tl[nb][:], op0=ALU.add, op1=ALU.mult, r=[f"tg{par}_{2 + nb}", TL[nb]],
                  w=[TL[nb]])
            S.ins("dve", "tensor_tensor", out=mbuf[:, nb * 512:(nb + 1) * 512], in0=ps_tl[nb][:],
                  in1=t1[:, nb * 512:(nb + 1) * 512], op=ALU.add, r=[TL[nb], "t1"], w=[f"mbuf_{nb}"])
        evac_plain(transposes8(mbuf, ["mbuf_0", "mbuf_1"]))
        yield
        dense_stream(0)
        yield
        for nb in range(2):
            S.ins("dve", "scalar_tensor_tensor", out=X[:, nb * 512:(nb + 1) * 512], in0=ps_tl[nb][:], scalar=0.25,
                  in1=X[:, nb * 512:(nb + 1) * 512], op0=ALU.mult, op1=ALU.add, r=[TL[nb], xres], w=[xres])
        S.ins("act", "copy", mbuf[:, 0:512], X[:, 0:512], r=[xres], w=["mbuf_0"])
        S.ins("dve", "tensor_copy", out=mbuf[:, 512:1024], in_=X[:, 512:1024], r=[xres], w=["mbuf_1"])
        S.ins("act", "activation", out=ybuf[:], in_=X[:], func=AF.Square, accum_out=stq[:, 8:9], r=[xres],
              w=[yres, "st_h"])
        S.ins("dve", "tensor_scalar", out=stq[:, 8:9], in0=stq[:, 8:9], scalar1=1024 * EPS, scalar2=4.0,
              op0=ALU.add, op1=ALU.mult, r=["st_h"], w=["st_h"])
        S.ins("pool", "tensor_tensor", out=stq[:, 9:10], in0=stq[:, 8:9], in1=mhalf[:, 0:1], op=ALU.pow,
              r=["st_h", "mhalf"], w=["st_rh"])
        bk = transposes8(mbuf, ["mbuf_0", "mbuf_1"])
        for hf in range(2):
            S.ins("dve", "tensor_tensor", out=v3(actT[:, hf * 512:(hf + 1) * 512], 128), in0=v3(bk[hf][0][:, 0:512], 128),
                  in1=bc3(gTple[:, 4 * hf:4 * hf + 4], 128), op=ALU.mult, r=[bk[hf][1], "gTple"],
                  w=["actT_a" if hf == 0 else "actT_b"])
        yield
        dense_stream(8)
        yield
        for nb in range(2):
            S.ins("act", "activation", out=t1[:, nb * 512:(nb + 1) * 512], in_=ps_tl[nb][:], func=AF.Tanh,
                  scale=stq[:, 9:10], r=[TL[nb], "st_rh"], w=["t1"])
        dense(W_pl, "W_pl", 0, 2, pT, "pT")
        yield
        for nb in range(2):
            S.ins("dve", "scalar_tensor_tensor", out=ps_tl[nb][:], in0=t1[:, nb * 512:(nb + 1) * 512],
                  scalar=1.0, in1=ps_tl[nb][:], op0=ALU.add, op1=ALU.mult, r=["t1", TL[nb]], w=[TL[nb]])
            S.ins("dve", "scalar_tensor_tensor", out=X[:, nb * 512:(nb + 1) * 512], in0=ps_tl[nb][:], scalar=0.5,
                  in1=X[:, nb * 512:(nb + 1) * 512], op0=ALU.mult, op1=ALU.add, r=[TL[nb], xres], w=[xres])
        if kind == "main":
            n = t["n"]
            S.ins("sp", "dma_start", out=y[n * 128:(n + 1) * 128, :], in_=X[:], r=[xres], dma=("st_" + xres, 1))
        else:
            S.ins("sp", "dma_start", out=ys[:, :], in_=X[0:64, :], r=[xres], dma=("st_" + xres, 1))
        yield

    def stage_C():
        regions = [(0, "qkn_qa"), (512, "qkn_qb"), (1152, "qkn_kb"), (0, "qkn_qa")]
        for i in range(4):
            sl = (SB + i) % RB_
            off, rres = regions[i]
            S.ins("pool", "dma_start", out=qkn[:, off:off + 512], in_=ckb[i * 128:(i + 1) * 128, :], w=[rres],
                  dma=(f"ccK{i % 3}", 1))
            trx, trres = next_tr()
            S.group("pe", [("transpose", (trx[:, c * 128:(c + 1) * 128], qkn[:, off + c * 128:off + (c + 1) * 128], ident[:]), {})
                           for c in range(4)], r=[rres, "ident"], w=[trres])
            S.ins("act", "copy", kbT[:, sl, :], trx[:, 0:512], r=[trres], w=[f"kbT{sl}"])
            S.ins("pool", "dma_start", out=vb[:, sl, :, 0:64], in_=v3(cvb[i * 128:(i + 1) * 128, :], 64),
                  w=[f"vb{sl}"], dma=(f"ccV{i}", 1))
            yield
        sl = (SB + 3) % RA_
        S.ins("pool", "dma_start", out=qkn[:, 1024:1152], in_=ckw[:, :], w=["qkn_ka"], dma=("ccKw", 1))
        trx, trres = next_tr()
        S.ins("pe", "transpose", trx[:, 0:128], qkn[:, 1024:1152], ident[:], r=["qkn_ka", "ident"], w=[trres])
        S.ins("act", "copy", kaT[:, sl, :], trx[:, 0:128], r=[trres], w=[f"kaT{sl}"])
        S.ins("pool", "dma_start", out=va[:, sl, :, 0:64], in_=v3(cvw[:, :], 64), w=[f"va{sl}"], dma=("ccVw", 1))
        S.ins("sp", "dma_start", out=kws[0:64, :], in_=ckw[64:128, :], dma=("kvc", 1))
        S.ins("sp", "dma_start", out=vws[0:64, :], in_=cvw[64:128, :], dma=("kvc", 1))
        S.ins("sp", "dma_start", out=kbs[0:448, :], in_=ckb[64:512, :], dma=("kvc", 1))
        S.ins("sp", "dma_start", out=vbs[0:448, :], in_=cvb[64:512, :], dma=("kvc", 1))
        yield

    def late_setup():
        S.ins("act", "activation", out=esink[:], in_=esink[:], func=AF.Exp, r=["esink"], w=["esink"])
        S.ins("dve", "tensor_tensor", out=v3(tabB[:, 0:1024], 128), in0=v3(kvf[:, 0:1024], 128),
              in1=bc3(cb[:, 0:8], 128), op=ALU.subtract, r=["kvf", "cb"], w=["tabB"])
        S.ins("dve", "tensor_tensor", out=v3(t1[:, 0:1024], 128), in0=v3(t1[:, 0:1024], 128),
              in1=bc3(cb[:, 0:8], 128), op=ALU.subtract, r=["t1", "cb"], w=["t1"])
        S.ins("dve", "tensor_tensor", out=tabB[:, 1024:2048], in0=t1[:, 0:1024], in1=mstage, op=ALU.add,
              r=["t1"] + MST, w=["tabB"])

    def run(gen):
        for _ in gen:
            pass

    halos = [t for t in tiles if t["kind"] == "halo"]
    mains = [t for t in tiles if t["kind"] == "main"]
    samp = [t for t in tiles if t["kind"] == "sample"][0]
    seq = []
    if stage >= 1:
        seq += [("P", t) for t in halos]
    if stage >= 2:
        for t in mains:
            seq += [("P", t), ("A", t), ("Z", t)]
    if stage >= 3:
        seq += [("P", samp), ("C", None), ("A", samp), ("Z", samp)]
    porder = [t for k, t in seq if k == "P"]
    nextload = [0]

    def load_upto(k):
        while nextload[0] < len(porder) and nextload[0] <= k:
            issue_load(porder[nextload[0]])
            nextload[0] += 1

    pidx = 0
    nextload[0] = 2
    late_done = False
    for k, t in seq:
        if not late_done and not (k == "P" and t["kind"] == "halo"):
            late_setup()
            late_done = True
        S.cur = (k, t["li"] if t is not None else -1)
        if k == "P":
            load_upto(pidx + 2)
            pidx += 1
            run(stage_P(t))
        elif k == "A":
            issue_pload(t)
            run(stage_A(t))
        elif k == "Z":
            run(stage_Z(t))
        else:
            run(stage_C())

    if not late_done:
        late_setup()
    dma_keys = sorted({o.dma[0] for s_ in S.streams.values() for o in s_ if o.dma is not None})
    sems = {e: es.enter_context(nc.semaphore(f"s_{e}")) for e in Sched.ENGS}
    dsems = {k: es.enter_context(nc.semaphore(f"d_{k}")) for k in dma_keys}
    if RESCHEDULE:
        S.reschedule()
        print("[sched] simulated time per core: %.1f us" % S.sim_time)
    S.assign(sems, dsems)
    with nc.Block() as block:
        @block.tensor
        def _(e):
            S.emit_stream("pe", e)

        @block.scalar
        def _(e):
            S.emit_stream("act", e)

        @block.vector
        def _(e):
            S.emit_stream("dve", e)

        @block.gpsimd
        def _(e):
            S.emit_stream("pool", e)

        @block.sync
        def _(e):
            S.emit_stream("sp", e)
            for k, tot in S.dma_tot.items():
                e.wait_ge(dsems[k], tot)
    es.close()
    return nc


_CACHE = {}


def _make_in_maps(inp, NT):
    f = lambda a: np.ascontiguousarray(np.asarray(a, dtype=np.float32))
    xp = f(inp["x_prompt"]); xsm = f(inp["x_sample"])
    pp = f(inp["p_prompt"])[0]; psm = f(inp["p_sample"])[0]
    ckw = f(inp["cache_k_win"])[0]; cvw = f(inp["cache_v_win"])[0]
    ckb = f(inp["cache_k_band"])[0]; cvb = f(inp["cache_v_band"])[0]
    TPC = NT * 128
    tabA, idx0, idx1, maskT = _geometry()
    rb = f(inp["rel_bias_band"])[0]
    tb = np.stack([rb[:, idx0], rb[:, idx1]], axis=0)
    tabB = np.ascontiguousarray(tb.transpose(2, 0, 1, 3)).reshape(128, 2048)
    shared = dict(
        w_in=f(inp["w_in"])[0], w_ow=f(inp["w_o_win"])[0], w_ob=f(inp["w_o_band"])[0],
        w_out=f(inp["w_out"])[0], w_pg=f(inp["w_ple_gate"])[0], w_pl=f(inp["w_ple"])[0],
        ident=np.eye(128, dtype=np.float32), tabA=tabA, tabB=tabB, maskT=maskT,
        cbrow=np.ascontiguousarray(rb[:, 256][None, :]),
        gTin=np.ascontiguousarray(f(inp["g_in"])[0].reshape(8, 128).T),
        gTple=np.ascontiguousarray(f(inp["g_ple"])[0].reshape(8, 128).T),
        gcol=np.ascontiguousarray(np.stack([np.tile(f(inp["g_q_win"])[0], 2), np.tile(f(inp["g_k_win"])[0], 2),
                                            np.tile(f(inp["g_q_band"])[0], 2), np.tile(f(inp["g_k_band"])[0], 2)], axis=1)),
        gkwb=np.ascontiguousarray(f(inp["g_k_win"])[0][None, :]),
        gkbb=np.ascontiguousarray(f(inp["g_k_band"])[0][None, :]),
        sinkrow=np.ascontiguousarray(f(inp["sink_win"])[0][None, :]),
    )
    maps = []
    for k in range(8):
        b, j = k // 4, k % 4
        s0 = j * TPC
        xh = np.zeros((512, D), np.float32)
        hm = np.zeros((128, 4), np.float32)
        for i in range(4):
            lo = s0 - 512 + i * 128
            if lo >= 0:
                xh[i * 128:(i + 1) * 128] = xp[b, lo:lo + 128]
            else:
                hm[:, i] = NEG
        xs_ = np.zeros((128, D), np.float32); xs_[:64] = xsm[k]
        ps_ = np.zeros((128, PLE), np.float32); ps_[:64] = psm[k]
        m = dict(shared)
        m.update(xh=xh, xm=np.ascontiguousarray(xp[b, s0:s0 + TPC]), pm=np.ascontiguousarray(pp[b, s0:s0 + TPC]),
                 xs=xs_, psd=ps_, hmask=hm,
                 ckw=np.ascontiguousarray(ckw[k].reshape(128, 128)), cvw=np.ascontiguousarray(cvw[k].reshape(128, 128)),
                 ckb=np.ascontiguousarray(ckb[k].reshape(512, 512)), cvb=np.ascontiguousarray(cvb[k].reshape(512, 512)))
        maps.append(m)
    return maps


def kernel(**inp):
    Sq = inp["x_prompt"].shape[1]
    NT = Sq // 4 // 128
    if NT not in _CACHE:
        _CACHE[NT] = build(NT)
    nc = _CACHE[NT]
    maps = _make_in_maps(inp, NT)
    res = run_bass_kernel_spmd(nc, maps, core_ids=list(range(8)))
    R = res.results
    TPC = NT * 128
    y = np.zeros((2, Sq, D), np.float32)
    for k in range(8):
        y[k // 4, (k % 4) * TPC:(k % 4 + 1) * TPC] = R[k]["y"]
    ysam = np.stack([R[k]["ys"] for k in range(8)], axis=0)
    last = [3, 7]
    kwp = np.stack([R[k]["kwin"].reshape(128, 2, 64) for k in last])[None]
    vwp = np.stack([R[k]["vwin"].reshape(128, 2, 64) for k in last])[None]
    kbp = np.stack([R[k]["kband"].reshape(512, 8, 64) for k in last])[None]
    vbp = np.stack([R[k]["vband"].reshape(512, 8, 64) for k in last])[None]
    kwsam = np.stack([R[k]["kws"].reshape(128, 2, 64) for k in range(8)])[None]
    vwsam = np.stack([R[k]["vws"].reshape(128, 2, 64) for k in range(8)])[None]
    kbsam = np.stack([R[k]["kbs"].reshape(512, 8, 64) for k in range(8)])[None]
    vbsam = np.stack([R[k]["vbs"].reshape(512, 8, 64) for k in range(8)])[None]
    return (y, ysam.astype(np.float32), kwp, vwp, kbp, vbp, kwsam, vwsam, kbsam, vbsam)
```

```python
import numpy as np
from contextlib import ExitStack

import concourse.bass as bass
import concourse.mybir as mybir
from concourse.bass_utils import run_bass_kernel_spmd

F32 = mybir.dt.float32
BF16 = mybir.dt.bfloat16
AF = mybir.ActivationFunctionType
ALU = mybir.AluOpType
AX = mybir.AxisListType

D = 1024
INC = 5376
PLE = 256
EPS = 1e-6
NEG = -30000.0
RA = 2
RB = 5
SAME_ENGINE_SYNC = True
ATTACH_WAIT = True
RESCHEDULE = True
SCHED_WINDOW = 400
P_BOOST = 200
INPROJ_SPLIT = 2


class _Op:
    __slots__ = ("eng", "fn", "deps", "dma", "signal", "val", "sem", "order", "cost", "lat", "users", "ndeps",
                 "ready", "start", "tag", "prio")


class Sched:
    ENGS = ("pe", "act", "dve", "pool", "sp")

    def __init__(self):
        self.streams = {e: [] for e in self.ENGS}
        self.lastw = {}
        self.readers = {}
        self.n = 0
        self.dma_tot = {}

    LIMIT = None

    def op(self, eng, fn, r=(), w=(), dma=None, cost=0.3, lat=0.0):
        if Sched.LIMIT is not None and self.n >= Sched.LIMIT:
            return None
        o = _Op()
        o.cost, o.lat = cost, lat
        o.tag = getattr(self, "cur", None)
        o.prio = self.n - getattr(self, "boost", 0)
        o.eng, o.fn, o.dma, o.signal, o.val, o.sem = eng, fn, dma, False, None, None
        o.order = self.n
        self.n += 1
        w = list(w) + [res for res in r if res.startswith("ps") and res not in w]
        deps = {}
        for res in r:
            d = self.lastw.get(res)
            if d is not None:
                deps[d.order] = d
        for res in w:
            d = self.lastw.get(res)
            if d is not None:
                deps[d.order] = d
            for rd in self.readers.get(res, ()):
                deps[rd.order] = rd
        for res in w:
            self.lastw[res] = o
            self.readers[res] = []
        ws = set(w)
        for res in r:
            if res not in ws:
                self.readers.setdefault(res, []).append(o)
        o.deps = list(deps.values())
        for d in o.deps:
            d.signal = True
        if dma is not None:
            o.signal = True
        self.streams[eng].append(o)
        return o

    @staticmethod
    def _fsize(ap):
        try:
            n = 1
            for d in ap.shape[1:]:
                n *= int(d)
            return n
        except Exception:
            return 256

    def ins(self, eng, method, *args, r=(), w=(), dma=None, lat_override=None, **kw):
        out = kw.get("out", args[0] if args else None)
        n = self._fsize(out) if out is not None else 256
        lat = 0.0
        if dma is not None:
            cost = 0.45 if eng == "sp" else 1.0
            nbytes = n * out.shape[0] * 4
            lat = 2.5 + nbytes / 150e3
            if lat_override is not None:
                lat = lat_override
        elif eng == "pe":
            cost = n / 2000.0 + (0.02 if n < 512 else 0.0)
        elif eng == "act":
            cost = 0.18 + n / 1200.0
        elif eng == "dve":
            cost = 0.15 + n / 950.0
        else:
            cost = (1.0 if kw.get("op", None) == ALU.pow else 0.3 + n / 600.0)
        def fn(e, hook=None):
            inst = getattr(e, method)(*args, **kw)
            if hook is not None:
                hook(inst)
            return inst
        return self.op(eng, fn, r=r, w=w, dma=dma, cost=cost, lat=lat)

    def group(self, eng, calls, r=(), w=()):
        def fn(e, hook=None):
            last = None
            for i_, (m, a, k) in enumerate(calls):
                last = getattr(e, m)(*a, **k)
                if i_ == 0 and hook is not None:
                    hook(last)
            return last
        cost = 0.0
        for m, a, k in calls:
            n = 128 if m == "transpose" else self._fsize(a[2])
            cost += n / 2000.0 + (0.02 if n < 512 else 0.0)
        return self.op(eng, fn, r=r, w=w, cost=cost)

    def reschedule(self):
        allops = sorted((o for s in self.streams.values() for o in s), key=lambda o: o.order)
        for o in allops:
            o.users = []
        for o in allops:
            o.ndeps = len(o.deps)
            o.ready = 0.0
            for d in o.deps:
                d.users.append(o)
        finish = {}
        free = {e: 0.0 for e in self.ENGS}
        cand = {e: [] for e in self.ENGS}
        for o in allops:
            if o.ndeps == 0:
                cand[o.eng].append(o)
        newstreams = {e: [] for e in self.ENGS}
        remaining = len(allops)
        WINDOW = SCHED_WINDOW
        oldest = 0
        scheduled = set()
        while remaining:
            best = None
            while oldest < len(allops) and allops[oldest].order in scheduled:
                oldest += 1
            lim = allops[oldest].order + WINDOW if oldest < len(allops) else 1 << 60
            for e in self.ENGS:
                c = cand[e]
                if not c:
                    continue
                bo, bk = None, None
                for o in c:
                    if o.order > lim:
                        continue
                    k = (max(free[e], o.ready), o.prio)
                    if bk is None or k < bk:
                        bo, bk = o, k
                if bo is not None and (best is None or bk < best[1]):
                    best = (bo, bk)
            o, (t0, _) = best
            e = o.eng
            cand[e].remove(o)
            o.start = t0
            t1 = t0 + o.cost
            free[e] = t1
            fin = t1 + o.lat + 0.2
            newstreams[e].append(o)
            scheduled.add(o.order)
            remaining -= 1
            for u in o.users:
                u.ndeps -= 1
                if fin > u.ready:
                    u.ready = fin
                if u.ndeps == 0:
                    cand[u.eng].append(u)
        self.streams = newstreams
        self.sim_time = max(free.values())

    def assign(self, sems, dma_sems):
        cnt = {e: 0 for e in self.ENGS}
        dcnt = {}
        for e in self.ENGS:
            for o in self.streams[e]:
                if o.dma is not None:
                    key, n = o.dma
                    dcnt[key] = dcnt.get(key, 0) + 16 * n
                    o.sem, o.val = dma_sems[key], dcnt[key]
                elif o.signal:
                    cnt[o.eng] += 1
                    o.sem, o.val = sems[o.eng], cnt[o.eng]
        self.dma_tot = dcnt

    def emit_stream(self, eng, e):
        seen = {}
        for o in self.streams[eng]:
            waits = {}
            for d in o.deps:
                if d.dma is None and d.eng == eng:
                    if eng == "pe" or not SAME_ENGINE_SYNC:
                        continue
                k = id(d.sem)
                if seen.get(k, -1) >= d.val:
                    continue
                if k not in waits or waits[k][1] < d.val:
                    waits[k] = (d.sem, d.val)
            wl = list(waits.values())
            for sem, val in wl:
                seen[id(sem)] = val
            hook = None
            if ATTACH_WAIT and wl and o.dma is None:
                s0, v0 = wl.pop()
                hook = (lambda inst, s0=s0, v0=v0: inst._wait_ge(s0, v0))
            for sem, val in wl:
                e.wait_ge(sem, val)
            res = o.fn(e, hook)
            if o.dma is not None:
                lst = res if isinstance(res, (list, tuple)) else [res]
                assert len(lst) == o.dma[1], (len(lst), o.dma)
                for i_ in lst:
                    i_.then_inc(o.sem, 16)
            elif o.signal:
                last = res[-1] if isinstance(res, (list, tuple)) else res
                last.then_inc(o.sem, 1)


def _geometry():
    j = np.arange(128)[:, None]
    i = np.arange(128)[None, :]
    slopes = (2.0 ** (-np.arange(1, 9))).astype(np.float32)
    m0 = (i >= 64) & (j < 64)
    m1 = (j >= 64) & (i < 64)
    d0 = (128 + i - j).astype(np.float32)
    d1 = np.abs(i - j).astype(np.float32)
    tabA = np.zeros((128, 2, 8, 128), np.float32)
    for h in range(8):
        tabA[:, 0, h, :] = np.where(m0, NEG, -slopes[h] * d0)
        tabA[:, 1, h, :] = np.where(m1, NEG, -slopes[h] * d1)
    idx0 = np.clip(128 + i - j, -128, 128) + 128
    idx1 = np.clip(i - j, -128, 128) + 128
    maskT = np.zeros((128, 8, 128), np.float32)
    maskT[:] = np.where(m1, NEG, 0.0)[:, None, :]
    return tabA.reshape(128, 2048), idx0, idx1, maskT.reshape(128, 1024)


def build(NT, stage=9):
    nc = bass.Bass("TRN2", target_bir_lowering=False)
    S = Sched()
    TPC = NT * 128
    RA_, RB_ = 3, 6
    NWR = 6
    NXB = 4

    def din(name, shape):
        return nc.dram_tensor(name, list(shape), F32, kind="ExternalInput").ap()

    def dout(name, shape):
        return nc.dram_tensor(name, list(shape), F32, kind="ExternalOutput").ap()

    xh = din("xh", [512, D]); xm = din("xm", [TPC, D]); pm = din("pm", [TPC, PLE])
    xs = din("xs", [128, D]); psd = din("psd", [128, PLE])
    ckw = din("ckw", [128, 128]); cvw = din("cvw", [128, 128])
    ckb = din("ckb", [512, 512]); cvb = din("cvb", [512, 512])
    w_in = din("w_in", [D, INC]); w_ow = din("w_ow", [512, D]); w_ob = din("w_ob", [512, D])
    w_out = din("w_out", [D, D]); w_pg = din("w_pg", [D, D]); w_pl = din("w_pl", [PLE, D])
    identd = din("ident", [128, 128]); tabAd = din("tabA", [128, 2048]); tabBd = din("tabB", [128, 2048])
    maskTd = din("maskT", [128, 1024]); cbd = din("cbrow", [1, 8]); hmaskd = din("hmask", [128, 4])
    gTind = din("gTin", [128, 8]); gTpled = din("gTple", [128, 8]); gcold = din("gcol", [128, 4])
    gkwbd = din("gkwb", [1, 64]); gkbbd = din("gkbb", [1, 64]); sinkd = din("sinkrow", [1, 8])
    wsc = nc.dram_tensor("wsc", [2 * D, D], BF16, kind="Internal").ap()

    y = dout("y", [TPC, D]); ys = dout("ys", [64, D])
    kwin = dout("kwin", [128, 128]); vwin = dout("vwin", [128, 128])
    kband = dout("kband", [512, 512]); vband = dout("vband", [512, 512])
    kws = dout("kws", [128, 128]); vws = dout("vws", [128, 128])
    kbs = dout("kbs", [512, 512]); vbs = dout("vbs", [512, 512])

    es = ExitStack()

    def sb(name, shape, dt):
        return es.enter_context(nc.sbuf_tensor(name, list(shape), dt))

    def pst(name, shape, dt):
        return es.enter_context(nc.psum_tensor(name, list(shape), dt))

    W_in = sb("W_in", [128, 8, INC], BF16)
    W_ow = sb("W_ow", [128, 4, D], BF16); W_ob = sb("W_ob", [128, 4, D], BF16)
    W_pl = sb("W_pl", [128, 2, D], BF16)
    wring = sb("wring", [128, NWR, D], BF16)
    ident = sb("identb", [128, 128], BF16)
    tabA = sb("tabAs", [128, 2048], BF16); tabB = sb("tabBs", [128, 2048], BF16)
    gTin = sb("gTins", [128, 8], F32); gTple = sb("gTples", [128, 8], F32); gcol = sb("gcols", [128, 4], F32)
    gkwb = sb("gkwbs", [128, 64], F32); gkbb = sb("gkbbs", [128, 64], F32)
    esink = sb("esink", [128, 8], F32); hmask = sb("hmasks", [128, 4], F32); cb = sb("cbs", [128, 8], F32)
    kaT = sb("kaT", [128, RA_, 128], BF16); kbT = sb("kbT", [128, RB_, 512], BF16)
    va = sb("va", [128, RA_, 2, 65], BF16); vb = sb("vb", [128, RB_, 8, 65], BF16)
    xb = [sb(f"xb{i}", [128, D], F32) for i in range(NXB)]
    pb = [sb(f"pb{i}", [128, PLE], F32) for i in range(2)]
    xn = sb("xn", [128, D], BF16); xnT = sb("xnT", [128, D], BF16)
    ybufs = [sb(f"ybuf{i}", [128, D], BF16) for i in range(2)]
    mbuf = sb("mbuf", [128, D], BF16); actT = sb("actT", [128, D], BF16)
    pbf = sb("pbf", [128, PLE], BF16); pT = sb("pT", [128, PLE], BF16)
    qkn = sb("qkn", [128, 1664], BF16)
    qaT = [sb(f"qaT{i}", [128, 512], BF16) for i in range(2)]
    qbT = [sb(f"qbT{i}", [128, 512], BF16) for i in range(2)]
    zs = [sb(f"zs{i}", [128, D], BF16) for i in range(2)]
    tg = [sb(f"tg{i}", [128, 2048], BF16) for i in range(2)]
    ET = [sb(f"ET{i}", [128, 512], BF16) for i in range(3)]
    sqt = sb("sqt", [128, 512], F32); tzt = sb("tzt", [128, 512], F32)
    kvft = sb("kvft", [128, 1280], F32); t1t = sb("t1t", [128, D], F32); ont = sb("ont", [128, 256], F32)
    st = sb("st", [128, 64], F32)
    stq = sb("stq", [128, 16], F32)
    mhalf = sb("mhalf", [128, 8], F32)

    NMM = 3
    ps_mm = [pst(f"psmm{i}", [128, 512], F32) for i in range(NMM)]
    ps_s = [pst(f"pss{i}", [128, 512], F32) for i in range(2)]
    ps_o1 = pst("pso0", [128, 512], F32)
    ps_tl = [pst(f"pstl{i}", [128, 512], F32) for i in range(2)]

    sq = sqt[:]; tz = tzt[:]; kvf = kvft[:]; t1 = t1t[:]; on = ont[:]

    def bc3(ap2, n):
        k = ap2.shape[1]
        return ap2.unsqueeze(2).to_broadcast([128, k, n])

    def v3(ap2, inner):
        return ap2.rearrange("p (a b) -> p a b", b=inner)

    S.ins("sp", "dma_start", out=xb[0][:], in_=xh[0:128, :], w=["xb0"], dma=("ld0", 1))
    S.ins("sp", "dma_start", out=xb[1][:], in_=xh[128:256, :], w=["xb1"], dma=("ld1", 1))
    S.ins("pool", "dma_start", out=ident[:], in_=identd, w=["ident"], dma=("c_ident", 1))
    for nm, dst, src in (("gTin", gTin, gTind), ("gTple", gTple, gTpled), ("gcol", gcol, gcold),
                         ("hmask", hmask, hmaskd)):
        S.ins("sp", "dma_start", out=dst[:], in_=src, w=[nm], dma=("c_" + nm, 1))
    S.ins("sp", "dma_start", out=gkwb[:], in_=gkwbd.to_broadcast([128, 64]), w=["gkwb"], dma=("c_gkwb", 1))
    S.ins("sp", "dma_start", out=gkbb[:], in_=gkbbd.to_broadcast([128, 64]), w=["gkbb"], dma=("c_gkbb", 1))
    S.ins("sp", "dma_start", out=esink[:], in_=sinkd.to_broadcast([128, 8]), w=["esink"], dma=("c_esink", 1))
    S.ins("sp", "dma_start", out=cb[:], in_=cbd.to_broadcast([128, 8]), w=["cb"], dma=("c_cb", 1))
    S.ins("sp", "dma_start", out=kvf[:, 0:1024], in_=tabBd[:, 0:1024], w=["kvf"], dma=("c_tb0", 1))
    S.ins("sp", "dma_start", out=t1[:, 0:1024], in_=tabBd[:, 1024:2048], w=["t1"], dma=("c_tb1", 1))
    mstage = tg[0][:].bitcast(F32)
    MST = [f"tg0_{i}" for i in range(4)]
    S.ins("sp", "dma_start", out=mstage, in_=maskTd, w=MST, dma=("c_mt", 1))
    w_in3 = w_in.rearrange("(c p) n -> p c n", p=128)
    WRANGES = [(512, 768), (1792, 2816), (0, 512), (768, 1792), (2816, 4096), (4096, INC)]
    cumb = [0.0]

    def wlat(nbytes):
        cumb[0] += nbytes
        return 3.0 + cumb[0] / 250e3
    for ri, (c0_, c1_) in enumerate(WRANGES):
        S.ins("pool", "dma_start", out=W_in[:, :, c0_:c1_], in_=w_in3[:, :, c0_:c1_],
              w=[f"W_in_r{ri}"], dma=(f"w_in_r{ri}", 1), lat_override=wlat(D * (c1_ - c0_) * 4))

    S.ins("pool", "dma_start", out=tabA[:], in_=tabAd, w=["tabA"], dma=("c_tabA", 1))

    def win_res(col0, ncols):
        return [f"W_in_r{ri}" for ri, (a_, b_) in enumerate(WRANGES) if a_ < col0 + ncols and col0 < b_]
    for nm, dst, src, kc in (("W_ow", W_ow, w_ow, 4), ("W_ob", W_ob, w_ob, 4), ("W_pl", W_pl, w_pl, 2)):
        for c in range(kc):
            S.ins("pool", "dma_start", out=dst[:, c, :], in_=src[c * 128:(c + 1) * 128, :],
                  w=[f"{nm}{c}"], dma=(f"w_{nm}{c}", 1), lat_override=wlat(128 * D * 4))
    S.ins("pool", "dma_start", out=wsc[0:D, :], in_=w_out, w=["wsc0"], dma=("w2a", 1), lat_override=wlat(D * D * 4))
    S.ins("pool", "dma_start", out=wsc[D:2 * D, :], in_=w_pg, w=["wsc1"], dma=("w2b", 1), lat_override=wlat(D * D * 4))
    S.ins("dve", "tensor_scalar", out=gTin[:], in0=gTin[:], scalar1=32.0, scalar2=None, op0=ALU.mult,
          r=["gTin"], w=["gTin"])
    S.ins("dve", "tensor_scalar", out=gTple[:], in0=gTple[:], scalar1=32.0, scalar2=None, op0=ALU.mult,
          r=["gTple"], w=["gTple"])
    for col in (1, 3):
        S.ins("dve", "tensor_scalar", out=gcol[:, col:col + 1], in0=gcol[:, col:col + 1], scalar1=8.0,
              scalar2=None, op0=ALU.mult, r=["gcol"], w=["gcol"])
    S.ins("dve", "tensor_scalar", out=gkwb[:], in0=gkwb[:], scalar1=8.0, scalar2=None, op0=ALU.mult,
          r=["gkwb"], w=["gkwb"])
    S.ins("dve", "tensor_scalar", out=gkbb[:], in0=gkbb[:], scalar1=8.0, scalar2=None, op0=ALU.mult,
          r=["gkbb"], w=["gkbb"])
    S.ins("pool", "memset", mhalf[:], -0.5, w=["mhalf"])
    S.ins("pool", "memset", va[:], 1.0, w=[f"va{i}" for i in range(RA_)])
    S.ins("pool", "memset", vb[:], 1.0, w=[f"vb{i}" for i in range(RB_)])

    tiles = []
    for i in range(4):
        tiles.append(dict(kind="halo", vt=i - 4, x=xh[i * 128:(i + 1) * 128, :], p=None))
    for n in range(NT):
        tiles.append(dict(kind="main", vt=n, n=n, x=xm[n * 128:(n + 1) * 128, :], p=pm[n * 128:(n + 1) * 128, :]))
    SB = NT + 8
    tiles.append(dict(kind="sample", vt=SB + 4, x=xs, p=psd))
    for li, t in enumerate(tiles):
        t["li"] = li

    def issue_load(t):
        bi = t["li"] % NXB
        S.ins("sp", "dma_start", out=xb[bi][:], in_=t["x"], w=[f"xb{bi}"], dma=(f"ld{bi}", 1))

    def issue_pload(t):
        pi = t["li"] % 2
        S.ins("sp", "dma_start", out=pb[pi][:], in_=t["p"], w=[f"pb{pi}"], dma=(f"ldp{pi}", 1))

    mmc = [0]

    def next_mm():
        b = mmc[0] % NMM
        mmc[0] += 1
        return ps_mm[b], f"psmm{b}"

    def next_tr():
        bank, res = next_mm()
        return bank[:].bitcast(BF16), res

    def inproj_ct(col0, ncols):
        bank, bres_ = next_mm()
        b = int(bres_[-1])
        for g0 in range(0, 8, INPROJ_SPLIT):
            calls = []
            for c in range(g0, g0 + INPROJ_SPLIT):
                calls.append(("matmul", (bank[:, 0:ncols], xnT[:, c * 128:(c + 1) * 128], W_in[:, c, col0:col0 + ncols]),
                              dict(start=(c == 0), stop=(c == 7))))
            S.group("pe", calls, r=["xnT"] + win_res(col0, ncols), w=[f"psmm{b}"])
        return bank, f"psmm{b}"

    def qknorm(bank, bres, c0, nh, dst, dst_res, sscol, kvf_ap=None, gtab=None, permute=False):
        w = 64 * nh
        S.ins("act", "activation", out=sq[:, 0:w], in_=bank[:, c0:c0 + w], func=AF.Square, r=[bres], w=["sq"])
        S.ins("dve", "tensor_reduce", out=st[:, sscol:sscol + nh], in_=v3(sq[:, 0:w], 64), axis=AX.X, op=ALU.add,
              r=["sq"], w=["st_ss"])
        S.ins("dve", "tensor_scalar", out=st[:, sscol:sscol + nh], in0=st[:, sscol:sscol + nh],
              scalar1=64 * EPS, scalar2=None, op0=ALU.add, r=["st_ss"], w=["st_ss"])
        S.ins("pool", "tensor_tensor", out=st[:, 32 + sscol:32 + sscol + nh], in0=st[:, sscol:sscol + nh],
              in1=mhalf[:, 0:nh], op=ALU.pow, r=["st_ss", "mhalf"], w=["st_rs"])
        rsb = bc3(st[:, 32 + sscol:32 + sscol + nh], 64)
        if permute:
            o_ap = dst.rearrange("p (i kv d) -> p kv i d", kv=2, d=64)
            i_ap = bank[:, c0:c0 + w].rearrange("p (kv i d) -> p kv i d", kv=2, d=64)
            r_ap = st[:, 32 + sscol:32 + sscol + nh].rearrange("p (kv i) -> p kv i", kv=2).unsqueeze(3).to_broadcast(
                [128, 2, 4, 64])
            S.ins("dve", "tensor_tensor", out=o_ap, in0=i_ap, in1=r_ap, op=ALU.mult, r=[bres, "st_rs"], w=[dst_res])
        else:
            S.ins("dve", "tensor_tensor", out=v3(dst, 64), in0=v3(bank[:, c0:c0 + w], 64), in1=rsb, op=ALU.mult,
                  r=[bres, "st_rs"], w=[dst_res])
        if kvf_ap is not None:
            S.ins("dve", "tensor_tensor", out=v3(kvf_ap, 64), in0=v3(bank[:, c0:c0 + w], 64), in1=rsb, op=ALU.mult,
                  r=[bres, "st_rs"], w=["kvf"])
            S.ins("pool", "tensor_tensor", out=v3(kvf_ap, 64), in0=v3(kvf_ap, 64),
                  in1=gtab.unsqueeze(1).to_broadcast([128, nh, 64]), op=ALU.mult, r=["kvf"], w=["kvf"])

    def stage_P(t):
        kind = t["kind"]
        vt = t["vt"]
        X = xb[t["li"] % NXB]
        xres = f"xb{t['li'] % NXB}"
        par = t["li"] % 2
        kvout = (kind == "sample") or (kind == "main" and t["n"] >= NT - 4)
        sa, sbn = vt % RA_, vt % RB_
        S.boost = P_BOOST
        S.ins("act", "activation", out=xn[:], in_=X[:], func=AF.Square, accum_out=st[:, 60:61], r=[xres],
              w=["xn", "st_x"])
        S.ins("dve", "tensor_scalar", out=st[:, 60:61], in0=st[:, 60:61], scalar1=1024 * EPS, scalar2=None,
              op0=ALU.add, r=["st_x"], w=["st_x"])
        S.ins("pool", "tensor_tensor", out=st[:, 61:62], in0=st[:, 60:61], in1=mhalf[:, 0:1], op=ALU.pow,
              r=["st_x", "mhalf"], w=["st_rx"])
        S.ins("act", "mul", xn[:], X[:], st[:, 61:62], r=[xres, "st_rx"], w=["xn"])
        trx, trres = next_tr()
        S.group("pe", [("transpose", (trx[:, c * 128:(c + 1) * 128], xn[:, c * 128:(c + 1) * 128], ident[:]), {})
                       for c in range(8)], r=["xn", "ident"], w=[trres])
        S.ins("dve", "tensor_tensor", out=v3(xnT[:], 128), in0=v3(trx[:, :], 128), in1=bc3(gTin[:, 0:8], 128),
              op=ALU.mult, r=[trres, "gTin"], w=["xnT"])
        S.boost = 0
        yield
        if kind != "halo":
            bank, br = inproj_ct(0, 512)
            qknorm(bank, br, 0, 8, qkn[:, 0:512], "qkn_qa", 0, permute=True)
            yield
        bank, br = inproj_ct(512, 256)
        qknorm(bank, br, 0, 2, qkn[:, 1024:1152], "qkn_ka", 8,
               kvf_ap=(kvf[:, 0:128] if kvout else None), gtab=gkwb[:])
        S.ins("act", "copy", va[:, sa, :, 0:64], v3(bank[:, 128:256], 64), r=[br], w=[f"va{sa}"])
        if kvout:
            S.ins("dve", "tensor_copy", out=kvf[:, 128:256], in_=bank[:, 128:256], r=[br], w=["kvf"])
        yield
        if kind != "halo":
            bank, br = inproj_ct(768, 512)
            S.ins("act", "activation", out=tz, in_=bank[:], func=AF.Tanh, scale=0.5, r=[br], w=["tz"])
            S.ins("dve", "scalar_tensor_tensor", out=zs[par][:, 0:512], in0=tz, scalar=1.0, in1=bank[:], op0=ALU.add,
                  op1=ALU.mult, r=["tz", br], w=[f"zs{par}a"])
            yield
            bank, br = inproj_ct(1280, 512)
            qknorm(bank, br, 0, 8, qkn[:, 512:1024], "qkn_qb", 10)
            yield
        bank, br = inproj_ct(1792, 512)
        qknorm(bank, br, 0, 8, qkn[:, 1152:1664], "qkn_kb", 18,
               kvf_ap=(kvf[:, 256:768] if kvout else None), gtab=gkbb[:])
        yield
        bank, br = inproj_ct(2304, 512)
        S.ins("act", "copy", vb[:, sbn, :, 0:64], v3(bank[:], 64), r=[br], w=[f"vb{sbn}"])
        if kvout:
            S.ins("dve", "tensor_copy", out=kvf[:, 768:1280], in_=bank[:], r=[br], w=["kvf"])
        yield
        if kind != "halo":
            bank, br = inproj_ct(2816, 512)
            S.ins("act", "activation", out=tz, in_=bank[:], func=AF.Tanh, scale=0.5, r=[br], w=["tz"])
            S.ins("dve", "scalar_tensor_tensor", out=zs[par][:, 512:1024], in0=tz, scalar=1.0, in1=bank[:],
                  op0=ALU.add, op1=ALU.mult, r=["tz", br], w=[f"zs{par}b"])
            yield
            for gi in range(4):
                bank, br = inproj_ct(3328 + gi * 512, 512)
                S.ins("act", "activation", out=tg[par][:, gi * 512:(gi + 1) * 512], in_=bank[:], func=AF.Tanh,
                      scale=0.5, r=[br], w=[f"tg{par}_{gi}"])
                yield
        if kvout:
            if kind == "main":
                r0 = (t["n"] - (NT - 4)) * 128
                S.ins("sp", "dma_start", out=kband[r0:r0 + 128, :], in_=kvf[:, 256:768], r=["kvf"], dma=("kv", 1))
                S.ins("sp", "dma_start", out=vband[r0:r0 + 128, :], in_=kvf[:, 768:1280], r=["kvf"], dma=("kv", 1))
                if t["n"] == NT - 1:
                    S.ins("sp", "dma_start", out=kwin[:, :], in_=kvf[:, 0:128], r=["kvf"], dma=("kv", 1))
                    S.ins("sp", "dma_start", out=vwin[:, :], in_=kvf[:, 128:256], r=["kvf"], dma=("kv", 1))
            else:
                S.ins("sp", "dma_start", out=kbs[448:512, :], in_=kvf[0:64, 256:768], r=["kvf"], dma=("kv", 1))
                S.ins("sp", "dma_start", out=vbs[448:512, :], in_=kvf[0:64, 768:1280], r=["kvf"], dma=("kv", 1))
                S.ins("sp", "dma_start", out=kws[64:128, :], in_=kvf[0:64, 0:128], r=["kvf"], dma=("kv", 1))
                S.ins("sp", "dma_start", out=vws[64:128, :], in_=kvf[0:64, 128:256], r=["kvf"], dma=("kv", 1))
        if kind != "halo":
            trx, trres = next_tr()
            S.group("pe", [("transpose", (trx[:, c * 128:(c + 1) * 128], qkn[:, c * 128:(c + 1) * 128], ident[:]), {})
                           for c in range(8)], r=["qkn_qa", "qkn_qb", "ident"], w=[trres])
            S.ins("act", "mul", qaT[par][:], trx[:, 0:512], gcol[:, 0:1], r=[trres, "gcol"], w=[f"qaT{par}"])
            S.ins("act", "mul", qbT[par][:], trx[:, 512:1024], gcol[:, 2:3], r=[trres, "gcol"], w=[f"qbT{par}"])
            yield
        trx, trres = next_tr()
        S.group("pe", [("transpose", (trx[:, c * 128:(c + 1) * 128], qkn[:, 1024 + c * 128:1152 + c * 128], ident[:]), {})
                       for c in range(5)], r=["qkn_ka", "qkn_kb", "ident"], w=[trres])
        S.ins("act", "mul", kaT[:, sa, :], trx[:, 0:128], gcol[:, 1:2], r=[trres, "gcol"], w=[f"kaT{sa}"])
        S.ins("act", "mul", kbT[:, sbn, :], trx[:, 128:640], gcol[:, 3:4], r=[trres, "gcol"], w=[f"kbT{sbn}"])
        yield

    wrc = [0]

    def stage_A(t):
        kind = t["kind"]
        vt = t["vt"]
        par = t["li"] % 2
        ybuf = ybufs[par]
        yres = f"ybuf{par}"
        uc = [0]
        obank = ps_o1
        ores = "pso0"
        for mixer in ("A", "B"):
            kts = [vt - 1, vt] if mixer == "A" else [vt - 4, vt - 3, vt - 2, vt - 1, vt]
            for hh in range(2):
                for n_, kt in enumerate(kts):
                    u = uc[0]
                    uc[0] += 1
                    sbk = ps_s[u % 2]
                    sres = f"pss{u % 2}"
                    e_i = u % 3
                    E = ET[e_i]
                    eres = f"ET{e_i}"
                    rel = kt - vt
                    tab = None
                    if mixer == "A":
                        sl = kt % RA_
                        S.ins("pe", "matmul", sbk[:, 0:512], kaT[hh * 64:(hh + 1) * 64, sl, :],
                              qaT[par][hh * 64:(hh + 1) * 64, :], start=True, stop=True,
                              r=[f"kaT{sl}", f"qaT{par}"], w=[sres])
                        tab = v3(tabA[:, (rel + 1) * 1024 + hh * 512:(rel + 1) * 1024 + (hh + 1) * 512], 128)
                        tres = "tabA"
                    else:
                        sl = kt % RB_
                        calls = []
                        for hq in range(4):
                            pr, half = hq, hh
                            calls.append(("matmul", (sbk[:, hq * 128:(hq + 1) * 128],
                                                     kbT[half * 64:(half + 1) * 64, sl, pr * 128:(pr + 1) * 128],
                                                     qbT[par][half * 64:(half + 1) * 64, pr * 128:(pr + 1) * 128]),
                                          dict(start=True, stop=True)))
                        S.group("pe", calls, r=[f"kbT{sl}", f"qbT{par}"], w=[sres])
                        if rel >= -1:
                            tab = tabB[:, (rel + 1) * 1024:(rel + 2) * 1024].rearrange(
                                "p (hq two i) -> p two hq i", two=2, i=128)[:, hh]
                            tres = "tabB"
                    if tab is not None:
                        S.ins("dve", "tensor_tensor", out=v3(sbk[:], 128), in0=v3(sbk[:], 128), in1=tab, op=ALU.add,
                              r=[sres, tres], w=[sres])
                    if kind == "main" and kt < 0:
                        S.ins("act", "activation", out=E[:], in_=sbk[:], func=AF.Exp, bias=hmask[:, kt + 4:kt + 5],
                              r=[sres, "hmask"], w=[eres])
                    else:
                        S.ins("act", "activation", out=E[:], in_=sbk[:], func=AF.Exp, r=[sres], w=[eres])
                    if mixer == "B" and rel == -4:
                        S.ins("pool", "memset", v3(E[0:64, :], 128)[:, :, 64:128], 0.0, r=[eres], w=[eres])
                    calls = []
                    for hq in range(4):
                        if mixer == "A":
                            rhs = va[:, sl, hh, 0:65]
                        else:
                            rhs = vb[:, sl, 2 * hq + hh, 0:65]
                        calls.append(("matmul", (obank[:, hq * 65:(hq + 1) * 65], E[:, hq * 128:(hq + 1) * 128], rhs),
                                      dict(start=(n_ == 0 and hq == 0), stop=(n_ == len(kts) - 1 and hq == 3),
                                           skip_group_check=True)))
                    S.group("pe", calls, r=[eres, (f"va{sl}" if mixer == "A" else f"vb{sl}")], w=[ores])
                    yield
                o3 = obank[:, 0:260].rearrange("p (h e) -> p h e", e=65)
                if mixer == "A":
                    S.ins("dve", "tensor_tensor", out=stq[:, 4:8], in0=o3[:, :, 64], in1=esink[:, 4 * hh:4 * hh + 4],
                          op=ALU.add, r=[ores, "esink"], w=["st_den"])
                    S.ins("dve", "reciprocal", out=stq[:, 0:4], in_=stq[:, 4:8], r=["st_den"], w=["st_rd"])
                else:
                    S.ins("dve", "reciprocal", out=stq[:, 0:4], in_=o3[:, :, 64], r=[ores], w=["st_rd"])
                S.ins("dve", "tensor_tensor", out=v3(on, 64), in0=o3[:, :, 0:64], in1=bc3(stq[:, 0:4], 64), op=ALU.mult,
                      r=[ores, "st_rd"], w=["on"])
                if mixer == "A":
                    yc = hh * 256
                    y_ap, z_ap = v3(ybuf[:, yc:yc + 256], 64), v3(zs[par][:, yc:yc + 256], 64)
                else:
                    y_ap = ybuf[:, 512:1024].rearrange("p (hq two d) -> p two hq d", two=2, d=64)[:, hh]
                    z_ap = zs[par][:, 512:1024].rearrange("p (hq two d) -> p two hq d", two=2, d=64)[:, hh]
                S.ins("pool", "tensor_tensor", out=y_ap, in0=v3(on, 64), in1=z_ap, op=ALU.mult,
                      r=["on", f"zs{par}a" if mixer == "A" else f"zs{par}b"], w=[yres])
                yield
        pi = t["li"] % 2
        S.ins("pool", "tensor_copy", out=pbf[:], in_=pb[pi][:], r=[f"pb{pi}"], w=["pbf"])
        trp, trpres = ps_o1[:].bitcast(BF16), "pso0"
        S.group("pe", [("transpose", (trp[:, c * 128:(c + 1) * 128], pbf[:, c * 128:(c + 1) * 128], ident[:]), {})
                       for c in range(2)], r=["pbf", "ident"], w=[trpres])
        S.ins("act", "copy", pT[:], trp[:, 0:256], r=[trpres], w=["pT"])
        yield

    def stage_Z(t):
        kind = t["kind"]
        X = xb[t["li"] % NXB]
        xres = f"xb{t['li'] % NXB}"
        par = t["li"] % 2
        pi = t["li"] % 2
        ybuf = ybufs[par]
        yres = f"ybuf{par}"
        TL = ["pstl0", "pstl1"]
        trz = ps_tl[0][:].bitcast(BF16)
        wslots = []
        for c in range(16):
            wslots.append(wrc[0] % NWR)
            wrc[0] += 1

        def wload(c):
            S.ins("sp", "dma_start", out=wring[:, wslots[c], :], in_=wsc[c * 128:(c + 1) * 128, :],
                  r=["wsc0" if c < 8 else "wsc1"], w=[f"wr{wslots[c]}"], dma=(f"wr{wslots[c]}", 1))
        for c in range(NWR):
            wload(c)
        wnext = [NWR]

        trzs = [ps_tl[0][:].bitcast(BF16), ps_tl[1][:].bitcast(BF16)]

        def transposes8(src, sres_, banks=None):
            sres_l = sres_ if isinstance(sres_, (list, tuple)) else [sres_, sres_]
            if banks is None:
                banks = [(trzs[0], TL[0]), (trzs[1], TL[1])]
            for hf in range(2):
                tb, tres_ = banks[hf]
                S.group("pe", [("transpose", (tb[:, c * 128:(c + 1) * 128],
                                              src[:, (4 * hf + c) * 128:(4 * hf + c + 1) * 128], ident[:]), {})
                               for c in range(4)], r=[sres_l[hf], "ident"], w=[tres_])
            return banks

        def evac_plain(banks):
            S.ins("act", "copy", actT[:, 0:512], banks[0][0][:, 0:512], r=[banks[0][1]], w=["actT_a"])
            S.ins("dve", "tensor_copy", out=actT[:, 512:1024], in_=banks[1][0][:, 0:512], r=[banks[1][1]], w=["actT_b"])

        def dense(Wt, wres, k0, kc, src, sres_):
            for nb in range(2):
                calls = []
                for c in range(kc):
                    calls.append(("matmul", (ps_tl[nb][:], src[:, (k0 + c) * 128:(k0 + c + 1) * 128],
                                             Wt[:, c, nb * 512:(nb + 1) * 512]),
                                  dict(start=(c == 0), stop=(c == kc - 1))))
                S.group("pe", calls, r=[sres_] + [f"{wres}{c}" for c in range(kc)], w=[TL[nb]])

        def dense_stream(c0):
            for c in range(8):
                slot = wslots[c0 + c]
                calls = []
                for nb in range(2):
                    calls.append(("matmul", (ps_tl[nb][:], actT[:, c * 128:(c + 1) * 128],
                                             wring[:, slot, nb * 512:(nb + 1) * 512]),
                                  dict(start=(c == 0), stop=(c == 7))))
                S.group("pe", calls, r=["actT_a" if c < 4 else "actT_b", f"wr{slot}"], w=[TL[0], TL[1]])
                if wnext[0] < 16:
                    wload(wnext[0])
                    wnext[0] += 1

        evac_plain(transposes8(ybuf, yres))
        yield
        dense(W_ow, "W_ow", 0, 4, actT, "actT_a")
        for nb in range(2):
            S.ins("dve", "scalar_tensor_tensor", out=t1[:, nb * 512:(nb + 1) * 512],
                  in0=tg[par][:, nb * 512:(nb + 1) * 512], scalar=1.0, in1=ps_tl[nb][:], op0=ALU.add, op1=ALU.mult,
                  r=[f"tg{par}_{nb}", TL[nb]], w=["t1"])
        yield
        dense(W_ob, "W_ob", 4, 4, actT, "actT_b")
        for nb in range(2):
            S.ins("dve", "scalar_tensor_tensor", out=ps_tl[nb][:],
                  in0=tg[par][:, 1024 + nb * 512:1024 + (nb + 1) * 512],
                  scalar=1.0, in1=ps_tl[nb][:], op0=ALU.add, op1=ALU.mult, r=[f"tg{par}_{2 + nb}", TL[nb]],
                  w=[TL[nb]])
            S.ins("dve", "tensor_tensor", out=mbuf[:, nb * 512:(nb + 1) * 512], in0=ps_tl[nb][:],
                  in1=t1[:, nb * 512:(nb + 1) * 512], op=ALU.add, r=[TL[nb], "t1"], w=[f"mbuf_{nb}"])
        evac_plain(transposes8(mbuf, ["mbuf_0", "mbuf_1"]))
        yield
        dense_stream(0)
        yield
        for nb in range(2):
            S.ins("dve", "scalar_tensor_tensor", out=X[:, nb * 512:(nb + 1) * 512], in0=ps_tl[nb][:], scalar=0.25,
                  in1=X[:, nb * 512:(nb + 1) * 512], op0=ALU.mult, op1=ALU.add, r=[TL[nb], xres], w=[xres])
        S.ins("act", "copy", mbuf[:, 0:512], X[:, 0:512], r=[xres], w=["mbuf_0"])
        S.ins("dve", "tensor_copy", out=mbuf[:, 512:1024], in_=X[:, 512:1024], r=[xres], w=["mbuf_1"])
        S.ins("act", "activation", out=ybuf[:], in_=X[:], func=AF.Square, accum_out=stq[:, 8:9], r=[xres],
              w=[yres, "st_h"])
        S.ins("dve", "tensor_scalar", out=stq[:, 8:9], in0=stq[:, 8:9], scalar1=1024 * EPS, scalar2=4.0,
              op0=ALU.add, op1=ALU.mult, r=["st_h"], w=["st_h"])
        S.ins("pool", "tensor_tensor", out=stq[:, 9:10], in0=stq[:, 8:9], in1=mhalf[:, 0:1], op=ALU.pow,
              r=["st_h", "mhalf"], w=["st_rh"])
        bk = transposes8(mbuf, ["mbuf_0", "mbuf_1"])
        for hf in range(2):
            S.ins("dve", "tensor_tensor", out=v3(actT[:, hf * 512:(hf + 1) * 512], 128), in0=v3(bk[hf][0][:, 0:512], 128),
                  in1=bc3(gTple[:, 4 * hf:4 * hf + 4], 128), op=ALU.mult, r=[bk[hf][1], "gTple"],
                  w=["actT_a" if hf == 0 else "actT_b"])
        yield
        dense_stream(8)
        yield
        for nb in range(2):
            S.ins("act", "activation", out=t1[:, nb * 512:(nb + 1) * 512], in_=ps_tl[nb][:], func=AF.Tanh,
                  scale=stq[:, 9:10], r=[TL[nb], "st_rh"], w=["t1"])
        dense(W_pl, "W_pl", 0, 2, pT, "pT")
        yield
        for nb in range(2):
            S.ins("dve", "scalar_tensor_tensor", out=ps_tl[nb][:], in0=t1[:, nb * 512:(nb + 1) * 512],
                  scalar=1.0, in1=ps_tl[nb][:], op0=ALU.add, op1=ALU.mult, r=["t1", TL[nb]], w=[TL[nb]])
            S.ins("dve", "scalar_tensor_tensor", out=X[:, nb * 512:(nb + 1) * 512], in0=ps_tl[nb][:], scalar=0.5,
                  in1=X[:, nb * 512:(nb + 1) * 512], op0=ALU.mult, op1=ALU.add, r=[TL[nb], xres], w=[xres])
        if kind == "main":
            n = t["n"]
            S.ins("sp", "dma_start", out=y[n * 128:(n + 1) * 128, :], in_=X[:], r=[xres], dma=("st_" + xres, 1))
        else:
            S.ins("sp", "dma_start", out=ys[:, :], in_=X[0:64, :], r=[xres], dma=("st_" + xres, 1))
        yield

    def stage_C():
        regions = [(0, "qkn_qa"), (512, "qkn_qb"), (1152, "qkn_kb"), (0, "qkn_qa")]
        for i in range(4):
            sl = (SB + i) % RB_
            off, rres = regions[i]
            S.ins("pool", "dma_start", out=qkn[:, off:off + 512], in_=ckb[i * 128:(i + 1) * 128, :], w=[rres],
                  dma=(f"ccK{i % 3}", 1))
            trx, trres = next_tr()
            S.group("pe", [("transpose", (trx[:, c * 128:(c + 1) * 128], qkn[:, off + c * 128:off + (c + 1) * 128], ident[:]), {})
                           for c in range(4)], r=[rres, "ident"], w=[trres])
            S.ins("act", "copy", kbT[:, sl, :], trx[:, 0:512], r=[trres], w=[f"kbT{sl}"])
            S.ins("pool", "dma_start", out=vb[:, sl, :, 0:64], in_=v3(cvb[i * 128:(i + 1) * 128, :], 64),
                  w=[f"vb{sl}"], dma=(f"ccV{i}", 1))
            yield
        sl = (SB + 3) % RA_
        S.ins("pool", "dma_start", out=qkn[:, 1024:1152], in_=ckw[:, :], w=["qkn_ka"], dma=("ccKw", 1))
        trx, trres = next_tr()
        S.ins("pe", "transpose", trx[:, 0:128], qkn[:, 1024:1152], ident[:], r=["qkn_ka", "ident"], w=[trres])
        S.ins("act", "copy", kaT[:, sl, :], trx[:, 0:128], r=[trres], w=[f"kaT{sl}"])
        S.ins("pool", "dma_start", out=va[:, sl, :, 0:64], in_=v3(cvw[:, :], 64), w=[f"va{sl}"], dma=("ccVw", 1))
        S.ins("sp", "dma_start", out=kws[0:64, :], in_=ckw[64:128, :], dma=("kvc", 1))
        S.ins("sp", "dma_start", out=vws[0:64, :], in_=cvw[64:128, :], dma=("kvc", 1))
        S.ins("sp", "dma_start", out=kbs[0:448, :], in_=ckb[64:512, :], dma=("kvc", 1))
        S.ins("sp", "dma_start", out=vbs[0:448, :], in_=cvb[64:512, :], dma=("kvc", 1))
        yield

    def late_setup():
        S.ins("act", "activation", out=esink[:], in_=esink[:], func=AF.Exp, r=["esink"], w=["esink"])
        S.ins("dve", "tensor_tensor", out=v3(tabB[:, 0:1024], 128), in0=v3(kvf[:, 0:1024], 128),
              in1=bc3(cb[:, 0:8], 128), op=ALU.subtract, r=["kvf", "cb"], w=["tabB"])
        S.ins("dve", "tensor_tensor", out=v3(t1[:, 0:1024], 128), in0=v3(t1[:, 0:1024], 128),
              in1=bc3(cb[:, 0:8], 128), op=ALU.subtract, r=["t1", "cb"], w=["t1"])
        S.ins("dve", "tensor_tensor", out=tabB[:, 1024:2048], in0=t1[:, 0:1024], in1=mstage, op=ALU.add,
              r=["t1"] + MST, w=["tabB"])

    def run(gen):
        for _ in gen:
            pass

    halos = [t for t in tiles if t["kind"] == "halo"]
    mains = [t for t in tiles if t["kind"] == "main"]
    samp = [t for t in tiles if t["kind"] == "sample"][0]
    seq = []
    if stage >= 1:
        seq += [("P", t) for t in halos]
    if stage >= 2:
        for t in mains:
            seq += [("P", t), ("A", t), ("Z", t)]
    if stage >= 3:
        seq += [("P", samp), ("C", None), ("A", samp), ("Z", samp)]
    porder = [t for k, t in seq if k == "P"]
    nextload = [0]

    def load_upto(k):
        while nextload[0] < len(porder) and nextload[0] <= k:
            issue_load(porder[nextload[0]])
            nextload[0] += 1

    pidx = 0
    nextload[0] = 2
    late_done = False
    for k, t in seq:
        if not late_done and not (k == "P" and t["kind"] == "halo"):
            late_setup()
            late_done = True
        S.cur = (k, t["li"] if t is not None else -1)
        if k == "P":
            load_upto(pidx + 2)
            pidx += 1
            run(stage_P(t))
        elif k == "A":
            issue_pload(t)
            run(stage_A(t))
        elif k == "Z":
            run(stage_Z(t))
        else:
            run(stage_C())

    if not late_done:
        late_setup()
    dma_keys = sorted({o.dma[0] for s_ in S.streams.values() for o in s_ if o.dma is not None})
    sems = {e: es.enter_context(nc.semaphore(f"s_{e}")) for e in Sched.ENGS}
    dsems = {k: es.enter_context(nc.semaphore(f"d_{k}")) for k in dma_keys}
    if RESCHEDULE:
        S.reschedule()
        print("[sched] simulated time per core: %.1f us" % S.sim_time)
    S.assign(sems, dsems)
    with nc.Block() as block:
        @block.tensor
        def _(e):
            S.emit_stream("pe", e)

        @block.scalar
        def _(e):
            S.emit_stream("act", e)

        @block.vector
        def _(e):
            S.emit_stream("dve", e)

        @block.gpsimd
        def _(e):
            S.emit_stream("pool", e)

        @block.sync
        def _(e):
            S.emit_stream("sp", e)
            for k, tot in S.dma_tot.items():
                e.wait_ge(dsems[k], tot)
    es.close()
    return nc


_CACHE = {}


def _make_in_maps(inp, NT):
    f = lambda a: np.ascontiguousarray(np.asarray(a, dtype=np.float32))
    xp = f(inp["x_prompt"]); xsm = f(inp["x_sample"])
    pp = f(inp["p_prompt"])[0]; psm = f(inp["p_sample"])[0]
    ckw = f(inp["cache_k_win"])[0]; cvw = f(inp["cache_v_win"])[0]
    ckb = f(inp["cache_k_band"])[0]; cvb = f(inp["cache_v_band"])[0]
    TPC = NT * 128
    tabA, idx0, idx1, maskT = _geometry()
    rb = f(inp["rel_bias_band"])[0]
    tb = np.stack([rb[:, idx0], rb[:, idx1]], axis=0)
    tabB = np.ascontiguousarray(tb.transpose(2, 0, 1, 3)).reshape(128, 2048)
    shared = dict(
        w_in=f(inp["w_in"])[0], w_ow=f(inp["w_o_win"])[0], w_ob=f(inp["w_o_band"])[0],
        w_out=f(inp["w_out"])[0], w_pg=f(inp["w_ple_gate"])[0], w_pl=f(inp["w_ple"])[0],
        ident=np.eye(128, dtype=np.float32), tabA=tabA, tabB=tabB, maskT=maskT,
        cbrow=np.ascontiguousarray(rb[:, 256][None, :]),
        gTin=np.ascontiguousarray(f(inp["g_in"])[0].reshape(8, 128).T),
        gTple=np.ascontiguousarray(f(inp["g_ple"])[0].reshape(8, 128).T),
        gcol=np.ascontiguousarray(np.stack([np.tile(f(inp["g_q_win"])[0], 2), np.tile(f(inp["g_k_win"])[0], 2),
                                            np.tile(f(inp["g_q_band"])[0], 2), np.tile(f(inp["g_k_band"])[0], 2)], axis=1)),
        gkwb=np.ascontiguousarray(f(inp["g_k_win"])[0][None, :]),
        gkbb=np.ascontiguousarray(f(inp["g_k_band"])[0][None, :]),
        sinkrow=np.ascontiguousarray(f(inp["sink_win"])[0][None, :]),
    )
    maps = []
    for k in range(8):
        b, j = k // 4, k % 4
        s0 = j * TPC
        xh = np.zeros((512, D), np.float32)
        hm = np.zeros((128, 4), np.float32)
        for i in range(4):
            lo = s0 - 512 + i * 128
            if lo >= 0:
                xh[i * 128:(i + 1) * 128] = xp[b, lo:lo + 128]
            else:
                hm[:, i] = NEG
        xs_ = np.zeros((128, D), np.float32); xs_[:64] = xsm[k]
        ps_ = np.zeros((128, PLE), np.float32); ps_[:64] = psm[k]
        m = dict(shared)
        m.update(xh=xh, xm=np.ascontiguousarray(xp[b, s0:s0 + TPC]), pm=np.ascontiguousarray(pp[b, s0:s0 + TPC]),
                 xs=xs_, psd=ps_, hmask=hm,
                 ckw=np.ascontiguousarray(ckw[k].reshape(128, 128)), cvw=np.ascontiguousarray(cvw[k].reshape(128, 128)),
                 ckb=np.ascontiguousarray(ckb[k].reshape(512, 512)), cvb=np.ascontiguousarray(cvb[k].reshape(512, 512)))
        maps.append(m)
    return maps


def kernel(**inp):
    Sq = inp["x_prompt"].shape[1]
    NT = Sq // 4 // 128
    if NT not in _CACHE:
        _CACHE[NT] = build(NT)
    nc = _CACHE[NT]
    maps = _make_in_maps(inp, NT)
    res = run_bass_kernel_spmd(nc, maps, core_ids=list(range(8)))
    R = res.results
    TPC = NT * 128
    y = np.zeros((2, Sq, D), np.float32)
    for k in range(8):
        y[k // 4, (k % 4) * TPC:(k % 4 + 1) * TPC] = R[k]["y"]
    ysam = np.stack([R[k]["ys"] for k in range(8)], axis=0)
    last = [3, 7]
    kwp = np.stack([R[k]["kwin"].reshape(128, 2, 64) for k in last])[None]
    vwp = np.stack([R[k]["vwin"].reshape(128, 2, 64) for k in last])[None]
    kbp = np.stack([R[k]["kband"].reshape(512, 8, 64) for k in last])[None]
    vbp = np.stack([R[k]["vband"].reshape(512, 8, 64) for k in last])[None]
    kwsam = np.stack([R[k]["kws"].reshape(128, 2, 64) for k in range(8)])[None]
    vwsam = np.stack([R[k]["vws"].reshape(128, 2, 64) for k in range(8)])[None]
    kbsam = np.stack([R[k]["kbs"].reshape(512, 8, 64) for k in range(8)])[None]
    vbsam = np.stack([R[k]["vbs"].reshape(512, 8, 64) for k in range(8)])[None]
    return (y, ysam.astype(np.float32), kwp, vwp, kbp, vbp, kwsam, vwsam, kbsam, vbsam)
```

```python
import numpy as np
from contextlib import ExitStack

import concourse.bass as bass
import concourse.mybir as mybir
from concourse.bass_utils import run_bass_kernel_spmd

F32 = mybir.dt.float32
BF16 = mybir.dt.bfloat16
AF = mybir.ActivationFunctionType
ALU = mybir.AluOpType
AX = mybir.AxisListType

D = 1024
INC = 5376
PLE = 256
EPS = 1e-6
NEG = -30000.0
RA = 2
RB = 5
SAME_ENGINE_SYNC = True
ATTACH_WAIT = True
RESCHEDULE = True
SCHED_WINDOW = 400
P_BOOST = 200
INPROJ_SPLIT = 2


class _Op:
    __slots__ = ("eng", "fn", "deps", "dma", "signal", "val", "sem", "order", "cost", "lat", "users", "ndeps",
                 "ready", "start", "tag", "prio")


class Sched:
    ENGS = ("pe", "act", "dve", "pool", "sp")

    def __init__(self):
        self.streams = {e: [] for e in self.ENGS}
        self.lastw = {}
        self.readers = {}
        self.n = 0
        self.dma_tot = {}

    LIMIT = None

    def op(self, eng, fn, r=(), w=(), dma=None, cost=0.3, lat=0.0):
        if Sched.LIMIT is not None and self.n >= Sched.LIMIT:
            return None
        o = _Op()
        o.cost, o.lat = cost, lat
        o.tag = getattr(self, "cur", None)
        o.prio = self.n - getattr(self, "boost", 0)
        o.eng, o.fn, o.dma, o.signal, o.val, o.sem = eng, fn, dma, False, None, None
        o.order = self.n
        self.n += 1
        w = list(w) + [res for res in r if res.startswith("ps") and res not in w]
        deps = {}
        for res in r:
            d = self.lastw.get(res)
            if d is not None:
                deps[d.order] = d
        for res in w:
            d = self.lastw.get(res)
            if d is not None:
                deps[d.order] = d
            for rd in self.readers.get(res, ()):
                deps[rd.order] = rd
        for res in w:
            self.lastw[res] = o
            self.readers[res] = []
        ws = set(w)
        for res in r:
            if res not in ws:
                self.readers.setdefault(res, []).append(o)
        o.deps = list(deps.values())
        for d in o.deps:
            d.signal = True
        if dma is not None:
            o.signal = True
        self.streams[eng].append(o)
        return o

    @staticmethod
    def _fsize(ap):
        try:
            n = 1
            for d in ap.shape[1:]:
                n *= int(d)
            return n
        except Exception:
            return 256

    def ins(self, eng, method, *args, r=(), w=(), dma=None, lat_override=None, **kw):
        out = kw.get("out", args[0] if args else None)
        n = self._fsize(out) if out is not None else 256
        lat = 0.0
        if dma is not None:
            cost = 0.45 if eng == "sp" else 1.0
            nbytes = n * out.shape[0] * 4
            lat = 2.5 + nbytes / 150e3
            if lat_override is not None:
                lat = lat_override
        elif eng == "pe":
            cost = n / 2000.0 + (0.02 if n < 512 else 0.0)
        elif eng == "act":
            cost = 0.18 + n / 1200.0
        elif eng == "dve":
            cost = 0.15 + n / 950.0
        else:
            cost = (1.0 if kw.get("op", None) == ALU.pow else 0.3 + n / 600.0)
        def fn(e, hook=None):
            inst = getattr(e, method)(*args, **kw)
            if hook is not None:
                hook(inst)
            return inst
        return self.op(eng, fn, r=r, w=w, dma=dma, cost=cost, lat=lat)

    def group(self, eng, calls, r=(), w=()):
        def fn(e, hook=None):
            last = None
            for i_, (m, a, k) in enumerate(calls):
                last = getattr(e, m)(*a, **k)
                if i_ == 0 and hook is not None:
                    hook(last)
            return last
        cost = 0.0
        for m, a, k in calls:
            n = 128 if m == "transpose" else self._fsize(a[2])
            cost += n / 2000.0 + (0.02 if n < 512 else 0.0)
        return self.op(eng, fn, r=r, w=w, cost=cost)

    def reschedule(self):
        allops = sorted((o for s in self.streams.values() for o in s), key=lambda o: o.order)
        for o in allops:
            o.users = []
        for o in allops:
            o.ndeps = len(o.deps)
            o.ready = 0.0
            for d in o.deps:
                d.users.append(o)
        finish = {}
        free = {e: 0.0 for e in self.ENGS}
        cand = {e: [] for e in self.ENGS}
        for o in allops:
            if o.ndeps == 0:
                cand[o.eng].append(o)
        newstreams = {e: [] for e in self.ENGS}
        remaining = len(allops)
        WINDOW = SCHED_WINDOW
        oldest = 0
        scheduled = set()
        while remaining:
            best = None
            while oldest < len(allops) and allops[oldest].order in scheduled:
                oldest += 1
            lim = allops[oldest].order + WINDOW if oldest < len(allops) else 1 << 60
            for e in self.ENGS:
                c = cand[e]
                if not c:
                    continue
                bo, bk = None, None
                for o in c:
                    if o.order > lim:
                        continue
                    k = (max(free[e], o.ready), o.prio)
                    if bk is None or k < bk:
                        bo, bk = o, k
                if bo is not None and (best is None or bk < best[1]):
                    best = (bo, bk)
            o, (t0, _) = best
            e = o.eng
            cand[e].remove(o)
            o.start = t0
            t1 = t0 + o.cost
            free[e] = t1
            fin = t1 + o.lat + 0.2
            newstreams[e].append(o)
            scheduled.add(o.order)
            remaining -= 1
            for u in o.users:
                u.ndeps -= 1
                if fin > u.ready:
                    u.ready = fin
                if u.ndeps == 0:
                    cand[u.eng].append(u)
        self.streams = newstreams
        self.sim_time = max(free.values())

    def assign(self, sems, dma_sems):
        cnt = {e: 0 for e in self.ENGS}
        dcnt = {}
        for e in self.ENGS:
            for o in self.streams[e]:
                if o.dma is not None:
                    key, n = o.dma
                    dcnt[key] = dcnt.get(key, 0) + 16 * n
                    o.sem, o.val = dma_sems[key], dcnt[key]
                elif o.signal:
                    cnt[o.eng] += 1
                    o.sem, o.val = sems[o.eng], cnt[o.eng]
        self.dma_tot = dcnt

    def emit_stream(self, eng, e):
        seen = {}
        for o in self.streams[eng]:
            waits = {}
            for d in o.deps:
                if d.dma is None and d.eng == eng:
                    if eng == "pe" or not SAME_ENGINE_SYNC:
                        continue
                k = id(d.sem)
                if seen.get(k, -1) >= d.val:
                    continue
                if k not in waits or waits[k][1] < d.val:
                    waits[k] = (d.sem, d.val)
            wl = list(waits.values())
            for sem, val in wl:
                seen[id(sem)] = val
            hook = None
            if ATTACH_WAIT and wl and o.dma is None:
                s0, v0 = wl.pop()
                hook = (lambda inst, s0=s0, v0=v0: inst._wait_ge(s0, v0))
            for sem, val in wl:
                e.wait_ge(sem, val)
            res = o.fn(e, hook)
            if o.dma is not None:
                lst = res if isinstance(res, (list, tuple)) else [res]
                assert len(lst) == o.dma[1], (len(lst), o.dma)
                for i_ in lst:
                    i_.then_inc(o.sem, 16)
            elif o.signal:
                last = res[-1] if isinstance(res, (list, tuple)) else res
                last.then_inc(o.sem, 1)


def _geometry():
    j = np.arange(128)[:, None]
    i = np.arange(128)[None, :]
    slopes = (2.0 ** (-np.arange(1, 9))).astype(np.float32)
    m0 = (i >= 64) & (j < 64)
    m1 = (j >= 64) & (i < 64)
    d0 = (128 + i - j).astype(np.float32)
    d1 = np.abs(i - j).astype(np.float32)
    tabA = np.zeros((128, 2, 8, 128), np.float32)
    for h in range(8):
        tabA[:, 0, h, :] = np.where(m0, NEG, -slopes[h] * d0)
        tabA[:, 1, h, :] = np.where(m1, NEG, -slopes[h] * d1)
    idx0 = np.clip(128 + i - j, -128, 128) + 128
    idx1 = np.clip(i - j, -128, 128) + 128
    maskT = np.zeros((128, 8, 128), np.float32)
    maskT[:] = np.where(m1, NEG, 0.0)[:, None, :]
    return tabA.reshape(128, 2048), idx0, idx1, maskT.reshape(128, 1024)


def build(NT, stage=9):
    nc = bass.Bass("TRN2", target_bir_lowering=False)
    S = Sched()
    TPC = NT * 128
    RA_, RB_ = 3, 6
    NWR = 6
    NXB = 4

    def din(name, shape):
        return nc.dram_tensor(name, list(shape), F32, kind="ExternalInput").ap()

    def dout(name, shape):
        return nc.dram_tensor(name, list(shape), F32, kind="ExternalOutput").ap()

    xh = din("xh", [512, D]); xm = din("xm", [TPC, D]); pm = din("pm", [TPC, PLE])
    xs = din("xs", [128, D]); psd = din("psd", [128, PLE])
    ckw = din("ckw", [128, 128]); cvw = din("cvw", [128, 128])
    ckb = din("ckb", [512, 512]); cvb = din("cvb", [512, 512])
    w_in = din("w_in", [D, INC]); w_ow = din("w_ow", [512, D]); w_ob = din("w_ob", [512, D])
    w_out = din("w_out", [D, D]); w_pg = din("w_pg", [D, D]); w_pl = din("w_pl", [PLE, D])
    identd = din("ident", [128, 128]); tabAd = din("tabA", [128, 2048]); tabBd = din("tabB", [128, 2048])
    maskTd = din("maskT", [128, 1024]); cbd = din("cbrow", [1, 8]); hmaskd = din("hmask", [128, 4])
    gTind = din("gTin", [128, 8]); gTpled = din("gTple", [128, 8]); gcold = din("gcol", [128, 4])
    gkwbd = din("gkwb", [1, 64]); gkbbd = din("gkbb", [1, 64]); sinkd = din("sinkrow", [1, 8])
    wsc = nc.dram_tensor("wsc", [2 * D, D], BF16, kind="Internal").ap()

    y = dout("y", [TPC, D]); ys = dout("ys", [64, D])
    kwin = dout("kwin", [128, 128]); vwin = dout("vwin", [128, 128])
    kband = dout("kband", [512, 512]); vband = dout("vband", [512, 512])
    kws = dout("kws", [128, 128]); vws = dout("vws", [128, 128])
    kbs = dout("kbs", [512, 512]); vbs = dout("vbs", [512, 512])

    es = ExitStack()

    def sb(name, shape, dt):
        return es.enter_context(nc.sbuf_tensor(name, list(shape), dt))

    def pst(name, shape, dt):
        return es.enter_context(nc.psum_tensor(name, list(shape), dt))

    W_in = sb("W_in", [128, 8, INC], BF16)
    W_ow = sb("W_ow", [128, 4, D], BF16); W_ob = sb("W_ob", [128, 4, D], BF16)
    W_pl = sb("W_pl", [128, 2, D], BF16)
    wring = sb("wring", [128, NWR, D], BF16)
    ident = sb("identb", [128, 128], BF16)
    tabA = sb("tabAs", [128, 2048], BF16); tabB = sb("tabBs", [128, 2048], BF16)
    gTin = sb("gTins", [128, 8], F32); gTple = sb("gTples", [128, 8], F32); gcol = sb("gcols", [128, 4], F32)
    gkwb = sb("gkwbs", [128, 64], F32); gkbb = sb("gkbbs", [128, 64], F32)
    esink = sb("esink", [128, 8], F32); hmask = sb("hmasks", [128, 4], F32); cb = sb("cbs", [128, 8], F32)
    kaT = sb("kaT", [128, RA_, 128], BF16); kbT = sb("kbT", [128, RB_, 512], BF16)
    va = sb("va", [128, RA_, 2, 65], BF16); vb = sb("vb", [128, RB_, 8, 65], BF16)
    xb = [sb(f"xb{i}", [128, D], F32) for i in range(NXB)]
    pb = [sb(f"pb{i}", [128, PLE], F32) for i in range(2)]
    xn = sb("xn", [128, D], BF16); xnT = sb("xnT", [128, D], BF16)
    ybufs = [sb(f"ybuf{i}", [128, D], BF16) for i in range(2)]
    mbuf = sb("mbuf", [128, D], BF16); actT = sb("actT", [128, D], BF16)
    pbf = sb("pbf", [128, PLE], BF16); pT = sb("pT", [128, PLE], BF16)
    qkn = sb("qkn", [128, 1664], BF16)
    qaT = [sb(f"qaT{i}", [128, 512], BF16) for i in range(2)]
    qbT = [sb(f"qbT{i}", [128, 512], BF16) for i in range(2)]
    zs = [sb(f"zs{i}", [128, D], BF16) for i in range(2)]
    tg = [sb(f"tg{i}", [128, 2048], BF16) for i in range(2)]
    ET = [sb(f"ET{i}", [128, 512], BF16) for i in range(3)]
    sqt = sb("sqt", [128, 512], F32); tzt = sb("tzt", [128, 512], F32)
    kvft = sb("kvft", [128, 1280], F32); t1t = sb("t1t", [128, D], F32); ont = sb("ont", [128, 256], F32)
    st = sb("st", [128, 64], F32)
    stq = sb("stq", [128, 16], F32)
    mhalf = sb("mhalf", [128, 8], F32)

    NMM = 3
    ps_mm = [pst(f"psmm{i}", [128, 512], F32) for i in range(NMM)]
    ps_s = [pst(f"pss{i}", [128, 512], F32) for i in range(2)]
    ps_o1 = pst("pso0", [128, 512], F32)
    ps_tl = [pst(f"pstl{i}", [128, 512], F32) for i in range(2)]

    sq = sqt[:]; tz = tzt[:]; kvf = kvft[:]; t1 = t1t[:]; on = ont[:]

    def bc3(ap2, n):
        k = ap2.shape[1]
        return ap2.unsqueeze(2).to_broadcast([128, k, n])

    def v3(ap2, inner):
        return ap2.rearrange("p (a b) -> p a b", b=inner)

    S.ins("pool", "memset", mhalf[:], -0.5, w=["mhalf"])
    S.ins("pool", "memset", va[:], 1.0, w=[f"va{i}" for i in range(RA_)])
    S.ins("pool", "memset", vb[:], 1.0, w=[f"vb{i}" for i in range(RB_)])
    S.ins("sp", "dma_start", out=xb[0][:], in_=xh[0:128, :], w=["xb0"], dma=("ld0", 1))
    S.ins("sp", "dma_start", out=xb[1][:], in_=xh[128:256, :], w=["xb1"], dma=("ld1", 1))
    S.ins("pool", "dma_start", out=ident[:], in_=identd, w=["ident"], dma=("c_ident", 1))
    for nm, dst, src in (("gTin", gTin, gTind), ("gTple", gTple, gTpled), ("gcol", gcol, gcold),
                         ("hmask", hmask, hmaskd)):
        S.ins("sp", "dma_start", out=dst[:], in_=src, w=[nm], dma=("c_" + nm, 1))
    S.ins("sp", "dma_start", out=gkwb[:], in_=gkwbd.to_broadcast([128, 64]), w=["gkwb"], dma=("c_gkwb", 1))
    S.ins("sp", "dma_start", out=gkbb[:], in_=gkbbd.to_broadcast([128, 64]), w=["gkbb"], dma=("c_gkbb", 1))
    S.ins("sp", "dma_start", out=esink[:], in_=sinkd.to_broadcast([128, 8]), w=["esink"], dma=("c_esink", 1))
    S.ins("sp", "dma_start", out=cb[:], in_=cbd.to_broadcast([128, 8]), w=["cb"], dma=("c_cb", 1))
    S.ins("sp", "dma_start", out=kvf[:, 0:1024], in_=tabBd[:, 0:1024], w=["kvf"], dma=("c_tb0", 1))
    S.ins("sp", "dma_start", out=t1[:, 0:1024], in_=tabBd[:, 1024:2048], w=["t1"], dma=("c_tb1", 1))
    mstage = tg[0][:].bitcast(F32)
    MST = [f"tg0_{i}" for i in range(4)]
    S.ins("sp", "dma_start", out=mstage, in_=maskTd, w=MST, dma=("c_mt", 1))
    w_in3 = w_in.rearrange("(c p) n -> p c n", p=128)
    WRANGES = [(512, 768), (1792, 2816), (0, 512), (768, 1792), (2816, 4096), (4096, INC)]
    cumb = [0.0]

    def wlat(nbytes):
        cumb[0] += nbytes
        return 3.0 + cumb[0] / 250e3
    for ri, (c0_, c1_) in enumerate(WRANGES):
        S.ins("pool", "dma_start", out=W_in[:, :, c0_:c1_], in_=w_in3[:, :, c0_:c1_],
              w=[f"W_in_r{ri}"], dma=(f"w_in_r{ri}", 1), lat_override=wlat(D * (c1_ - c0_) * 4))

    S.ins("pool", "dma_start", out=tabA[:], in_=tabAd, w=["tabA"], dma=("c_tabA", 1))

    def win_res(col0, ncols):
        return [f"W_in_r{ri}" for ri, (a_, b_) in enumerate(WRANGES) if a_ < col0 + ncols and col0 < b_]
    for nm, dst, src, kc in (("W_ow", W_ow, w_ow, 4), ("W_ob", W_ob, w_ob, 4), ("W_pl", W_pl, w_pl, 2)):
        for c in range(kc):
            S.ins("pool", "dma_start", out=dst[:, c, :], in_=src[c * 128:(c + 1) * 128, :],
                  w=[f"{nm}{c}"], dma=(f"w_{nm}{c}", 1), lat_override=wlat(128 * D * 4))
    S.ins("pool", "dma_start", out=wsc[0:D, :], in_=w_out, w=["wsc0"], dma=("w2a", 1), lat_override=wlat(D * D * 4))
    S.ins("pool", "dma_start", out=wsc[D:2 * D, :], in_=w_pg, w=["wsc1"], dma=("w2b", 1), lat_override=wlat(D * D * 4))
    S.ins("dve", "tensor_scalar", out=gTin[:], in0=gTin[:], scalar1=32.0, scalar2=None, op0=ALU.mult,
          r=["gTin"], w=["gTin"])
    for col in (1, 3):
        S.ins("dve", "tensor_scalar", out=gcol[:, col:col + 1], in0=gcol[:, col:col + 1], scalar1=8.0,
              scalar2=None, op0=ALU.mult, r=["gcol"], w=["gcol"])

    tiles = []
    for i in range(4):
        tiles.append(dict(kind="halo", vt=i - 4, x=xh[i * 128:(i + 1) * 128, :], p=None))
    for n in range(NT):
        tiles.append(dict(kind="main", vt=n, n=n, x=xm[n * 128:(n + 1) * 128, :], p=pm[n * 128:(n + 1) * 128, :]))
    SB = NT + 8
    tiles.append(dict(kind="sample", vt=SB + 4, x=xs, p=psd))
    for li, t in enumerate(tiles):
        t["li"] = li

    def issue_load(t):
        bi = t["li"] % NXB
        S.ins("sp", "dma_start", out=xb[bi][:], in_=t["x"], w=[f"xb{bi}"], dma=(f"ld{bi}", 1))

    def issue_pload(t):
        pi = t["li"] % 2
        S.ins("sp", "dma_start", out=pb[pi][:], in_=t["p"], w=[f"pb{pi}"], dma=(f"ldp{pi}", 1))

    mmc = [0]

    def next_mm():
        b = mmc[0] % NMM
        mmc[0] += 1
        return ps_mm[b], f"psmm{b}"

    def next_tr():
        bank, res = next_mm()
        return bank[:].bitcast(BF16), res

    def inproj_ct(col0, ncols):
        bank, bres_ = next_mm()
        b = int(bres_[-1])
        for g0 in range(0, 8, INPROJ_SPLIT):
            calls = []
            for c in range(g0, g0 + INPROJ_SPLIT):
                calls.append(("matmul", (bank[:, 0:ncols], xnT[:, c * 128:(c + 1) * 128], W_in[:, c, col0:col0 + ncols]),
                              dict(start=(c == 0), stop=(c == 7))))
            S.group("pe", calls, r=["xnT"] + win_res(col0, ncols), w=[f"psmm{b}"])
        return bank, f"psmm{b}"

    def qknorm(bank, bres, c0, nh, dst, dst_res, sscol, kvf_ap=None, gtab=None, permute=False):
        w = 64 * nh
        S.ins("act", "activation", out=sq[:, 0:w], in_=bank[:, c0:c0 + w], func=AF.Square, r=[bres], w=["sq"])
        S.ins("dve", "tensor_reduce", out=st[:, sscol:sscol + nh], in_=v3(sq[:, 0:w], 64), axis=AX.X, op=ALU.add,
              r=["sq"], w=["st_ss"])
        S.ins("dve", "tensor_scalar", out=st[:, sscol:sscol + nh], in0=st[:, sscol:sscol + nh],
              scalar1=64 * EPS, scalar2=None, op0=ALU.add, r=["st_ss"], w=["st_ss"])
        S.ins("pool", "tensor_tensor", out=st[:, 32 + sscol:32 + sscol + nh], in0=st[:, sscol:sscol + nh],
              in1=mhalf[:, 0:nh], op=ALU.pow, r=["st_ss", "mhalf"], w=["st_rs"])
        rsb = bc3(st[:, 32 + sscol:32 + sscol + nh], 64)
        if permute:
            o_ap = dst.rearrange("p (i kv d) -> p kv i d", kv=2, d=64)
            i_ap = bank[:, c0:c0 + w].rearrange("p (kv i d) -> p kv i d", kv=2, d=64)
            r_ap = st[:, 32 + sscol:32 + sscol + nh].rearrange("p (kv i) -> p kv i", kv=2).unsqueeze(3).to_broadcast(
                [128, 2, 4, 64])
            S.ins("dve", "tensor_tensor", out=o_ap, in0=i_ap, in1=r_ap, op=ALU.mult, r=[bres, "st_rs"], w=[dst_res])
        else:
            S.ins("dve", "tensor_tensor", out=v3(dst, 64), in0=v3(bank[:, c0:c0 + w], 64), in1=rsb, op=ALU.mult,
                  r=[bres, "st_rs"], w=[dst_res])
        if kvf_ap is not None:
            S.ins("dve", "tensor_tensor", out=v3(kvf_ap, 64), in0=v3(bank[:, c0:c0 + w], 64), in1=rsb, op=ALU.mult,
                  r=[bres, "st_rs"], w=["kvf"])
            S.ins("pool", "tensor_tensor", out=v3(kvf_ap, 64), in0=v3(kvf_ap, 64),
                  in1=gtab.unsqueeze(1).to_broadcast([128, nh, 64]), op=ALU.mult, r=["kvf"], w=["kvf"])

    def stage_P(t):
        kind = t["kind"]
        vt = t["vt"]
        X = xb[t["li"] % NXB]
        xres = f"xb{t['li'] % NXB}"
        par = t["li"] % 2
        kvout = (kind == "sample") or (kind == "main" and t["n"] >= NT - 4)
        sa, sbn = vt % RA_, vt % RB_
        S.boost = P_BOOST
        S.ins("act", "activation", out=xn[:], in_=X[:], func=AF.Square, accum_out=st[:, 60:61], r=[xres],
              w=["xn", "st_x"])
        S.ins("dve", "tensor_scalar", out=st[:, 60:61], in0=st[:, 60:61], scalar1=1024 * EPS, scalar2=None,
              op0=ALU.add, r=["st_x"], w=["st_x"])
        S.ins("pool", "tensor_tensor", out=st[:, 61:62], in0=st[:, 60:61], in1=mhalf[:, 0:1], op=ALU.pow,
              r=["st_x", "mhalf"], w=["st_rx"])
        S.ins("act", "mul", xn[:], X[:], st[:, 61:62], r=[xres, "st_rx"], w=["xn"])
        trx, trres = next_tr()
        S.group("pe", [("transpose", (trx[:, c * 128:(c + 1) * 128], xn[:, c * 128:(c + 1) * 128], ident[:]), {})
                       for c in range(8)], r=["xn", "ident"], w=[trres])
        S.ins("dve", "tensor_tensor", out=v3(xnT[:], 128), in0=v3(trx[:, :], 128), in1=bc3(gTin[:, 0:8], 128),
              op=ALU.mult, r=[trres, "gTin"], w=["xnT"])
        S.boost = 0
        yield
        if kind != "halo":
            bank, br = inproj_ct(0, 512)
            qknorm(bank, br, 0, 8, qkn[:, 0:512], "qkn_qa", 0, permute=True)
            yield
        bank, br = inproj_ct(512, 256)
        qknorm(bank, br, 0, 2, qkn[:, 1024:1152], "qkn_ka", 8,
               kvf_ap=(kvf[:, 0:128] if kvout else None), gtab=gkwb[:])
        S.ins("act", "copy", va[:, sa, :, 0:64], v3(bank[:, 128:256], 64), r=[br], w=[f"va{sa}"])
        if kvout:
            S.ins("dve", "tensor_copy", out=kvf[:, 128:256], in_=bank[:, 128:256], r=[br], w=["kvf"])
        yield
        if kind != "halo":
            bank, br = inproj_ct(768, 512)
            S.ins("act", "activation", out=tz, in_=bank[:], func=AF.Tanh, scale=0.5, r=[br], w=["tz"])
            S.ins("dve", "scalar_tensor_tensor", out=zs[par][:, 0:512], in0=tz, scalar=1.0, in1=bank[:], op0=ALU.add,
                  op1=ALU.mult, r=["tz", br], w=[f"zs{par}a"])
            yield
            bank, br = inproj_ct(1280, 512)
            qknorm(bank, br, 0, 8, qkn[:, 512:1024], "qkn_qb", 10)
            yield
        bank, br = inproj_ct(1792, 512)
        qknorm(bank, br, 0, 8, qkn[:, 1152:1664], "qkn_kb", 18,
               kvf_ap=(kvf[:, 256:768] if kvout else None), gtab=gkbb[:])
        yield
        bank, br = inproj_ct(2304, 512)
        S.ins("act", "copy", vb[:, sbn, :, 0:64], v3(bank[:], 64), r=[br], w=[f"vb{sbn}"])
        if kvout:
            S.ins("dve", "tensor_copy", out=kvf[:, 768:1280], in_=bank[:], r=[br], w=["kvf"])
        yield
        if kind != "halo":
            bank, br = inproj_ct(2816, 512)
            S.ins("act", "activation", out=tz, in_=bank[:], func=AF.Tanh, scale=0.5, r=[br], w=["tz"])
            S.ins("dve", "scalar_tensor_tensor", out=zs[par][:, 512:1024], in0=tz, scalar=1.0, in1=bank[:],
                  op0=ALU.add, op1=ALU.mult, r=["tz", br], w=[f"zs{par}b"])
            yield
            for gi in range(4):
                bank, br = inproj_ct(3328 + gi * 512, 512)
                S.ins("act", "activation", out=tg[par][:, gi * 512:(gi + 1) * 512], in_=bank[:], func=AF.Tanh,
                      scale=0.5, r=[br], w=[f"tg{par}_{gi}"])
                yield
        if kvout:
            if kind == "main":
                r0 = (t["n"] - (NT - 4)) * 128
                S.ins("sp", "dma_start", out=kband[r0:r0 + 128, :], in_=kvf[:, 256:768], r=["kvf"], dma=("kv", 1))
                S.ins("sp", "dma_start", out=vband[r0:r0 + 128, :], in_=kvf[:, 768:1280], r=["kvf"], dma=("kv", 1))
                if t["n"] == NT - 1:
                    S.ins("sp", "dma_start", out=kwin[:, :], in_=kvf[:, 0:128], r=["kvf"], dma=("kv", 1))
                    S.ins("sp", "dma_start", out=vwin[:, :], in_=kvf[:, 128:256], r=["kvf"], dma=("kv", 1))
            else:
                S.ins("sp", "dma_start", out=kbs[448:512, :], in_=kvf[0:64, 256:768], r=["kvf"], dma=("kv", 1))
                S.ins("sp", "dma_start", out=vbs[448:512, :], in_=kvf[0:64, 768:1280], r=["kvf"], dma=("kv", 1))
                S.ins("sp", "dma_start", out=kws[64:128, :], in_=kvf[0:64, 0:128], r=["kvf"], dma=("kv", 1))
                S.ins("sp", "dma_start", out=vws[64:128, :], in_=kvf[0:64, 128:256], r=["kvf"], dma=("kv", 1))
        if kind != "halo":
            trx, trres = next_tr()
            S.group("pe", [("transpose", (trx[:, c * 128:(c + 1) * 128], qkn[:, c * 128:(c + 1) * 128], ident[:]), {})
                           for c in range(8)], r=["qkn_qa", "qkn_qb", "ident"], w=[trres])
            S.ins("act", "mul", qaT[par][:], trx[:, 0:512], gcol[:, 0:1], r=[trres, "gcol"], w=[f"qaT{par}"])
            S.ins("act", "mul", qbT[par][:], trx[:, 512:1024], gcol[:, 2:3], r=[trres, "gcol"], w=[f"qbT{par}"])
            yield
        trx, trres = next_tr()
        S.group("pe", [("transpose", (trx[:, c * 128:(c + 1) * 128], qkn[:, 1024 + c * 128:1152 + c * 128], ident[:]), {})
                       for c in range(5)], r=["qkn_ka", "qkn_kb", "ident"], w=[trres])
        S.ins("act", "mul", kaT[:, sa, :], trx[:, 0:128], gcol[:, 1:2], r=[trres, "gcol"], w=[f"kaT{sa}"])
        S.ins("act", "mul", kbT[:, sbn, :], trx[:, 128:640], gcol[:, 3:4], r=[trres, "gcol"], w=[f"kbT{sbn}"])
        yield

    wrc = [0]

    def stage_A(t):
        kind = t["kind"]
        vt = t["vt"]
        par = t["li"] % 2
        ybuf = ybufs[par]
        yres = f"ybuf{par}"
        uc = [0]
        obank = ps_o1
        ores = "pso0"
        for mixer in ("A", "B"):
            kts = [vt - 1, vt] if mixer == "A" else [vt - 4, vt - 3, vt - 2, vt - 1, vt]
            for hh in range(2):
                for n_, kt in enumerate(kts):
                    u = uc[0]
                    uc[0] += 1
                    sbk = ps_s[u % 2]
                    sres = f"pss{u % 2}"
                    e_i = u % 3
                    E = ET[e_i]
                    eres = f"ET{e_i}"
                    rel = kt - vt
                    tab = None
                    if mixer == "A":
                        sl = kt % RA_
                        S.ins("pe", "matmul", sbk[:, 0:512], kaT[hh * 64:(hh + 1) * 64, sl, :],
                              qaT[par][hh * 64:(hh + 1) * 64, :], start=True, stop=True,
                              r=[f"kaT{sl}", f"qaT{par}"], w=[sres])
                        tab = v3(tabA[:, (rel + 1) * 1024 + hh * 512:(rel + 1) * 1024 + (hh + 1) * 512], 128)
                        tres = "tabA"
                    else:
                        sl = kt % RB_
                        calls = []
                        for hq in range(4):
                            pr, half = hq, hh
                            calls.append(("matmul", (sbk[:, hq * 128:(hq + 1) * 128],
                                                     kbT[half * 64:(half + 1) * 64, sl, pr * 128:(pr + 1) * 128],
                                                     qbT[par][half * 64:(half + 1) * 64, pr * 128:(pr + 1) * 128]),
                                          dict(start=True, stop=True)))
                        S.group("pe", calls, r=[f"kbT{sl}", f"qbT{par}"], w=[sres])
                        if rel >= -1:
                            tab = tabB[:, (rel + 1) * 1024:(rel + 2) * 1024].rearrange(
                                "p (hq two i) -> p two hq i", two=2, i=128)[:, hh]
                            tres = "tabB"
                    if tab is not None:
                        S.ins("dve", "tensor_tensor", out=v3(sbk[:], 128), in0=v3(sbk[:], 128), in1=tab, op=ALU.add,
                              r=[sres, tres], w=[sres])
                    if kind == "main" and kt < 0:
                        S.ins("act", "activation", out=E[:], in_=sbk[:], func=AF.Exp, bias=hmask[:, kt + 4:kt + 5],
                              r=[sres, "hmask"], w=[eres])
                    else:
                        S.ins("act", "activation", out=E[:], in_=sbk[:], func=AF.Exp, r=[sres], w=[eres])
                    if mixer == "B" and rel == -4:
                        S.ins("pool", "memset", v3(E[0:64, :], 128)[:, :, 64:128], 0.0, r=[eres], w=[eres])
                    calls = []
                    for hq in range(4):
                        if mixer == "A":
                            rhs = va[:, sl, hh, 0:65]
                        else:
                            rhs = vb[:, sl, 2 * hq + hh, 0:65]
                        calls.append(("matmul", (obank[:, hq * 65:(hq + 1) * 65], E[:, hq * 128:(hq + 1) * 128], rhs),
                                      dict(start=(n_ == 0 and hq == 0), stop=(n_ == len(kts) - 1 and hq == 3),
                                           skip_group_check=True)))
                    S.group("pe", calls, r=[eres, (f"va{sl}" if mixer == "A" else f"vb{sl}")], w=[ores])
                    yield
                o3 = obank[:, 0:260].rearrange("p (h e) -> p h e", e=65)
                if mixer == "A":
                    S.ins("dve", "tensor_tensor", out=stq[:, 4:8], in0=o3[:, :, 64], in1=esink[:, 4 * hh:4 * hh + 4],
                          op=ALU.add, r=[ores, "esink"], w=["st_den"])
                    S.ins("dve", "reciprocal", out=stq[:, 0:4], in_=stq[:, 4:8], r=["st_den"], w=["st_rd"])
                else:
                    S.ins("dve", "reciprocal", out=stq[:, 0:4], in_=o3[:, :, 64], r=[ores], w=["st_rd"])
                S.ins("dve", "tensor_tensor", out=v3(on, 64), in0=o3[:, :, 0:64], in1=bc3(stq[:, 0:4], 64), op=ALU.mult,
                      r=[ores, "st_rd"], w=["on"])
                if mixer == "A":
                    yc = hh * 256
                    y_ap, z_ap = v3(ybuf[:, yc:yc + 256], 64), v3(zs[par][:, yc:yc + 256], 64)
                else:
                    y_ap = ybuf[:, 512:1024].rearrange("p (hq two d) -> p two hq d", two=2, d=64)[:, hh]
                    z_ap = zs[par][:, 512:1024].rearrange("p (hq two d) -> p two hq d", two=2, d=64)[:, hh]
                S.ins("pool", "tensor_tensor", out=y_ap, in0=v3(on, 64), in1=z_ap, op=ALU.mult,
                      r=["on", f"zs{par}a" if mixer == "A" else f"zs{par}b"], w=[yres])
                yield
        pi = t["li"] % 2
        S.ins("pool", "tensor_copy", out=pbf[:], in_=pb[pi][:], r=[f"pb{pi}"], w=["pbf"])
        trp, trpres = ps_o1[:].bitcast(BF16), "pso0"
        S.group("pe", [("transpose", (trp[:, c * 128:(c + 1) * 128], pbf[:, c * 128:(c + 1) * 128], ident[:]), {})
                       for c in range(2)], r=["pbf", "ident"], w=[trpres])
        S.ins("act", "copy", pT[:], trp[:, 0:256], r=[trpres], w=["pT"])
        yield

    def stage_Z(t):
        kind = t["kind"]
        X = xb[t["li"] % NXB]
        xres = f"xb{t['li'] % NXB}"
        par = t["li"] % 2
        pi = t["li"] % 2
        ybuf = ybufs[par]
        yres = f"ybuf{par}"
        TL = ["pstl0", "pstl1"]
        trz = ps_tl[0][:].bitcast(BF16)
        wslots = []
        for c in range(16):
            wslots.append(wrc[0] % NWR)
            wrc[0] += 1

        def wload(c):
            S.ins("sp", "dma_start", out=wring[:, wslots[c], :], in_=wsc[c * 128:(c + 1) * 128, :],
                  r=["wsc0" if c < 8 else "wsc1"], w=[f"wr{wslots[c]}"], dma=(f"wr{wslots[c]}", 1))
        for c in range(NWR):
            wload(c)
        wnext = [NWR]

        trzs = [ps_tl[0][:].bitcast(BF16), ps_tl[1][:].bitcast(BF16)]

        def transposes8(src, sres_, banks=None):
            sres_l = sres_ if isinstance(sres_, (list, tuple)) else [sres_, sres_]
            if banks is None:
                banks = [(trzs[0], TL[0]), (trzs[1], TL[1])]
            for hf in range(2):
                tb, tres_ = banks[hf]
                S.group("pe", [("transpose", (tb[:, c * 128:(c + 1) * 128],
                                              src[:, (4 * hf + c) * 128:(4 * hf + c + 1) * 128], ident[:]), {})
                               for c in range(4)], r=[sres_l[hf], "ident"], w=[tres_])
            return banks

        def evac_plain(banks):
            S.ins("act", "copy", actT[:, 0:512], banks[0][0][:, 0:512], r=[banks[0][1]], w=["actT_a"])
            S.ins("dve", "tensor_copy", out=actT[:, 512:1024], in_=banks[1][0][:, 0:512], r=[banks[1][1]], w=["actT_b"])

        def dense(Wt, wres, k0, kc, src, sres_):
            for nb in range(2):
                calls = []
                for c in range(kc):
                    calls.append(("matmul", (ps_tl[nb][:], src[:, (k0 + c) * 128:(k0 + c + 1) * 128],
                                             Wt[:, c, nb * 512:(nb + 1) * 512]),
                                  dict(start=(c == 0), stop=(c == kc - 1))))
                S.group("pe", calls, r=[sres_] + [f"{wres}{c}" for c in range(kc)], w=[TL[nb]])

        def dense_stream(c0):
            for c in range(8):
                slot = wslots[c0 + c]
                calls = []
                for nb in range(2):
                    calls.append(("matmul", (ps_tl[nb][:], actT[:, c * 128:(c + 1) * 128],
                                             wring[:, slot, nb * 512:(nb + 1) * 512]),
                                  dict(start=(c == 0), stop=(c == 7))))
                S.group("pe", calls, r=["actT_a" if c < 4 else "actT_b", f"wr{slot}"], w=[TL[0], TL[1]])
                if wnext[0] < 16:
                    wload(wnext[0])
                    wnext[0] += 1

        evac_plain(transposes8(ybuf, yres))
        yield
        dense(W_ow, "W_ow", 0, 4, actT, "actT_a")
        for nb in range(2):
            S.ins("dve", "scalar_tensor_tensor", out=t1[:, nb * 512:(nb + 1) * 512],
                  in0=tg[par][:, nb * 512:(nb + 1) * 512], scalar=1.0, in1=ps_tl[nb][:], op0=ALU.add, op1=ALU.mult,
                  r=[f"tg{par}_{nb}", TL[nb]], w=["t1"])
        yield
        dense(W_ob, "W_ob", 4, 4, actT, "actT_b")
        for nb in range(2):
            S.ins("dve", "scalar_tensor_tensor", out=ps_tl[nb][:],
                  in0=tg[par][:, 1024 + nb * 512:1024 + (nb + 1) * 512],
                  scalar=1.0, in1=ps_tl[nb][:], op0=ALU.add, op1=ALU.mult, r=[f"tg{par}_{2 + nb}", TL[nb]],
                  w=[TL[nb]])
            S.ins("dve", "tensor_tensor", out=mbuf[:, nb * 512:(nb + 1) * 512], in0=ps_tl[nb][:],
                  in1=t1[:, nb * 512:(nb + 1) * 512], op=ALU.add, r=[TL[nb], "t1"], w=[f"mbuf_{nb}"])
        evac_plain(transposes8(mbuf, ["mbuf_0", "mbuf_1"]))
        yield
        dense_stream(0)
        yield
        for nb in range(2):
            S.ins("dve", "scalar_tensor_tensor", out=X[:, nb * 512:(nb + 1) * 512], in0=ps_tl[nb][:], scalar=0.25,
                  in1=X[:, nb * 512:(nb + 1) * 512], op0=ALU.mult, op1=ALU.add, r=[TL[nb], xres], w=[xres])
        S.ins("act", "copy", mbuf[:, 0:512], X[:, 0:512], r=[xres], w=["mbuf_0"])
        S.ins("dve", "tensor_copy", out=mbuf[:, 512:1024], in_=X[:, 512:1024], r=[xres], w=["mbuf_1"])
        S.ins("act", "activation", out=ybuf[:], in_=X[:], func=AF.Square, accum_out=stq[:, 8:9], r=[xres],
              w=[yres, "st_h"])
        S.ins("dve", "tensor_scalar", out=stq[:, 8:9], in0=stq[:, 8:9], scalar1=1024 * EPS, scalar2=4.0,
              op0=ALU.add, op1=ALU.mult, r=["st_h"], w=["st_h"])
        S.ins("pool", "tensor_tensor", out=stq[:, 9:10], in0=stq[:, 8:9], in1=mhalf[:, 0:1], op=ALU.pow,
              r=["st_h", "mhalf"], w=["st_rh"])
        bk = transposes8(mbuf, ["mbuf_0", "mbuf_1"])
        for hf in range(2):
            S.ins("dve", "tensor_tensor", out=v3(actT[:, hf * 512:(hf + 1) * 512], 128), in0=v3(bk[hf][0][:, 0:512], 128),
                  in1=bc3(gTple[:, 4 * hf:4 * hf + 4], 128), op=ALU.mult, r=[bk[hf][1], "gTple"],
                  w=["actT_a" if hf == 0 else "actT_b"])
        yield
        dense_stream(8)
        yield
        for nb in range(2):
            S.ins("act", "activation", out=t1[:, nb * 512:(nb + 1) * 512], in_=ps_tl[nb][:], func=AF.Tanh,
                  scale=stq[:, 9:10], r=[TL[nb], "st_rh"], w=["t1"])
        dense(W_pl, "W_pl", 0, 2, pT, "pT")
        yield
        for nb in range(2):
            S.ins("dve", "scalar_tensor_tensor", out=ps_tl[nb][:], in0=t1[:, nb * 512:(nb + 1) * 512],
                  scalar=1.0, in1=ps_tl[nb][:], op0=ALU.add, op1=ALU.mult, r=["t1", TL[nb]], w=[TL[nb]])
            S.ins("dve", "scalar_tensor_tensor", out=X[:, nb * 512:(nb + 1) * 512], in0=ps_tl[nb][:], scalar=0.5,
                  in1=X[:, nb * 512:(nb + 1) * 512], op0=ALU.mult, op1=ALU.add, r=[TL[nb], xres], w=[xres])
        if kind == "main":
            n = t["n"]
            S.ins("sp", "dma_start", out=y[n * 128:(n + 1) * 128, :], in_=X[:], r=[xres], dma=("st_" + xres, 1))
        else:
            S.ins("sp", "dma_start", out=ys[:, :], in_=X[0:64, :], r=[xres], dma=("st_" + xres, 1))
        yield

    def stage_C():
        regions = [(0, "qkn_qa"), (512, "qkn_qb"), (1152, "qkn_kb"), (0, "qkn_qa")]
        for i in range(4):
            sl = (SB + i) % RB_
            off, rres = regions[i]
            S.ins("pool", "dma_start", out=qkn[:, off:off + 512], in_=ckb[i * 128:(i + 1) * 128, :], w=[rres],
                  dma=(f"ccK{i % 3}", 1))
            trx, trres = next_tr()
            S.group("pe", [("transpose", (trx[:, c * 128:(c + 1) * 128], qkn[:, off + c * 128:off + (c + 1) * 128], ident[:]), {})
                           for c in range(4)], r=[rres, "ident"], w=[trres])
            S.ins("act", "copy", kbT[:, sl, :], trx[:, 0:512], r=[trres], w=[f"kbT{sl}"])
            S.ins("pool", "dma_start", out=vb[:, sl, :, 0:64], in_=v3(cvb[i * 128:(i + 1) * 128, :], 64),
                  w=[f"vb{sl}"], dma=(f"ccV{i}", 1))
            yield
        sl = (SB + 3) % RA_
        S.ins("pool", "dma_start", out=qkn[:, 1024:1152], in_=ckw[:, :], w=["qkn_ka"], dma=("ccKw", 1))
        trx, trres = next_tr()
        S.ins("pe", "transpose", trx[:, 0:128], qkn[:, 1024:1152], ident[:], r=["qkn_ka", "ident"], w=[trres])
        S.ins("act", "copy", kaT[:, sl, :], trx[:, 0:128], r=[trres], w=[f"kaT{sl}"])
        S.ins("pool", "dma_start", out=va[:, sl, :, 0:64], in_=v3(cvw[:, :], 64), w=[f"va{sl}"], dma=("ccVw", 1))
        S.ins("sp", "dma_start", out=kws[0:64, :], in_=ckw[64:128, :], dma=("kvc", 1))
        S.ins("sp", "dma_start", out=vws[0:64, :], in_=cvw[64:128, :], dma=("kvc", 1))
        S.ins("sp", "dma_start", out=kbs[0:448, :], in_=ckb[64:512, :], dma=("kvc", 1))
        S.ins("sp", "dma_start", out=vbs[0:448, :], in_=cvb[64:512, :], dma=("kvc", 1))
        yield

    def late_setup():
        LATE = [f"kbT{(-1) % RB_}"]
        S.ins("act", "activation", out=esink[:], in_=esink[:], func=AF.Exp, r=["esink"] + LATE, w=["esink"])
        S.ins("dve", "tensor_scalar", out=gTple[:], in0=gTple[:], scalar1=32.0, scalar2=None, op0=ALU.mult,
              r=["gTple"] + LATE, w=["gTple"])
        S.ins("dve", "tensor_scalar", out=gkwb[:], in0=gkwb[:], scalar1=8.0, scalar2=None, op0=ALU.mult,
              r=["gkwb"] + LATE, w=["gkwb"])
        S.ins("dve", "tensor_scalar", out=gkbb[:], in0=gkbb[:], scalar1=8.0, scalar2=None, op0=ALU.mult,
              r=["gkbb"] + LATE, w=["gkbb"])
        S.ins("dve", "tensor_tensor", out=v3(tabB[:, 0:1024], 128), in0=v3(kvf[:, 0:1024], 128),
              in1=bc3(cb[:, 0:8], 128), op=ALU.subtract, r=["kvf", "cb"] + LATE, w=["tabB"])
        S.ins("dve", "tensor_tensor", out=v3(t1[:, 0:1024], 128), in0=v3(t1[:, 0:1024], 128),
              in1=bc3(cb[:, 0:8], 128), op=ALU.subtract, r=["t1", "cb"] + LATE, w=["t1"])
        S.ins("dve", "tensor_tensor", out=tabB[:, 1024:2048], in0=t1[:, 0:1024], in1=mstage, op=ALU.add,
              r=["t1"] + MST, w=["tabB"])

    def run(gen):
        for _ in gen:
            pass

    halos = [t for t in tiles if t["kind"] == "halo"]
    mains = [t for t in tiles if t["kind"] == "main"]
    samp = [t for t in tiles if t["kind"] == "sample"][0]
    seq = []
    if stage >= 1:
        seq += [("P", t) for t in halos]
    if stage >= 2:
        for t in mains:
            seq += [("P", t), ("A", t), ("Z", t)]
    if stage >= 3:
        seq += [("P", samp), ("C", None), ("A", samp), ("Z", samp)]
    porder = [t for k, t in seq if k == "P"]
    nextload = [0]

    def load_upto(k):
        while nextload[0] < len(porder) and nextload[0] <= k:
            issue_load(porder[nextload[0]])
            nextload[0] += 1

    pidx = 0
    nextload[0] = 2
    late_done = False
    for k, t in seq:
        if not late_done and not (k == "P" and t["kind"] == "halo"):
            late_setup()
            late_done = True
        S.cur = (k, t["li"] if t is not None else -1)
        if k == "P":
            load_upto(pidx + 2)
            pidx += 1
            run(stage_P(t))
        elif k == "A":
            issue_pload(t)
            run(stage_A(t))
        elif k == "Z":
            run(stage_Z(t))
        else:
            run(stage_C())

    if not late_done:
        late_setup()
    dma_keys = sorted({o.dma[0] for s_ in S.streams.values() for o in s_ if o.dma is not None})
    sems = {e: es.enter_context(nc.semaphore(f"s_{e}")) for e in Sched.ENGS}
    dsems = {k: es.enter_context(nc.semaphore(f"d_{k}")) for k in dma_keys}
    if RESCHEDULE:
        S.reschedule()
        print("[sched] simulated time per core: %.1f us" % S.sim_time)
    S.assign(sems, dsems)
    with nc.Block() as block:
        @block.tensor
        def _(e):
            S.emit_stream("pe", e)

        @block.scalar
        def _(e):
            S.emit_stream("act", e)

        @block.vector
        def _(e):
            S.emit_stream("dve", e)

        @block.gpsimd
        def _(e):
            S.emit_stream("pool", e)

        @block.sync
        def _(e):
            S.emit_stream("sp", e)
            for k, tot in S.dma_tot.items():
                e.wait_ge(dsems[k], tot)
    es.close()
    return nc


_CACHE = {}


def _make_in_maps(inp, NT):
    f = lambda a: np.ascontiguousarray(np.asarray(a, dtype=np.float32))
    xp = f(inp["x_prompt"]); xsm = f(inp["x_sample"])
    pp = f(inp["p_prompt"])[0]; psm = f(inp["p_sample"])[0]
    ckw = f(inp["cache_k_win"])[0]; cvw = f(inp["cache_v_win"])[0]
    ckb = f(inp["cache_k_band"])[0]; cvb = f(inp["cache_v_band"])[0]
    TPC = NT * 128
    tabA, idx0, idx1, maskT = _geometry()
    rb = f(inp["rel_bias_band"])[0]
    tb = np.stack([rb[:, idx0], rb[:, idx1]], axis=0)
    tabB = np.ascontiguousarray(tb.transpose(2, 0, 1, 3)).reshape(128, 2048)
    shared = dict(
        w_in=f(inp["w_in"])[0], w_ow=f(inp["w_o_win"])[0], w_ob=f(inp["w_o_band"])[0],
        w_out=f(inp["w_out"])[0], w_pg=f(inp["w_ple_gate"])[0], w_pl=f(inp["w_ple"])[0],
        ident=np.eye(128, dtype=np.float32), tabA=tabA, tabB=tabB, maskT=maskT,
        cbrow=np.ascontiguousarray(rb[:, 256][None, :]),
        gTin=np.ascontiguousarray(f(inp["g_in"])[0].reshape(8, 128).T),
        gTple=np.ascontiguousarray(f(inp["g_ple"])[0].reshape(8, 128).T),
        gcol=np.ascontiguousarray(np.stack([np.tile(f(inp["g_q_win"])[0], 2), np.tile(f(inp["g_k_win"])[0], 2),
                                            np.tile(f(inp["g_q_band"])[0], 2), np.tile(f(inp["g_k_band"])[0], 2)], axis=1)),
        gkwb=np.ascontiguousarray(f(inp["g_k_win"])[0][None, :]),
        gkbb=np.ascontiguousarray(f(inp["g_k_band"])[0][None, :]),
        sinkrow=np.ascontiguousarray(f(inp["sink_win"])[0][None, :]),
    )
    maps = []
    for k in range(8):
        b, j = k // 4, k % 4
        s0 = j * TPC
        xh = np.zeros((512, D), np.float32)
        hm = np.zeros((128, 4), np.float32)
        for i in range(4):
            lo = s0 - 512 + i * 128
            if lo >= 0:
                xh[i * 128:(i + 1) * 128] = xp[b, lo:lo + 128]
            else:
                hm[:, i] = NEG
        xs_ = np.zeros((128, D), np.float32); xs_[:64] = xsm[k]
        ps_ = np.zeros((128, PLE), np.float32); ps_[:64] = psm[k]
        m = dict(shared)
        m.update(xh=xh, xm=np.ascontiguousarray(xp[b, s0:s0 + TPC]), pm=np.ascontiguousarray(pp[b, s0:s0 + TPC]),
                 xs=xs_, psd=ps_, hmask=hm,
                 ckw=np.ascontiguousarray(ckw[k].reshape(128, 128)), cvw=np.ascontiguousarray(cvw[k].reshape(128, 128)),
                 ckb=np.ascontiguousarray(ckb[k].reshape(512, 512)), cvb=np.ascontiguousarray(cvb[k].reshape(512, 512)))
        maps.append(m)
    return maps


def kernel(**inp):
    Sq = inp["x_prompt"].shape[1]
    NT = Sq // 4 // 128
    if NT not in _CACHE:
        _CACHE[NT] = build(NT)
    nc = _CACHE[NT]
    maps = _make_in_maps(inp, NT)
    res = run_bass_kernel_spmd(nc, maps, core_ids=list(range(8)))
    R = res.results
    TPC = NT * 128
    y = np.zeros((2, Sq, D), np.float32)
    for k in range(8):
        y[k // 4, (k % 4) * TPC:(k % 4 + 1) * TPC] = R[k]["y"]
    ysam = np.stack([R[k]["ys"] for k in range(8)], axis=0)
    last = [3, 7]
    kwp = np.stack([R[k]["kwin"].reshape(128, 2, 64) for k in last])[None]
    vwp = np.stack([R[k]["vwin"].reshape(128, 2, 64) for k in last])[None]
    kbp = np.stack([R[k]["kband"].reshape(512, 8, 64) for k in last])[None]
    vbp = np.stack([R[k]["vband"].reshape(512, 8, 64) for k in last])[None]
    kwsam = np.stack([R[k]["kws"].reshape(128, 2, 64) for k in range(8)])[None]
    vwsam = np.stack([R[k]["vws"].reshape(128, 2, 64) for k in range(8)])[None]
    kbsam = np.stack([R[k]["kbs"].reshape(512, 8, 64) for k in range(8)])[None]
    vbsam = np.stack([R[k]["vbs"].reshape(512, 8, 64) for k in range(8)])[None]
    return (y, ysam.astype(np.float32), kwp, vwp, kbp, vbp, kwsam, vwsam, kbsam, vbsam)
```

```python
import numpy as np
from contextlib import ExitStack

import concourse.bass as bass
import concourse.mybir as mybir
from concourse.bass_utils import run_bass_kernel_spmd

F32 = mybir.dt.float32
BF16 = mybir.dt.bfloat16
AF = mybir.ActivationFunctionType
ALU = mybir.AluOpType
AX = mybir.AxisListType

D = 1024
INC = 5376
PLE = 256
EPS = 1e-6
NEG = -30000.0
RA = 2
RB = 5
SAME_ENGINE_SYNC = True
ATTACH_WAIT = True
RESCHEDULE = True
SCHED_WINDOW = 400
P_BOOST = 200
INPROJ_SPLIT = 2


class _Op:
    __slots__ = ("eng", "fn", "deps", "dma", "signal", "val", "sem", "order", "cost", "lat", "users", "ndeps",
                 "ready", "start", "tag", "prio")


class Sched:
    ENGS = ("pe", "act", "dve", "pool", "sp")

    def __init__(self):
        self.streams = {e: [] for e in self.ENGS}
        self.lastw = {}
        self.readers = {}
        self.n = 0
        self.dma_tot = {}

    LIMIT = None

    def op(self, eng, fn, r=(), w=(), dma=None, cost=0.3, lat=0.0):
        if Sched.LIMIT is not None and self.n >= Sched.LIMIT:
            return None
        o = _Op()
        o.cost, o.lat = cost, lat
        o.tag = getattr(self, "cur", None)
        o.prio = self.n - getattr(self, "boost", 0)
        o.eng, o.fn, o.dma, o.signal, o.val, o.sem = eng, fn, dma, False, None, None
        o.order = self.n
        self.n += 1
        w = list(w) + [res for res in r if res.startswith("ps") and res not in w]
        deps = {}
        for res in r:
            d = self.lastw.get(res)
            if d is not None:
                deps[d.order] = d
        for res in w:
            d = self.lastw.get(res)
            if d is not None:
                deps[d.order] = d
            for rd in self.readers.get(res, ()):
                deps[rd.order] = rd
        for res in w:
            self.lastw[res] = o
            self.readers[res] = []
        ws = set(w)
        for res in r:
            if res not in ws:
                self.readers.setdefault(res, []).append(o)
        o.deps = list(deps.values())
        for d in o.deps:
            d.signal = True
        if dma is not None:
            o.signal = True
        self.streams[eng].append(o)
        return o

    @staticmethod
    def _fsize(ap):
        try:
            n = 1
            for d in ap.shape[1:]:
                n *= int(d)
            return n
        except Exception:
            return 256

    def ins(self, eng, method, *args, r=(), w=(), dma=None, lat_override=None, **kw):
        out = kw.get("out", args[0] if args else None)
        n = self._fsize(out) if out is not None else 256
        lat = 0.0
        if dma is not None:
            cost = 0.45 if eng == "sp" else 1.0
            nbytes = n * out.shape[0] * 4
            lat = 2.5 + nbytes / 150e3
            if lat_override is not None:
                lat = lat_override
        elif eng == "pe":
            cost = n / 2000.0 + (0.02 if n < 512 else 0.0)
        elif eng == "act":
            cost = 0.18 + n / 1200.0
        elif eng == "dve":
            cost = 0.15 + n / 950.0
        else:
            cost = (1.0 if kw.get("op", None) == ALU.pow else 0.3 + n / 600.0)
        def fn(e, hook=None):
            inst = getattr(e, method)(*args, **kw)
            if hook is not None:
                hook(inst)
            return inst
        return self.op(eng, fn, r=r, w=w, dma=dma, cost=cost, lat=lat)

    def group(self, eng, calls, r=(), w=()):
        def fn(e, hook=None):
            last = None
            for i_, (m, a, k) in enumerate(calls):
                last = getattr(e, m)(*a, **k)
                if i_ == 0 and hook is not None:
                    hook(last)
            return last
        cost = 0.0
        for m, a, k in calls:
            n = 128 if m == "transpose" else self._fsize(a[2])
            cost += n / 2000.0 + (0.02 if n < 512 else 0.0)
        return self.op(eng, fn, r=r, w=w, cost=cost)

    def reschedule(self):
        allops = sorted((o for s in self.streams.values() for o in s), key=lambda o: o.order)
        for o in allops:
            o.users = []
        for o in allops:
            o.ndeps = len(o.deps)
            o.ready = 0.0
            for d in o.deps:
                d.users.append(o)
        finish = {}
        free = {e: 0.0 for e in self.ENGS}
        cand = {e: [] for e in self.ENGS}
        for o in allops:
            if o.ndeps == 0:
                cand[o.eng].append(o)
        newstreams = {e: [] for e in self.ENGS}
        remaining = len(allops)
        WINDOW = SCHED_WINDOW
        oldest = 0
        scheduled = set()
        while remaining:
            best = None
            while oldest < len(allops) and allops[oldest].order in scheduled:
                oldest += 1
            lim = allops[oldest].order + WINDOW if oldest < len(allops) else 1 << 60
            for e in self.ENGS:
                c = cand[e]
                if not c:
                    continue
                bo, bk = None, None
                for o in c:
                    if o.order > lim:
                        continue
                    k = (max(free[e], o.ready), o.prio)
                    if bk is None or k < bk:
                        bo, bk = o, k
                if bo is not None and (best is None or bk < best[1]):
                    best = (bo, bk)
            o, (t0, _) = best
            e = o.eng
            cand[e].remove(o)
            o.start = t0
            t1 = t0 + o.cost
            free[e] = t1
            fin = t1 + o.lat + 0.2
            newstreams[e].append(o)
            scheduled.add(o.order)
            remaining -= 1
            for u in o.users:
                u.ndeps -= 1
                if fin > u.ready:
                    u.ready = fin
                if u.ndeps == 0:
                    cand[u.eng].append(u)
        self.streams = newstreams
        self.sim_time = max(free.values())

    def assign(self, sems, dma_sems):
        cnt = {e: 0 for e in self.ENGS}
        dcnt = {}
        for e in self.ENGS:
            for o in self.streams[e]:
                if o.dma is not None:
                    key, n = o.dma
                    dcnt[key] = dcnt.get(key, 0) + 16 * n
                    o.sem, o.val = dma_sems[key], dcnt[key]
                elif o.signal:
                    cnt[o.eng] += 1
                    o.sem, o.val = sems[o.eng], cnt[o.eng]
        self.dma_tot = dcnt

    def emit_stream(self, eng, e):
        seen = {}
        for o in self.streams[eng]:
            waits = {}
            for d in o.deps:
                if d.dma is None and d.eng == eng:
                    if eng == "pe" or not SAME_ENGINE_SYNC:
                        continue
                k = id(d.sem)
                if seen.get(k, -1) >= d.val:
                    continue
                if k not in waits or waits[k][1] < d.val:
                    waits[k] = (d.sem, d.val)
            wl = list(waits.values())
            for sem, val in wl:
                seen[id(sem)] = val
            hook = None
            if ATTACH_WAIT and wl and o.dma is None:
                s0, v0 = wl.pop()
                hook = (lambda inst, s0=s0, v0=v0: inst._wait_ge(s0, v0))
            for sem, val in wl:
                e.wait_ge(sem, val)
            res = o.fn(e, hook)
            if o.dma is not None:
                lst = res if isinstance(res, (list, tuple)) else [res]
                assert len(lst) == o.dma[1], (len(lst), o.dma)
                for i_ in lst:
                    i_.then_inc(o.sem, 16)
            elif o.signal:
                last = res[-1] if isinstance(res, (list, tuple)) else res
                last.then_inc(o.sem, 1)


def _geometry():
    j = np.arange(128)[:, None]
    i = np.arange(128)[None, :]
    slopes = (2.0 ** (-np.arange(1, 9))).astype(np.float32)
    m0 = (i >= 64) & (j < 64)
    m1 = (j >= 64) & (i < 64)
    d0 = (128 + i - j).astype(np.float32)
    d1 = np.abs(i - j).astype(np.float32)
    tabA = np.zeros((128, 2, 8, 128), np.float32)
    for h in range(8):
        tabA[:, 0, h, :] = np.where(m0, NEG, -slopes[h] * d0)
        tabA[:, 1, h, :] = np.where(m1, NEG, -slopes[h] * d1)
    idx0 = np.clip(128 + i - j, -128, 128) + 128
    idx1 = np.clip(i - j, -128, 128) + 128
    maskT = np.zeros((128, 8, 128), np.float32)
    maskT[:] = np.where(m1, NEG, 0.0)[:, None, :]
    return tabA.reshape(128, 2048), idx0, idx1, maskT.reshape(128, 1024)


def build(NT, stage=9):
    nc = bass.Bass("TRN2", target_bir_lowering=False)
    S = Sched()
    TPC = NT * 128
    RA_, RB_ = 3, 6
    NWR = 6
    NXB = 4

    def din(name, shape):
        return nc.dram_tensor(name, list(shape), F32, kind="ExternalInput").ap()

    def dout(name, shape):
        return nc.dram_tensor(name, list(shape), F32, kind="ExternalOutput").ap()

    xh = din("xh", [512, D]); xm = din("xm", [TPC, D]); pm = din("pm", [TPC, PLE])
    xs = din("xs", [128, D]); psd = din("psd", [128, PLE])
    ckw = din("ckw", [128, 128]); cvw = din("cvw", [128, 128])
    ckb = din("ckb", [512, 512]); cvb = din("cvb", [512, 512])
    w_in = din("w_in", [D, INC]); w_ow = din("w_ow", [512, D]); w_ob = din("w_ob", [512, D])
    w_out = din("w_out", [D, D]); w_pg = din("w_pg", [D, D]); w_pl = din("w_pl", [PLE, D])
    identd = din("ident", [128, 128]); tabAd = din("tabA", [128, 2048]); tabBd = din("tabB", [128, 2048])
    maskTd = din("maskT", [128, 1024]); cbd = din("cbrow", [1, 8]); hmaskd = din("hmask", [128, 4])
    gTind = din("gTin", [128, 8]); gTpled = din("gTple", [128, 8]); gcold = din("gcol", [128, 4])
    gkwbd = din("gkwb", [1, 64]); gkbbd = din("gkbb", [1, 64]); sinkd = din("sinkrow", [1, 8])
    wsc = nc.dram_tensor("wsc", [2 * D, D], BF16, kind="Internal").ap()

    y = dout("y", [TPC, D]); ys = dout("ys", [64, D])
    kwin = dout("kwin", [128, 128]); vwin = dout("vwin", [128, 128])
    kband = dout("kband", [512, 512]); vband = dout("vband", [512, 512])
    kws = dout("kws", [128, 128]); vws = dout("vws", [128, 128])
    kbs = dout("kbs", [512, 512]); vbs = dout("vbs", [512, 512])

    es = ExitStack()

    def sb(name, shape, dt):
        return es.enter_context(nc.sbuf_tensor(name, list(shape), dt))

    def pst(name, shape, dt):
        return es.enter_context(nc.psum_tensor(name, list(shape), dt))

    W_in = sb("W_in", [128, 8, INC], BF16)
    W_ow = sb("W_ow", [128, 4, D], BF16); W_ob = sb("W_ob", [128, 4, D], BF16)
    W_pl = sb("W_pl", [128, 2, D], BF16)
    wring = sb("wring", [128, NWR, D], BF16)
    ident = sb("identb", [128, 128], BF16)
    tabA = sb("tabAs", [128, 2048], BF16); tabB = sb("tabBs", [128, 2048], BF16)
    gTin = sb("gTins", [128, 8], F32); gTple = sb("gTples", [128, 8], F32); gcol = sb("gcols", [128, 4], F32)
    gkwb = sb("gkwbs", [128, 64], F32); gkbb = sb("gkbbs", [128, 64], F32)
    esink = sb("esink", [128, 8], F32); hmask = sb("hmasks", [128, 4], F32); cb = sb("cbs", [128, 8], F32)
    kaT = sb("kaT", [128, RA_, 128], BF16); kbT = sb("kbT", [128, RB_, 512], BF16)
    va = sb("va", [128, RA_, 2, 65], BF16); vb = sb("vb", [128, RB_, 8, 65], BF16)
    xb = [sb(f"xb{i}", [128, D], F32) for i in range(NXB)]
    pb = [sb(f"pb{i}", [128, PLE], F32) for i in range(2)]
    xn = sb("xn", [128, D], BF16); xnT = sb("xnT", [128, D], BF16)
    ybufs = [sb(f"ybuf{i}", [128, D], BF16) for i in range(2)]
    mbuf = sb("mbuf", [128, D], BF16); actT = sb("actT", [128, D], BF16)
    pbf = sb("pbf", [128, PLE], BF16); pT = sb("pT", [128, PLE], BF16)
    qkn = sb("qkn", [128, 1664], BF16)
    qaT = [sb(f"qaT{i}", [128, 512], BF16) for i in range(2)]
    qbT = [sb(f"qbT{i}", [128, 512], BF16) for i in range(2)]
    zs = [sb(f"zs{i}", [128, D], BF16) for i in range(2)]
    tg = [sb(f"tg{i}", [128, 2048], BF16) for i in range(2)]
    ET = [sb(f"ET{i}", [128, 512], BF16) for i in range(3)]
    sqt = sb("sqt", [128, 512], F32); tzt = sb("tzt", [128, 512], F32)
    kvft = sb("kvft", [128, 1280], F32); t1t = sb("t1t", [128, D], F32); ont = sb("ont", [128, 256], F32)
    st = sb("st", [128, 64], F32)
    stq = sb("stq", [128, 16], F32)
    mhalf = sb("mhalf", [128, 8], F32)

    NMM = 3
    ps_mm = [pst(f"psmm{i}", [128, 512], F32) for i in range(NMM)]
    ps_s = [pst(f"pss{i}", [128, 512], F32) for i in range(2)]
    ps_o1 = pst("pso0", [128, 512], F32)
    ps_tl = [pst(f"pstl{i}", [128, 512], F32) for i in range(2)]

    sq = sqt[:]; tz = tzt[:]; kvf = kvft[:]; t1 = t1t[:]; on = ont[:]

    def bc3(ap2, n):
        k = ap2.shape[1]
        return ap2.unsqueeze(2).to_broadcast([128, k, n])

    def v3(ap2, inner):
        return ap2.rearrange("p (a b) -> p a b", b=inner)

    S.ins("sp", "dma_start", out=xb[0][:], in_=xh[0:128, :], w=["xb0"], dma=("ld0", 1))
    S.ins("sp", "dma_start", out=xb[1][:], in_=xh[128:256, :], w=["xb1"], dma=("ld1", 1))
    S.ins("pool", "dma_start", out=ident[:], in_=identd, w=["ident"], dma=("c_ident", 1))
    for nm, dst, src in (("gTin", gTin, gTind), ("gTple", gTple, gTpled), ("gcol", gcol, gcold),
                         ("hmask", hmask, hmaskd)):
        S.ins("sp", "dma_start", out=dst[:], in_=src, w=[nm], dma=("c_" + nm, 1))
    S.ins("sp", "dma_start", out=gkwb[:], in_=gkwbd.to_broadcast([128, 64]), w=["gkwb"], dma=("c_gkwb", 1))
    S.ins("sp", "dma_start", out=gkbb[:], in_=gkbbd.to_broadcast([128, 64]), w=["gkbb"], dma=("c_gkbb", 1))
    S.ins("sp", "dma_start", out=esink[:], in_=sinkd.to_broadcast([128, 8]), w=["esink"], dma=("c_esink", 1))
    S.ins("sp", "dma_start", out=cb[:], in_=cbd.to_broadcast([128, 8]), w=["cb"], dma=("c_cb", 1))
    S.ins("sp", "dma_start", out=kvf[:, 0:1024], in_=tabBd[:, 0:1024], w=["kvf"], dma=("c_tb0", 1))
    S.ins("sp", "dma_start", out=t1[:, 0:1024], in_=tabBd[:, 1024:2048], w=["t1"], dma=("c_tb1", 1))
    mstage = tg[0][:].bitcast(F32)
    MST = [f"tg0_{i}" for i in range(4)]
    S.ins("sp", "dma_start", out=mstage, in_=maskTd, w=MST, dma=("c_mt", 1))
    w_in3 = w_in.rearrange("(c p) n -> p c n", p=128)
    WRANGES = [(512, 768), (1792, 2816), (0, 512), (768, 1792), (2816, 4096), (4096, INC)]
    cumb = [0.0]

    def wlat(nbytes):
        cumb[0] += nbytes
        return 3.0 + cumb[0] / 250e3
    for ri, (c0_, c1_) in enumerate(WRANGES):
        S.ins("pool", "dma_start", out=W_in[:, :, c0_:c1_], in_=w_in3[:, :, c0_:c1_],
              w=[f"W_in_r{ri}"], dma=(f"w_in_r{ri}", 1), lat_override=wlat(D * (c1_ - c0_) * 4))

    S.ins("pool", "dma_start", out=tabA[:], in_=tabAd, w=["tabA"], dma=("c_tabA", 1))

    def win_res(col0, ncols):
        return [f"W_in_r{ri}" for ri, (a_, b_) in enumerate(WRANGES) if a_ < col0 + ncols and col0 < b_]
    for nm, dst, src, kc in (("W_ow", W_ow, w_ow, 4), ("W_ob", W_ob, w_ob, 4), ("W_pl", W_pl, w_pl, 2)):
        for c in range(kc):
            S.ins("pool", "dma_start", out=dst[:, c, :], in_=src[c * 128:(c + 1) * 128, :],
                  w=[f"{nm}{c}"], dma=(f"w_{nm}{c}", 1), lat_override=wlat(128 * D * 4))
    S.ins("pool", "dma_start", out=wsc[0:D, :], in_=w_out, w=["wsc0"], dma=("w2a", 1), lat_override=wlat(D * D * 4))
    S.ins("pool", "dma_start", out=wsc[D:2 * D, :], in_=w_pg, w=["wsc1"], dma=("w2b", 1), lat_override=wlat(D * D * 4))
    S.ins("dve", "tensor_scalar", out=gTin[:], in0=gTin[:], scalar1=32.0, scalar2=None, op0=ALU.mult,
          r=["gTin"], w=["gTin"])
    S.ins("dve", "tensor_scalar", out=gTple[:], in0=gTple[:], scalar1=32.0, scalar2=None, op0=ALU.mult,
          r=["gTple"], w=["gTple"])
    for col in (1, 3):
        S.ins("dve", "tensor_scalar", out=gcol[:, col:col + 1], in0=gcol[:, col:col + 1], scalar1=8.0,
              scalar2=None, op0=ALU.mult, r=["gcol"], w=["gcol"])
    S.ins("dve", "tensor_scalar", out=gkwb[:], in0=gkwb[:], scalar1=8.0, scalar2=None, op0=ALU.mult,
          r=["gkwb"], w=["gkwb"])
    S.ins("dve", "tensor_scalar", out=gkbb[:], in0=gkbb[:], scalar1=8.0, scalar2=None, op0=ALU.mult,
          r=["gkbb"], w=["gkbb"])
    S.ins("pool", "memset", mhalf[:], -0.5, w=["mhalf"])
    S.ins("pool", "memset", va[:], 1.0, w=[f"va{i}" for i in range(RA_)])
    S.ins("pool", "memset", vb[:], 1.0, w=[f"vb{i}" for i in range(RB_)])

    tiles = []
    for i in range(4):
        tiles.append(dict(kind="halo", vt=i - 4, x=xh[i * 128:(i + 1) * 128, :], p=None))
    for n in range(NT):
        tiles.append(dict(kind="main", vt=n, n=n, x=xm[n * 128:(n + 1) * 128, :], p=pm[n * 128:(n + 1) * 128, :]))
    SB = NT + 8
    tiles.append(dict(kind="sample", vt=SB + 4, x=xs, p=psd))
    for li, t in enumerate(tiles):
        t["li"] = li

    def issue_load(t):
        bi = t["li"] % NXB
        S.ins("sp", "dma_start", out=xb[bi][:], in_=t["x"], w=[f"xb{bi}"], dma=(f"ld{bi}", 1))

    def issue_pload(t):
        pi = t["li"] % 2
        S.ins("sp", "dma_start", out=pb[pi][:], in_=t["p"], w=[f"pb{pi}"], dma=(f"ldp{pi}", 1))

    mmc = [0]

    def next_mm():
        b = mmc[0] % NMM
        mmc[0] += 1
        return ps_mm[b], f"psmm{b}"

    def next_tr():
        bank, res = next_mm()
        return bank[:].bitcast(BF16), res

    def inproj_ct(col0, ncols):
        bank, bres_ = next_mm()
        b = int(bres_[-1])
        for g0 in range(0, 8, INPROJ_SPLIT):
            calls = []
            for c in range(g0, g0 + INPROJ_SPLIT):
                calls.append(("matmul", (bank[:, 0:ncols], xnT[:, c * 128:(c + 1) * 128], W_in[:, c, col0:col0 + ncols]),
                              dict(start=(c == 0), stop=(c == 7))))
            S.group("pe", calls, r=["xnT_a" if g0 < 4 else "xnT_b"] + win_res(col0, ncols), w=[f"psmm{b}"])
        return bank, f"psmm{b}"

    def qknorm(bank, bres, c0, nh, dst, dst_res, sscol, kvf_ap=None, gtab=None, permute=False):
        w = 64 * nh
        S.ins("act", "activation", out=sq[:, 0:w], in_=bank[:, c0:c0 + w], func=AF.Square, r=[bres], w=["sq"])
        S.ins("dve", "tensor_reduce", out=st[:, sscol:sscol + nh], in_=v3(sq[:, 0:w], 64), axis=AX.X, op=ALU.add,
              r=["sq"], w=["st_ss"])
        S.ins("dve", "tensor_scalar", out=st[:, sscol:sscol + nh], in0=st[:, sscol:sscol + nh],
              scalar1=64 * EPS, scalar2=None, op0=ALU.add, r=["st_ss"], w=["st_ss"])
        S.ins("pool", "tensor_tensor", out=st[:, 32 + sscol:32 + sscol + nh], in0=st[:, sscol:sscol + nh],
              in1=mhalf[:, 0:nh], op=ALU.pow, r=["st_ss", "mhalf"], w=["st_rs"])
        rsb = bc3(st[:, 32 + sscol:32 + sscol + nh], 64)
        if permute:
            o_ap = dst.rearrange("p (i kv d) -> p kv i d", kv=2, d=64)
            i_ap = bank[:, c0:c0 + w].rearrange("p (kv i d) -> p kv i d", kv=2, d=64)
            r_ap = st[:, 32 + sscol:32 + sscol + nh].rearrange("p (kv i) -> p kv i", kv=2).unsqueeze(3).to_broadcast(
                [128, 2, 4, 64])
            S.ins("dve", "tensor_tensor", out=o_ap, in0=i_ap, in1=r_ap, op=ALU.mult, r=[bres, "st_rs"], w=[dst_res])
        else:
            S.ins("dve", "tensor_tensor", out=v3(dst, 64), in0=v3(bank[:, c0:c0 + w], 64), in1=rsb, op=ALU.mult,
                  r=[bres, "st_rs"], w=[dst_res])
        if kvf_ap is not None:
            S.ins("dve", "tensor_tensor", out=v3(kvf_ap, 64), in0=v3(bank[:, c0:c0 + w], 64), in1=rsb, op=ALU.mult,
                  r=[bres, "st_rs"], w=["kvf"])
            S.ins("pool", "tensor_tensor", out=v3(kvf_ap, 64), in0=v3(kvf_ap, 64),
                  in1=gtab.unsqueeze(1).to_broadcast([128, nh, 64]), op=ALU.mult, r=["kvf"], w=["kvf"])

    def stage_P(t):
        kind = t["kind"]
        vt = t["vt"]
        X = xb[t["li"] % NXB]
        xres = f"xb{t['li'] % NXB}"
        par = t["li"] % 2
        kvout = (kind == "sample") or (kind == "main" and t["n"] >= NT - 4)
        sa, sbn = vt % RA_, vt % RB_
        S.boost = P_BOOST
        S.ins("act", "activation", out=xn[:], in_=X[:], func=AF.Square, accum_out=st[:, 60:61], r=[xres],
              w=["xn", "st_x"])
        S.ins("dve", "tensor_scalar", out=st[:, 60:61], in0=st[:, 60:61], scalar1=1024 * EPS, scalar2=None,
              op0=ALU.add, r=["st_x"], w=["st_x"])
        S.ins("pool", "tensor_tensor", out=st[:, 61:62], in0=st[:, 60:61], in1=mhalf[:, 0:1], op=ALU.pow,
              r=["st_x", "mhalf"], w=["st_rx"])
        S.ins("pool", "tensor_tensor", out=xn[:], in0=X[:], in1=st[:, 61:62].to_broadcast([128, D]), op=ALU.mult,
              r=[xres, "st_rx"], w=["xn"])
        trx, trres = next_tr()
        S.group("pe", [("transpose", (trx[:, c * 128:(c + 1) * 128], xn[:, c * 128:(c + 1) * 128], ident[:]), {})
                       for c in range(8)], r=["xn", "ident"], w=[trres])
        for hf in range(2):
            S.ins("dve", "tensor_tensor", out=v3(xnT[:, hf * 512:(hf + 1) * 512], 128),
                  in0=v3(trx[:, hf * 512:(hf + 1) * 512], 128), in1=bc3(gTin[:, 4 * hf:4 * hf + 4], 128),
                  op=ALU.mult, r=[trres, "gTin"], w=["xnT_a" if hf == 0 else "xnT_b"])
        S.boost = 0
        yield
        if kind != "halo":
            bank, br = inproj_ct(0, 512)
            qknorm(bank, br, 0, 8, qkn[:, 0:512], "qkn_qa", 0, permute=True)
            yield
        bank, br = inproj_ct(512, 256)
        qknorm(bank, br, 0, 2, qkn[:, 1024:1152], "qkn_ka", 8,
               kvf_ap=(kvf[:, 0:128] if kvout else None), gtab=gkwb[:])
        S.ins("act", "copy", va[:, sa, :, 0:64], v3(bank[:, 128:256], 64), r=[br], w=[f"va{sa}"])
        if kvout:
            S.ins("dve", "tensor_copy", out=kvf[:, 128:256], in_=bank[:, 128:256], r=[br], w=["kvf"])
        yield
        if kind != "halo":
            bank, br = inproj_ct(768, 512)
            S.ins("act", "activation", out=tz, in_=bank[:], func=AF.Tanh, scale=0.5, r=[br], w=["tz"])
            S.ins("dve", "scalar_tensor_tensor", out=zs[par][:, 0:512], in0=tz, scalar=1.0, in1=bank[:], op0=ALU.add,
                  op1=ALU.mult, r=["tz", br], w=[f"zs{par}a"])
            yield
            bank, br = inproj_ct(1280, 512)
            qknorm(bank, br, 0, 8, qkn[:, 512:1024], "qkn_qb", 10)
            yield
        bank, br = inproj_ct(1792, 512)
        qknorm(bank, br, 0, 8, qkn[:, 1152:1664], "qkn_kb", 18,
               kvf_ap=(kvf[:, 256:768] if kvout else None), gtab=gkbb[:])
        yield
        bank, br = inproj_ct(2304, 512)
        S.ins("act", "copy", vb[:, sbn, :, 0:64], v3(bank[:], 64), r=[br], w=[f"vb{sbn}"])
        if kvout:
            S.ins("dve", "tensor_copy", out=kvf[:, 768:1280], in_=bank[:], r=[br], w=["kvf"])
        yield
        if kind != "halo":
            bank, br = inproj_ct(2816, 512)
            S.ins("act", "activation", out=tz, in_=bank[:], func=AF.Tanh, scale=0.5, r=[br], w=["tz"])
            S.ins("dve", "scalar_tensor_tensor", out=zs[par][:, 512:1024], in0=tz, scalar=1.0, in1=bank[:],
                  op0=ALU.add, op1=ALU.mult, r=["tz", br], w=[f"zs{par}b"])
            yield
            for gi in range(4):
                bank, br = inproj_ct(3328 + gi * 512, 512)
                S.ins("act", "activation", out=tg[par][:, gi * 512:(gi + 1) * 512], in_=bank[:], func=AF.Tanh,
                      scale=0.5, r=[br], w=[f"tg{par}_{gi}"])
                yield
        if kvout:
            if kind == "main":
                r0 = (t["n"] - (NT - 4)) * 128
                S.ins("sp", "dma_start", out=kband[r0:r0 + 128, :], in_=kvf[:, 256:768], r=["kvf"], dma=("kv", 1))
                S.ins("sp", "dma_start", out=vband[r0:r0 + 128, :], in_=kvf[:, 768:1280], r=["kvf"], dma=("kv", 1))
                if t["n"] == NT - 1:
                    S.ins("sp", "dma_start", out=kwin[:, :], in_=kvf[:, 0:128], r=["kvf"], dma=("kv", 1))
                    S.ins("sp", "dma_start", out=vwin[:, :], in_=kvf[:, 128:256], r=["kvf"], dma=("kv", 1))
            else:
                S.ins("sp", "dma_start", out=kbs[448:512, :], in_=kvf[0:64, 256:768], r=["kvf"], dma=("kv", 1))
                S.ins("sp", "dma_start", out=vbs[448:512, :], in_=kvf[0:64, 768:1280], r=["kvf"], dma=("kv", 1))
                S.ins("sp", "dma_start", out=kws[64:128, :], in_=kvf[0:64, 0:128], r=["kvf"], dma=("kv", 1))
                S.ins("sp", "dma_start", out=vws[64:128, :], in_=kvf[0:64, 128:256], r=["kvf"], dma=("kv", 1))
        if kind != "halo":
            trx, trres = next_tr()
            S.group("pe", [("transpose", (trx[:, c * 128:(c + 1) * 128], qkn[:, c * 128:(c + 1) * 128], ident[:]), {})
                           for c in range(8)], r=["qkn_qa", "qkn_qb", "ident"], w=[trres])
            S.ins("act", "mul", qaT[par][:], trx[:, 0:512], gcol[:, 0:1], r=[trres, "gcol"], w=[f"qaT{par}"])
            S.ins("act", "mul", qbT[par][:], trx[:, 512:1024], gcol[:, 2:3], r=[trres, "gcol"], w=[f"qbT{par}"])
            yield
        trx, trres = next_tr()
        S.group("pe", [("transpose", (trx[:, c * 128:(c + 1) * 128], qkn[:, 1024 + c * 128:1152 + c * 128], ident[:]), {})
                       for c in range(5)], r=["qkn_ka", "qkn_kb", "ident"], w=[trres])
        S.ins("act", "mul", kaT[:, sa, :], trx[:, 0:128], gcol[:, 1:2], r=[trres, "gcol"], w=[f"kaT{sa}"])
        S.ins("act", "mul", kbT[:, sbn, :], trx[:, 128:640], gcol[:, 3:4], r=[trres, "gcol"], w=[f"kbT{sbn}"])
        yield

    wrc = [0]

    def stage_A(t):
        kind = t["kind"]
        vt = t["vt"]
        par = t["li"] % 2
        ybuf = ybufs[par]
        yres = f"ybuf{par}"
        uc = [0]
        obank = ps_o1
        ores = "pso0"
        for mixer in ("A", "B"):
            kts = [vt - 1, vt] if mixer == "A" else [vt - 4, vt - 3, vt - 2, vt - 1, vt]
            for hh in range(2):
                for n_, kt in enumerate(kts):
                    u = uc[0]
                    uc[0] += 1
                    sbk = ps_s[u % 2]
                    sres = f"pss{u % 2}"
                    e_i = u % 3
                    E = ET[e_i]
                    eres = f"ET{e_i}"
                    rel = kt - vt
                    tab = None
                    if mixer == "A":
                        sl = kt % RA_
                        S.ins("pe", "matmul", sbk[:, 0:512], kaT[hh * 64:(hh + 1) * 64, sl, :],
                              qaT[par][hh * 64:(hh + 1) * 64, :], start=True, stop=True,
                              r=[f"kaT{sl}", f"qaT{par}"], w=[sres])
                        tab = v3(tabA[:, (rel + 1) * 1024 + hh * 512:(rel + 1) * 1024 + (hh + 1) * 512], 128)
                        tres = "tabA"
                    else:
                        sl = kt % RB_
                        calls = []
                        for hq in range(4):
                            pr, half = hq, hh
                            calls.append(("matmul", (sbk[:, hq * 128:(hq + 1) * 128],
                                                     kbT[half * 64:(half + 1) * 64, sl, pr * 128:(pr + 1) * 128],
                                                     qbT[par][half * 64:(half + 1) * 64, pr * 128:(pr + 1) * 128]),
                                          dict(start=True, stop=True)))
                        S.group("pe", calls, r=[f"kbT{sl}", f"qbT{par}"], w=[sres])
                        if rel >= -1:
                            tab = tabB[:, (rel + 1) * 1024:(rel + 2) * 1024].rearrange(
                                "p (hq two i) -> p two hq i", two=2, i=128)[:, hh]
                            tres = "tabB"
                    if tab is not None:
                        S.ins("dve", "tensor_tensor", out=v3(sbk[:], 128), in0=v3(sbk[:], 128), in1=tab, op=ALU.add,
                              r=[sres, tres], w=[sres])
                    if kind == "main" and kt < 0:
                        S.ins("act", "activation", out=E[:], in_=sbk[:], func=AF.Exp, bias=hmask[:, kt + 4:kt + 5],
                              r=[sres, "hmask"], w=[eres])
                    else:
                        S.ins("act", "activation", out=E[:], in_=sbk[:], func=AF.Exp, r=[sres], w=[eres])
                    if mixer == "B" and rel == -4:
                        S.ins("pool", "memset", v3(E[0:64, :], 128)[:, :, 64:128], 0.0, r=[eres], w=[eres])
                    calls = []
                    for hq in range(4):
                        if mixer == "A":
                            rhs = va[:, sl, hh, 0:65]
                        else:
                            rhs = vb[:, sl, 2 * hq + hh, 0:65]
                        calls.append(("matmul", (obank[:, hq * 65:(hq + 1) * 65], E[:, hq * 128:(hq + 1) * 128], rhs),
                                      dict(start=(n_ == 0 and hq == 0), stop=(n_ == len(kts) - 1 and hq == 3),
                                           skip_group_check=True)))
                    S.group("pe", calls, r=[eres, (f"va{sl}" if mixer == "A" else f"vb{sl}")], w=[ores])
                    yield
                o3 = obank[:, 0:260].rearrange("p (h e) -> p h e", e=65)
                if mixer == "A":
                    S.ins("dve", "tensor_tensor", out=stq[:, 4:8], in0=o3[:, :, 64], in1=esink[:, 4 * hh:4 * hh + 4],
                          op=ALU.add, r=[ores, "esink"], w=["st_den"])
                    S.ins("dve", "reciprocal", out=stq[:, 0:4], in_=stq[:, 4:8], r=["st_den"], w=["st_rd"])
                else:
                    S.ins("dve", "reciprocal", out=stq[:, 0:4], in_=o3[:, :, 64], r=[ores], w=["st_rd"])
                S.ins("dve", "tensor_tensor", out=v3(on, 64), in0=o3[:, :, 0:64], in1=bc3(stq[:, 0:4], 64), op=ALU.mult,
                      r=[ores, "st_rd"], w=["on"])
                if mixer == "A":
                    yc = hh * 256
                    y_ap, z_ap = v3(ybuf[:, yc:yc + 256], 64), v3(zs[par][:, yc:yc + 256], 64)
                else:
                    y_ap = ybuf[:, 512:1024].rearrange("p (hq two d) -> p two hq d", two=2, d=64)[:, hh]
                    z_ap = zs[par][:, 512:1024].rearrange("p (hq two d) -> p two hq d", two=2, d=64)[:, hh]
                S.ins("pool", "tensor_tensor", out=y_ap, in0=v3(on, 64), in1=z_ap, op=ALU.mult,
                      r=["on", f"zs{par}a" if mixer == "A" else f"zs{par}b"], w=[yres])
                yield
        pi = t["li"] % 2
        S.ins("pool", "tensor_copy", out=pbf[:], in_=pb[pi][:], r=[f"pb{pi}"], w=["pbf"])
        trp, trpres = ps_o1[:].bitcast(BF16), "pso0"
        S.group("pe", [("transpose", (trp[:, c * 128:(c + 1) * 128], pbf[:, c * 128:(c + 1) * 128], ident[:]), {})
                       for c in range(2)], r=["pbf", "ident"], w=[trpres])
        S.ins("act", "copy", pT[:], trp[:, 0:256], r=[trpres], w=["pT"])
        yield

    def stage_Z(t):
        kind = t["kind"]
        X = xb[t["li"] % NXB]
        xres = f"xb{t['li'] % NXB}"
        par = t["li"] % 2
        pi = t["li"] % 2
        ybuf = ybufs[par]
        yres = f"ybuf{par}"
        TL = ["pstl0", "pstl1"]
        trz = ps_tl[0][:].bitcast(BF16)
        wslots = []
        for c in range(16):
            wslots.append(wrc[0] % NWR)
            wrc[0] += 1

        def wload(c):
            S.ins("sp", "dma_start", out=wring[:, wslots[c], :], in_=wsc[c * 128:(c + 1) * 128, :],
                  r=["wsc0" if c < 8 else "wsc1"], w=[f"wr{wslots[c]}"], dma=(f"wr{wslots[c]}", 1))
        for c in range(NWR):
            wload(c)
        wnext = [NWR]

        trzs = [ps_tl[0][:].bitcast(BF16), ps_tl[1][:].bitcast(BF16)]

        def transposes8(src, sres_, banks=None):
            sres_l = sres_ if isinstance(sres_, (list, tuple)) else [sres_, sres_]
            if banks is None:
                banks = [(trzs[0], TL[0]), (trzs[1], TL[1])]
            for hf in range(2):
                tb, tres_ = banks[hf]
                S.group("pe", [("transpose", (tb[:, c * 128:(c + 1) * 128],
                                              src[:, (4 * hf + c) * 128:(4 * hf + c + 1) * 128], ident[:]), {})
                               for c in range(4)], r=[sres_l[hf], "ident"], w=[tres_])
            return banks

        def evac_plain(banks):
            S.ins("act", "copy", actT[:, 0:512], banks[0][0][:, 0:512], r=[banks[0][1]], w=["actT_a"])
            S.ins("dve", "tensor_copy", out=actT[:, 512:1024], in_=banks[1][0][:, 0:512], r=[banks[1][1]], w=["actT_b"])

        def dense(Wt, wres, k0, kc, src, sres_):
            for nb in range(2):
                calls = []
                for c in range(kc):
                    calls.append(("matmul", (ps_tl[nb][:], src[:, (k0 + c) * 128:(k0 + c + 1) * 128],
                                             Wt[:, c, nb * 512:(nb + 1) * 512]),
                                  dict(start=(c == 0), stop=(c == kc - 1))))
                S.group("pe", calls, r=[sres_] + [f"{wres}{c}" for c in range(kc)], w=[TL[nb]])

        def dense_stream(c0):
            for c in range(8):
                slot = wslots[c0 + c]
                calls = []
                for nb in range(2):
                    calls.append(("matmul", (ps_tl[nb][:], actT[:, c * 128:(c + 1) * 128],
                                             wring[:, slot, nb * 512:(nb + 1) * 512]),
                                  dict(start=(c == 0), stop=(c == 7))))
                S.group("pe", calls, r=["actT_a" if c < 4 else "actT_b", f"wr{slot}"], w=[TL[0], TL[1]])
                if wnext[0] < 16:
                    wload(wnext[0])
                    wnext[0] += 1

        evac_plain(transposes8(ybuf, yres))
        yield
        dense(W_ow, "W_ow", 0, 4, actT, "actT_a")
        for nb in range(2):
            S.ins("dve", "scalar_tensor_tensor", out=t1[:, nb * 512:(nb + 1) * 512],
                  in0=tg[par][:, nb * 512:(nb + 1) * 512], scalar=1.0, in1=ps_tl[nb][:], op0=ALU.add, op1=ALU.mult,
                  r=[f"tg{par}_{nb}", TL[nb]], w=["t1"])
        yield
        dense(W_ob, "W_ob", 4, 4, actT, "actT_b")
        for nb in range(2):
            S.ins("dve", "scalar_tensor_tensor", out=ps_tl[nb][:],
                  in0=tg[par][:, 1024 + nb * 512:1024 + (nb + 1) * 512],
                  scalar=1.0, in1=ps_tl[nb][:], op0=ALU.add, op1=ALU.mult, r=[f"tg{par}_{2 + nb}", TL[nb]],
                  w=[TL[nb]])
            S.ins("dve", "tensor_tensor", out=mbuf[:, nb * 512:(nb + 1) * 512], in0=ps_tl[nb][:],
                  in1=t1[:, nb * 512:(nb + 1) * 512], op=ALU.add, r=[TL[nb], "t1"], w=[f"mbuf_{nb}"])
        evac_plain(transposes8(mbuf, ["mbuf_0", "mbuf_1"]))
        yield
        dense_stream(0)
        yield
        for nb in range(2):
            S.ins("dve", "scalar_tensor_tensor", out=X[:, nb * 512:(nb + 1) * 512], in0=ps_tl[nb][:], scalar=0.25,
                  in1=X[:, nb * 512:(nb + 1) * 512], op0=ALU.mult, op1=ALU.add, r=[TL[nb], xres], w=[xres])
        S.ins("act", "copy", mbuf[:, 0:512], X[:, 0:512], r=[xres], w=["mbuf_0"])
        S.ins("dve", "tensor_copy", out=mbuf[:, 512:1024], in_=X[:, 512:1024], r=[xres], w=["mbuf_1"])
        S.ins("act", "activation", out=ybuf[:], in_=X[:], func=AF.Square, accum_out=stq[:, 8:9], r=[xres],
              w=[yres, "st_h"])
        S.ins("dve", "tensor_scalar", out=stq[:, 8:9], in0=stq[:, 8:9], scalar1=1024 * EPS, scalar2=4.0,
              op0=ALU.add, op1=ALU.mult, r=["st_h"], w=["st_h"])
        S.ins("pool", "tensor_tensor", out=stq[:, 9:10], in0=stq[:, 8:9], in1=mhalf[:, 0:1], op=ALU.pow,
              r=["st_h", "mhalf"], w=["st_rh"])
        bk = transposes8(mbuf, ["mbuf_0", "mbuf_1"])
        for hf in range(2):
            S.ins("dve", "tensor_tensor", out=v3(actT[:, hf * 512:(hf + 1) * 512], 128), in0=v3(bk[hf][0][:, 0:512], 128),
                  in1=bc3(gTple[:, 4 * hf:4 * hf + 4], 128), op=ALU.mult, r=[bk[hf][1], "gTple"],
                  w=["actT_a" if hf == 0 else "actT_b"])
        yield
        dense_stream(8)
        yield
        for nb in range(2):
            S.ins("act", "activation", out=t1[:, nb * 512:(nb + 1) * 512], in_=ps_tl[nb][:], func=AF.Tanh,
                  scale=stq[:, 9:10], r=[TL[nb], "st_rh"], w=["t1"])
        dense(W_pl, "W_pl", 0, 2, pT, "pT")
        yield
        for nb in range(2):
            S.ins("dve", "scalar_tensor_tensor", out=ps_tl[nb][:], in0=t1[:, nb * 512:(nb + 1) * 512],
                  scalar=1.0, in1=ps_tl[nb][:], op0=ALU.add, op1=ALU.mult, r=["t1", TL[nb]], w=[TL[nb]])
            S.ins("dve", "scalar_tensor_tensor", out=X[:, nb * 512:(nb + 1) * 512], in0=ps_tl[nb][:], scalar=0.5,
                  in1=X[:, nb * 512:(nb + 1) * 512], op0=ALU.mult, op1=ALU.add, r=[TL[nb], xres], w=[xres])
        if kind == "main":
            n = t["n"]
            S.ins("sp", "dma_start", out=y[n * 128:(n + 1) * 128, :], in_=X[:], r=[xres], dma=("st_" + xres, 1))
        else:
            S.ins("sp", "dma_start", out=ys[:, :], in_=X[0:64, :], r=[xres], dma=("st_" + xres, 1))
        yield

    def stage_C():
        regions = [(0, "qkn_qa"), (512, "qkn_qb"), (1152, "qkn_kb"), (0, "qkn_qa")]
        for i in range(4):
            sl = (SB + i) % RB_
            off, rres = regions[i]
            S.ins("pool", "dma_start", out=qkn[:, off:off + 512], in_=ckb[i * 128:(i + 1) * 128, :], w=[rres],
                  dma=(f"ccK{i % 3}", 1))
            trx, trres = next_tr()
            S.group("pe", [("transpose", (trx[:, c * 128:(c + 1) * 128], qkn[:, off + c * 128:off + (c + 1) * 128], ident[:]), {})
                           for c in range(4)], r=[rres, "ident"], w=[trres])
            S.ins("act", "copy", kbT[:, sl, :], trx[:, 0:512], r=[trres], w=[f"kbT{sl}"])
            S.ins("pool", "dma_start", out=vb[:, sl, :, 0:64], in_=v3(cvb[i * 128:(i + 1) * 128, :], 64),
                  w=[f"vb{sl}"], dma=(f"ccV{i}", 1))
            yield
        sl = (SB + 3) % RA_
        S.ins("pool", "dma_start", out=qkn[:, 1024:1152], in_=ckw[:, :], w=["qkn_ka"], dma=("ccKw", 1))
        trx, trres = next_tr()
        S.ins("pe", "transpose", trx[:, 0:128], qkn[:, 1024:1152], ident[:], r=["qkn_ka", "ident"], w=[trres])
        S.ins("act", "copy", kaT[:, sl, :], trx[:, 0:128], r=[trres], w=[f"kaT{sl}"])
        S.ins("pool", "dma_start", out=va[:, sl, :, 0:64], in_=v3(cvw[:, :], 64), w=[f"va{sl}"], dma=("ccVw", 1))
        S.ins("sp", "dma_start", out=kws[0:64, :], in_=ckw[64:128, :], dma=("kvc", 1))
        S.ins("sp", "dma_start", out=vws[0:64, :], in_=cvw[64:128, :], dma=("kvc", 1))
        S.ins("sp", "dma_start", out=kbs[0:448, :], in_=ckb[64:512, :], dma=("kvc", 1))
        S.ins("sp", "dma_start", out=vbs[0:448, :], in_=cvb[64:512, :], dma=("kvc", 1))
        yield

    def late_setup():
        S.ins("act", "activation", out=esink[:], in_=esink[:], func=AF.Exp, r=["esink"], w=["esink"])
        S.ins("dve", "tensor_tensor", out=v3(tabB[:, 0:1024], 128), in0=v3(kvf[:, 0:1024], 128),
              in1=bc3(cb[:, 0:8], 128), op=ALU.subtract, r=["kvf", "cb"], w=["tabB"])
        S.ins("dve", "tensor_tensor", out=v3(t1[:, 0:1024], 128), in0=v3(t1[:, 0:1024], 128),
              in1=bc3(cb[:, 0:8], 128), op=ALU.subtract, r=["t1", "cb"], w=["t1"])
        S.ins("dve", "tensor_tensor", out=tabB[:, 1024:2048], in0=t1[:, 0:1024], in1=mstage, op=ALU.add,
              r=["t1"] + MST, w=["tabB"])

    def run(gen):
        for _ in gen:
            pass

    halos = [t for t in tiles if t["kind"] == "halo"]
    mains = [t for t in tiles if t["kind"] == "main"]
    samp = [t for t in tiles if t["kind"] == "sample"][0]
    seq = []
    if stage >= 1:
        seq += [("P", t) for t in halos]
    if stage >= 2:
        for t in mains:
            seq += [("P", t), ("A", t), ("Z", t)]
    if stage >= 3:
        seq += [("P", samp), ("C", None), ("A", samp), ("Z", samp)]
    porder = [t for k, t in seq if k == "P"]
    nextload = [0]

    def load_upto(k):
        while nextload[0] < len(porder) and nextload[0] <= k:
            issue_load(porder[nextload[0]])
            nextload[0] += 1

    pidx = 0
    nextload[0] = 2
    late_done = False
    for k, t in seq:
        if not late_done and not (k == "P" and t["kind"] == "halo"):
            late_setup()
            late_done = True
        S.cur = (k, t["li"] if t is not None else -1)
        if k == "P":
            load_upto(pidx + 2)
            pidx += 1
            run(stage_P(t))
        elif k == "A":
            issue_pload(t)
            run(stage_A(t))
        elif k == "Z":
            run(stage_Z(t))
        else:
            run(stage_C())

    if not late_done:
        late_setup()
    dma_keys = sorted({o.dma[0] for s_ in S.streams.values() for o in s_ if o.dma is not None})
    sems = {e: es.enter_context(nc.semaphore(f"s_{e}")) for e in Sched.ENGS}
    dsems = {k: es.enter_context(nc.semaphore(f"d_{k}")) for k in dma_keys}
    if RESCHEDULE:
        S.reschedule()
        print("[sched] simulated time per core: %.1f us" % S.sim_time)
    S.assign(sems, dsems)
    with nc.Block() as block:
        @block.tensor
        def _(e):
            S.emit_stream("pe", e)

        @block.scalar
        def _(e):
            S.emit_stream("act", e)

        @block.vector
        def _(e):
            S.emit_stream("dve", e)

        @block.gpsimd
        def _(e):
            S.emit_stream("pool", e)

        @block.sync
        def _(e):
            S.emit_stream("sp", e)
            for k, tot in S.dma_tot.items():
                e.wait_ge(dsems[k], tot)
    es.close()
    return nc


_CACHE = {}


def _make_in_maps(inp, NT):
    f = lambda a: np.ascontiguousarray(np.asarray(a, dtype=np.float32))
    xp = f(inp["x_prompt"]); xsm = f(inp["x_sample"])
    pp = f(inp["p_prompt"])[0]; psm = f(inp["p_sample"])[0]
    ckw = f(inp["cache_k_win"])[0]; cvw = f(inp["cache_v_win"])[0]
    ckb = f(inp["cache_k_band"])[0]; cvb = f(inp["cache_v_band"])[0]
    TPC = NT * 128
    tabA, idx0, idx1, maskT = _geometry()
    rb = f(inp["rel_bias_band"])[0]
    tb = np.stack([rb[:, idx0], rb[:, idx1]], axis=0)
    tabB = np.ascontiguousarray(tb.transpose(2, 0, 1, 3)).reshape(128, 2048)
    shared = dict(
        w_in=f(inp["w_in"])[0], w_ow=f(inp["w_o_win"])[0], w_ob=f(inp["w_o_band"])[0],
        w_out=f(inp["w_out"])[0], w_pg=f(inp["w_ple_gate"])[0], w_pl=f(inp["w_ple"])[0],
        ident=np.eye(128, dtype=np.float32), tabA=tabA, tabB=tabB, maskT=maskT,
        cbrow=np.ascontiguousarray(rb[:, 256][None, :]),
        gTin=np.ascontiguousarray(f(inp["g_in"])[0].reshape(8, 128).T),
        gTple=np.ascontiguousarray(f(inp["g_ple"])[0].reshape(8, 128).T),
        gcol=np.ascontiguousarray(np.stack([np.tile(f(inp["g_q_win"])[0], 2), np.tile(f(inp["g_k_win"])[0], 2),
                                            np.tile(f(inp["g_q_band"])[0], 2), np.tile(f(inp["g_k_band"])[0], 2)], axis=1)),
        gkwb=np.ascontiguousarray(f(inp["g_k_win"])[0][None, :]),
        gkbb=np.ascontiguousarray(f(inp["g_k_band"])[0][None, :]),
        sinkrow=np.ascontiguousarray(f(inp["sink_win"])[0][None, :]),
    )
    maps = []
    for k in range(8):
        b, j = k // 4, k % 4
        s0 = j * TPC
        xh = np.zeros((512, D), np.float32)
        hm = np.zeros((128, 4), np.float32)
        for i in range(4):
            lo = s0 - 512 + i * 128
            if lo >= 0:
                xh[i * 128:(i + 1) * 128] = xp[b, lo:lo + 128]
            else:
                hm[:, i] = NEG
        xs_ = np.zeros((128, D), np.float32); xs_[:64] = xsm[k]
        ps_ = np.zeros((128, PLE), np.float32); ps_[:64] = psm[k]
        m = dict(shared)
        m.update(xh=xh, xm=np.ascontiguousarray(xp[b, s0:s0 + TPC]), pm=np.ascontiguousarray(pp[b, s0:s0 + TPC]),
                 xs=xs_, psd=ps_, hmask=hm,
                 ckw=np.ascontiguousarray(ckw[k].reshape(128, 128)), cvw=np.ascontiguousarray(cvw[k].reshape(128, 128)),
                 ckb=np.ascontiguousarray(ckb[k].reshape(512, 512)), cvb=np.ascontiguousarray(cvb[k].reshape(512, 512)))
        maps.append(m)
    return maps


def kernel(**inp):
    Sq = inp["x_prompt"].shape[1]
    NT = Sq // 4 // 128
    if NT not in _CACHE:
        _CACHE[NT] = build(NT)
    nc = _CACHE[NT]
    maps = _make_in_maps(inp, NT)
    res = run_bass_kernel_spmd(nc, maps, core_ids=list(range(8)))
    R = res.results
    TPC = NT * 128
    y = np.zeros((2, Sq, D), np.float32)
    for k in range(8):
        y[k // 4, (k % 4) * TPC:(k % 4 + 1) * TPC] = R[k]["y"]
    ysam = np.stack([R[k]["ys"] for k in range(8)], axis=0)
    last = [3, 7]
    kwp = np.stack([R[k]["kwin"].reshape(128, 2, 64) for k in last])[None]
    vwp = np.stack([R[k]["vwin"].reshape(128, 2, 64) for k in last])[None]
    kbp = np.stack([R[k]["kband"].reshape(512, 8, 64) for k in last])[None]
    vbp = np.stack([R[k]["vband"].reshape(512, 8, 64) for k in last])[None]
    kwsam = np.stack([R[k]["kws"].reshape(128, 2, 64) for k in range(8)])[None]
    vwsam = np.stack([R[k]["vws"].reshape(128, 2, 64) for k in range(8)])[None]
    kbsam = np.stack([R[k]["kbs"].reshape(512, 8, 64) for k in range(8)])[None]
    vbsam = np.stack([R[k]["vbs"].reshape(512, 8, 64) for k in range(8)])[None]
    return (y, ysam.astype(np.float32), kwp, vwp, kbp, vbp, kwsam, vwsam, kbsam, vbsam)
```
